# Optimizing a Trainium2 kernel written in Bass

```python
import jax, jax.numpy as jnp
from jax import lax
import numpy as np

D_MODEL = 2048
BATCH = 4
SEQ = 2048
DEPTH = 2

GRID_W = 64
CTX_LEN = 256
EPS = 1e-6
D_FF = (11 * D_MODEL) // 4
N_MOD = 9
BRANCH_W = D_MODEL // 2
N_BRANCH = 3
M_HEADS = 4
M_HEAD_DIM = BRANCH_W // M_HEADS
M_CHUNK = 64
ROPE_BASE = 10000.0
NA_HEADS = 8
NA_HEAD_DIM = BRANCH_W // NA_HEADS
NA_KH = 8
NA_KW = 16
LRU_BLOCKS = 8
LRU_BLOCK_DIM = BRANCH_W // LRU_BLOCKS
LRU_CONV = 4
LRU_C = 8.0
IN_SPLITS = (BRANCH_W, BRANCH_W, BRANCH_W, BRANCH_W, 4 * M_HEADS, BRANCH_W, BRANCH_W, BRANCH_W, BRANCH_W, BRANCH_W, N_BRANCH * D_MODEL)
P_IN = sum(IN_SPLITS)

kernel_name = "hybrid_mlstm_natten_rglru_dit_block"


def rmsnorm(x, g):
    xf = x.astype(jnp.float32)
    y = xf * lax.rsqrt(jnp.mean(xf * xf, axis=-1, keepdims=True) + EPS)
    return (y * g.astype(jnp.float32)).astype(x.dtype)


def adaln(x, g, shift, scale):
    return rmsnorm(x, g) * (1 + scale) + shift


def swiglu(h, w_in, w_out):
    gte, up = jnp.split(h @ w_in, 2, axis=-1)
    return (jax.nn.silu(gte) * up) @ w_out


def split_cols(p):
    idx, acc = [], 0
    for s in IN_SPLITS[:-1]:
        acc += s
        idx.append(acc)
    return jnp.split(p, idx, axis=-1)


def rope_2d(t, prow, pcol):
    half = t.shape[-1] // 2

    def rot(u, pos):
        nf = u.shape[-1] // 2
        inv = ROPE_BASE ** (-jnp.arange(nf, dtype=jnp.float32) / nf)
        ang = pos.astype(jnp.float32)[:, None] * inv[None, :]
        cos = jnp.cos(ang)[None, :, None, :]
        sin = jnp.sin(ang)[None, :, None, :]
        u1, u2 = u[..., :nf], u[..., nf:]
        return jnp.concatenate([u1 * cos - u2 * sin, u1 * sin + u2 * cos], axis=-1)

    return jnp.concatenate([rot(t[..., :half], prow), rot(t[..., half:], pcol)], axis=-1)


def _to_chunks(t):
    b, l, h = t.shape[:3]
    t = t.reshape((b, l // M_CHUNK, M_CHUNK, h) + t.shape[3:])
    return jnp.swapaxes(jnp.moveaxis(t, 1, 0), 2, 3)


def mlstm_scan(q, k, v, i_pre, f_pre, state, need_out):
    b, l, h, dh = q.shape
    tril = jnp.tril(jnp.ones((M_CHUNK, M_CHUNK), dtype=bool))

    def step(carry, inp):
        c_mem, n_mem, m_prev = carry
        qc, kc, vc, ic, lf = inp
        cum = jnp.cumsum(lf, axis=-1)
        logw = jnp.where(tril, cum[..., :, None] - cum[..., None, :] + ic[..., None, :], -jnp.inf)
        m_row = jnp.maximum(cum + m_prev[..., None], jnp.max(logw, axis=-1))
        m_new = m_row[..., -1]
        w_state = jnp.exp(cum[..., -1:] - cum + ic - m_new[..., None])
        decay = jnp.exp(cum[..., -1] + m_prev - m_new)
        c_new = decay[..., None, None] * c_mem + jnp.einsum('bhs,bhsv,bhsk->bhvk', w_state, vc, kc)
        n_new = decay[..., None] * n_mem + jnp.einsum('bhs,bhsk->bhk', w_state, kc)
        if not need_out:
            return (c_new, n_new, m_new), None
        inter = jnp.exp(cum + m_prev[..., None] - m_row)
        sc = jnp.einsum('bhtk,bhsk->bhts', qc, kc) * jnp.exp(logw - m_row[..., None])
        num = inter[..., None] * jnp.einsum('bhvk,bhtk->bhtv', c_mem, qc) + jnp.einsum('bhts,bhsv->bhtv', sc, vc)
        den = inter * jnp.einsum('bhk,bhtk->bht', n_mem, qc) + jnp.sum(sc, axis=-1)
        hid = num / jnp.maximum(jnp.abs(den), jnp.exp(-m_row))[..., None]
        return (c_new, n_new, m_new), hid

    xs = (_to_chunks(q), _to_chunks(k), _to_chunks(v), _to_chunks(i_pre), _to_chunks(jax.nn.log_sigmoid(f_pre)))
    state, hs = lax.scan(step, state, xs)
    if need_out:
        hs = jnp.transpose(hs, (1, 0, 3, 2, 4)).reshape(b, l, h, dh)
    return hs, state


def _flip(t):
    return jnp.flip(t, axis=1)


def mlstm_branch(q, k, v, o, g, qc, kc, vc, oc, gc, b_i, b_f, gn, prow, pcol, need_ctx_out):
    f32 = jnp.float32
    b, l = q.shape[:2]
    lc = qc.shape[1]
    heads = lambda t: t.reshape(t.shape[0], t.shape[1], M_HEADS, M_HEAD_DIM).astype(f32)
    kscale = M_HEAD_DIM ** -0.5
    q = rope_2d(heads(q), prow, pcol)
    k = rope_2d(heads(k), prow, pcol) * kscale
    v = heads(v)
    qc, kc, vc = heads(qc), heads(kc) * kscale, heads(vc)
    g = g.astype(f32).reshape(b, l, 2, 2, M_HEADS)
    gc = gc.astype(f32).reshape(b, lc, 2, 2, M_HEADS)
    zero = (jnp.zeros((b, M_HEADS, M_HEAD_DIM, M_HEAD_DIM), f32), jnp.zeros((b, M_HEADS, M_HEAD_DIM), f32),
            jnp.full((b, M_HEADS), -jnp.inf, f32))
    outs, outs_c = [], []
    for d in range(2):
        fl = _flip if d == 1 else (lambda t: t)
        hc_d, st = mlstm_scan(fl(qc), fl(kc), fl(vc), fl(gc[:, :, d, 0] + b_i[d]), fl(gc[:, :, d, 1] + b_f[d]), zero, need_ctx_out)
        hx_d, _ = mlstm_scan(fl(q), fl(k), fl(v), fl(g[:, :, d, 0] + b_i[d]), fl(g[:, :, d, 1] + b_f[d]), st, True)
        outs.append(fl(hx_d))
        if need_ctx_out:
            outs_c.append(fl(hc_d))

    def finish(hsum, og):
        hn = hsum * lax.rsqrt(jnp.mean(hsum * hsum, axis=-1, keepdims=True) + EPS)
        hn = hn.reshape(hn.shape[0], hn.shape[1], BRANCH_W) * gn.astype(f32)
        return (jax.nn.sigmoid(og.astype(f32)) * hn).astype(og.dtype)

    y = finish(outs[0] + outs[1], o)
    yc = finish(outs_c[0] + outs_c[1], oc) if need_ctx_out else None
    return y, yc


def na_branch(q, k, v, qc, kc, vc, rpb, need_ctx_out):
    f32 = jnp.float32
    b, l = q.shape[:2]
    rows = l // GRID_W
    kh = min(NA_KH, rows)
    scale = NA_HEAD_DIM ** -0.5
    grid = lambda t: t.reshape(b, rows, GRID_W, NA_HEADS, NA_HEAD_DIM)
    heads = lambda t: t.reshape(t.shape[0], t.shape[1], NA_HEADS, NA_HEAD_DIM)
    qg, kg, vg = grid(q), grid(k), grid(v)
    qc, kc, vc = heads(qc), heads(kc), heads(vc)
    r = jnp.arange(rows)
    row_idx = jnp.clip(r - kh // 2, 0, rows - kh)[:, None] + jnp.arange(kh)[None, :]
    k_win = kg[:, row_idx]
    v_win = vg[:, row_idx]
    col = jnp.arange(GRID_W)
    col_start = jnp.clip(col - NA_KW // 2, 0, GRID_W - NA_KW)
    col_ok = (col[None, :] >= col_start[:, None]) & (col[None, :] < col_start[:, None] + NA_KW)
    dr = row_idx - r[:, None] + (NA_KH - 1)
    dc = jnp.clip(col[None, :] - col[:, None], -(NA_KW - 1), NA_KW - 1) + (NA_KW - 1)
    bias = rpb[:, dr[:, None, :, None], dc[None, :, None, :]].astype(f32)
    bias = jnp.where(col_ok[None, None, :, None, :], bias, -jnp.inf)
    s_win = jnp.einsum('brqhd,brjkhd->bhrqjk', qg, k_win).astype(f32) * scale + bias
    s_ctx = jnp.einsum('brqhd,bchd->bhrqc', qg, kc).astype(f32) * scale
    nwin = kh * GRID_W
    s = jnp.concatenate([s_win.reshape(b, NA_HEADS, rows, GRID_W, nwin), s_ctx], axis=-1)
    p = jax.nn.softmax(s, axis=-1).astype(v.dtype)
    p_win = p[..., :nwin].reshape(b, NA_HEADS, rows, GRID_W, kh, GRID_W)
    out = jnp.einsum('bhrqjk,brjkhd->brqhd', p_win, v_win) + jnp.einsum('bhrqc,bchd->brqhd', p[..., nwin:], vc)
    out = out.reshape(b, l, BRANCH_W)
    if not need_ctx_out:
        return out, None
    sc = jnp.einsum('bqhd,bkhd->bhqk', qc, kc).astype(f32) * scale
    pc = jax.nn.softmax(sc, axis=-1).astype(vc.dtype)
    outc = jnp.einsum('bhqk,bkhd->bqhd', pc, vc).reshape(b, qc.shape[1], BRANCH_W)
    return out, outc


def dwconv(x, w, bias):
    l = x.shape[1]
    xp = jnp.pad(x, ((0, 0), ((LRU_CONV - 1) // 2, LRU_CONV // 2), (0, 0)))
    y = xp[:, 0:l] * w[0]
    for j in range(1, LRU_CONV):
        y = y + xp[:, j:j + l] * w[j]
    return y + bias


def rglru_coeffs(xb, w_a, b_a, w_x, b_x, lam):
    f32 = jnp.float32
    b, l = xb.shape[:2]
    xf = xb.astype(f32)
    xg = xf.reshape(b, l, LRU_BLOCKS, LRU_BLOCK_DIM)
    r = jax.nn.sigmoid(jnp.einsum('blgi,ngij->blngj', xg, w_a.astype(f32)).reshape(b, l, 2, BRANCH_W) + b_a.astype(f32))
    i = jax.nn.sigmoid(jnp.einsum('blgi,ngij->blngj', xg, w_x.astype(f32)).reshape(b, l, 2, BRANCH_W) + b_x.astype(f32))
    log_a = -LRU_C * r * jax.nn.softplus(-lam.astype(f32))
    a = jnp.exp(log_a)
    u = jnp.sqrt(-jnp.expm1(2.0 * log_a)) * i * xf[:, :, None, :]
    return a, u


def linear_scan(a, u, h0):
    acum, ucum = lax.associative_scan(lambda e1, e2: (e1[0] * e2[0], e2[0] * e1[1] + e2[1]), (a, u), axis=1)
    return ucum + acum * h0[:, None, :]


def lru_branch(xb, gate, xbc, gatec, conv_w, conv_b, w_a, b_a, w_x, b_x, lam, need_ctx_out):
    f32 = jnp.float32
    b = xb.shape[0]
    a, u = rglru_coeffs(dwconv(xb, conv_w, conv_b), w_a, b_a, w_x, b_x, lam)
    ac, uc = rglru_coeffs(dwconv(xbc, conv_w, conv_b), w_a, b_a, w_x, b_x, lam)
    zero = jnp.zeros((b, BRANCH_W), f32)
    hc_f = linear_scan(ac[:, :, 0], uc[:, :, 0], zero)
    hc_b = _flip(linear_scan(_flip(ac[:, :, 1]), _flip(uc[:, :, 1]), zero))
    h_f = linear_scan(a[:, :, 0], u[:, :, 0], hc_f[:, -1])
    h_b = _flip(linear_scan(_flip(a[:, :, 1]), _flip(u[:, :, 1]), hc_b[:, 0]))
    y = ((h_f + h_b) * jax.nn.gelu(gate.astype(f32))).astype(gate.dtype)
    yc = ((hc_f + hc_b) * jax.nn.gelu(gatec.astype(f32))).astype(gatec.dtype) if need_ctx_out else None
    return y, yc


def merge_branches(ys, gcols, w_branch, w_out):
    b, l = gcols.shape[:2]
    stacked = jnp.stack(ys, axis=2)
    proj = jnp.einsum('blnw,nwd->blnd', stacked, w_branch)
    gates = jax.nn.sigmoid(gcols.reshape(b, l, N_BRANCH, D_MODEL))
    return jnp.sum(gates * proj, axis=2) @ w_out


def mixer(hx, hc, w_in, b_i, b_f, gn, rpb, conv_w, conv_b, w_a, b_a, w_x, b_x, lam, w_branch, w_out, prow, pcol, need_ctx_out):
    mq, mk, mv, mo, mg, nq, nk, nv, lx, lg, gx = split_cols(hx @ w_in)
    mqc, mkc, mvc, moc, mgc, nqc, nkc, nvc, lxc, lgc, gxc = split_cols(hc @ w_in)
    y_m, yc_m = mlstm_branch(mq, mk, mv, mo, mg, mqc, mkc, mvc, moc, mgc, b_i, b_f, gn, prow, pcol, need_ctx_out)
    y_n, yc_n = na_branch(nq, nk, nv, nqc, nkc, nvc, rpb, need_ctx_out)
    y_l, yc_l = lru_branch(lx, lg, lxc, lgc, conv_w, conv_b, w_a, b_a, w_x, b_x, lam, need_ctx_out)
    y = merge_branches([y_m, y_n, y_l], gx, w_branch, w_out)
    yc = merge_branches([yc_m, yc_n, yc_l], gxc, w_branch, w_out) if need_ctx_out else None
    return y, yc


def setup_inputs(seed: int = 0) -> dict:
    key = jax.random.key(seed)
    ks = jax.random.split(key, 28)
    f32 = jnp.float32

    def nrm(k, shape, s):
        return jax.random.normal(k, shape, f32) * s

    def gain(k, shape):
        return 1.0 + nrm(k, shape, 0.1)

    a0 = jax.random.uniform(ks[26], (DEPTH, 2, BRANCH_W), f32, 0.9, 0.999)
    return {
        "x": nrm(ks[0], (BATCH, SEQ, D_MODEL), 1.0),
        "c": nrm(ks[1], (BATCH, D_MODEL), 1.0),
        "ctx": nrm(ks[2], (BATCH, CTX_LEN, D_MODEL), 1.0),
        "c_ctx": nrm(ks[3], (D_MODEL,), 1.0),
        "w_mod": nrm(ks[4], (DEPTH, D_MODEL, N_MOD * D_MODEL), 0.5 * D_MODEL ** -0.5),
        "b_mod": nrm(ks[5], (DEPTH, N_MOD * D_MODEL), 0.02),
        "norm_ffn1": gain(ks[6], (DEPTH, D_MODEL)),
        "norm_mix": gain(ks[7], (DEPTH, D_MODEL)),
        "norm_ffn2": gain(ks[8], (DEPTH, D_MODEL)),
        "ffn1_w_in": nrm(ks[9], (DEPTH, D_MODEL, 2 * D_FF), D_MODEL ** -0.5),
        "ffn1_w_out": nrm(ks[10], (DEPTH, D_FF, D_MODEL), D_FF ** -0.5),
        "ffn2_w_in": nrm(ks[11], (DEPTH, D_MODEL, 2 * D_FF), D_MODEL ** -0.5),
        "ffn2_w_out": nrm(ks[12], (DEPTH, D_FF, D_MODEL), D_FF ** -0.5),
        "w_in": nrm(ks[13], (DEPTH, D_MODEL, P_IN), D_MODEL ** -0.5),
        "mlstm_b_i": nrm(ks[14], (DEPTH, 2, M_HEADS), 0.1),
        "mlstm_b_f": jnp.linspace(3.0, 6.0, M_HEADS, dtype=f32) + nrm(ks[15], (DEPTH, 2, M_HEADS), 0.1),
        "mlstm_gn": gain(ks[16], (DEPTH, BRANCH_W)),
        "na_rpb": nrm(ks[17], (DEPTH, NA_HEADS, 2 * NA_KH - 1, 2 * NA_KW - 1), 0.5),
        "lru_conv_w": nrm(ks[18], (DEPTH, LRU_CONV, BRANCH_W), LRU_CONV ** -0.5),
        "lru_conv_b": nrm(ks[19], (DEPTH, BRANCH_W), 0.02),
        "lru_w_a": nrm(ks[20], (DEPTH, 2, LRU_BLOCKS, LRU_BLOCK_DIM, LRU_BLOCK_DIM), LRU_BLOCK_DIM ** -0.5),
        "lru_b_a": nrm(ks[21], (DEPTH, 2, BRANCH_W), 0.02),
        "lru_w_x": nrm(ks[22], (DEPTH, 2, LRU_BLOCKS, LRU_BLOCK_DIM, LRU_BLOCK_DIM), LRU_BLOCK_DIM ** -0.5),
        "lru_b_x": nrm(ks[23], (DEPTH, 2, BRANCH_W), 0.02),
        "lru_lambda": jnp.log(a0) - jnp.log1p(-a0),
        "w_branch": nrm(ks[24], (DEPTH, N_BRANCH, BRANCH_W, D_MODEL), BRANCH_W ** -0.5),
        "w_out": nrm(ks[25], (DEPTH, D_MODEL, D_MODEL), D_MODEL ** -0.5),
        "norm_final": gain(ks[27], (D_MODEL,)),
    }


def reference(x, c, ctx, c_ctx, w_mod, b_mod, norm_ffn1, norm_mix, norm_ffn2, ffn1_w_in, ffn1_w_out, ffn2_w_in, ffn2_w_out,
              w_in, mlstm_b_i, mlstm_b_f, mlstm_gn, na_rpb, lru_conv_w, lru_conv_b, lru_w_a, lru_b_a, lru_w_x, lru_b_x,
              lru_lambda, w_branch, w_out, norm_final):
    l = x.shape[1]
    pos = jnp.arange(l)
    prow, pcol = pos // GRID_W, pos % GRID_W
    c_act = jax.nn.silu(c)
    cc_act = jax.nn.silu(c_ctx)
    xc = ctx
    for li in range(DEPTH):
        last = li == DEPTH - 1
        mod = (c_act @ w_mod[li] + b_mod[li])[:, None, :]
        modc = (cc_act @ w_mod[li] + b_mod[li])[None, None, :]
        sh1, sc1, g1, sh2, sc2, g2, sh3, sc3, g3 = jnp.split(mod, N_MOD, axis=-1)
        csh1, csc1, cg1, csh2, csc2, cg2, csh3, csc3, cg3 = jnp.split(modc, N_MOD, axis=-1)
        x = x + 0.5 * g1 * swiglu(adaln(x, norm_ffn1[li], sh1, sc1), ffn1_w_in[li], ffn1_w_out[li])
        xc = xc + 0.5 * cg1 * swiglu(adaln(xc, norm_ffn1[li], csh1, csc1), ffn1_w_in[li], ffn1_w_out[li])
        y, yc = mixer(adaln(x, norm_mix[li], sh2, sc2), adaln(xc, norm_mix[li], csh2, csc2), w_in[li],
                      mlstm_b_i[li], mlstm_b_f[li], mlstm_gn[li], na_rpb[li], lru_conv_w[li], lru_conv_b[li],
                      lru_w_a[li], lru_b_a[li], lru_w_x[li], lru_b_x[li], lru_lambda[li], w_branch[li], w_out[li],
                      prow, pcol, not last)
        x = x + g2 * y
        x = x + 0.5 * g3 * swiglu(adaln(x, norm_ffn2[li], sh3, sc3), ffn2_w_in[li], ffn2_w_out[li])
        if not last:
            xc = xc + cg2 * yc
            xc = xc + 0.5 * cg3 * swiglu(adaln(xc, norm_ffn2[li], csh3, csc3), ffn2_w_in[li], ffn2_w_out[li])
    return rmsnorm(x, norm_final)
```

```python
import contextlib
import numpy as np
import concourse.bass as bass
import concourse.mybir as mybir
from concourse.bass_utils import run_bass_kernel_spmd

F32 = mybir.dt.float32
BF16 = mybir.dt.bfloat16
AF = mybir.ActivationFunctionType
ALU = mybir.AluOpType

COMPUTE = ("pe", "act", "dve", "pool")
DMAQ = ("sp", "act", "pool")
NRING = 8


class Buf:
    __slots__ = ("wc", "wd", "rc", "rd")

    def __init__(self):
        self.wc = None
        self.wd = {}
        self.rc = {}
        self.rd = {}


class V:
    __slots__ = ("ap", "bufs")

    def __init__(self, ap, bufs):
        self.ap = ap
        self.bufs = tuple(bufs)

    def __getitem__(self, idx):
        return V(self.ap[idx], self.bufs)

    def re(self, pattern, **kw):
        return V(self.ap.rearrange(pattern, **kw), self.bufs)


def _ap(x):
    return x.ap if isinstance(x, V) else x


class Op:
    __slots__ = ("eng", "fn", "deps", "sig", "sigval", "is_dma", "qidx", "id")


class Prog:
    def __init__(self, nc):
        self.nc = nc
        self.ops = []
        self.by_eng = {e: [] for e in ("pe", "act", "dve", "pool", "sp")}
        self.ndma = {e: 0 for e in DMAQ}
        self.bar = {}
        self.psl = []
        self.psi = 0

    def new(self, ap):
        return V(ap, [Buf()])

    def ps(self):
        v = self.psl[self.psi % len(self.psl)]
        self.psi += 1
        return v

    def barrier(self):
        deps = []
        for e in ("pe", "act", "dve", "pool", "sp"):
            lst = self.by_eng[e]
            seen_c = False
            nd = 0
            for op in reversed(lst):
                if op.is_dma:
                    if nd < NRING:
                        deps.append(op.id)
                        nd += 1
                elif not seen_c:
                    deps.append(op.id)
                    seen_c = True
                if seen_c and nd >= NRING:
                    break
        self.bar = {e: list(deps) for e in ("pe", "act", "dve", "pool", "sp")}

    def emit(self, eng, fn, reads=(), writes=(), dma=False):
        op = Op()
        op.eng = eng
        op.fn = fn
        op.sig = False
        op.sigval = None
        op.is_dma = dma
        op.id = len(self.ops)
        op.qidx = None
        if dma:
            op.qidx = self.ndma[eng]
            self.ndma[eng] += 1
        ops = self.ops
        ceng = {}
        dmadeps = set()

        def add(i):
            d = ops[i]
            if d.is_dma:
                dmadeps.add(i)
            else:
                if d.eng == "pe" and eng == "pe" and not dma:
                    return
                if ceng.get(d.eng, -1) < i:
                    ceng[d.eng] = i

        pend = self.bar.pop(eng, None)
        if pend:
            for i in pend:
                add(i)
        for v in reads:
            if not isinstance(v, V):
                continue
            for b in v.bufs:
                if b.wc is not None:
                    add(b.wc)
                for l in b.wd.values():
                    for i in l:
                        add(i)
        for v in writes:
            for b in v.bufs:
                had_reads = bool(b.rc) or bool(b.rd)
                for i in b.rc.values():
                    add(i)
                for l in b.rd.values():
                    for i in l:
                        add(i)
                if b.wc is not None:
                    add(b.wc)
                if (not dma) or had_reads:
                    for l in b.wd.values():
                        for i in l:
                            add(i)
        deps = list(ceng.values()) + list(dmadeps)
        for i in deps:
            ops[i].sig = True
        op.deps = deps
        for v in reads:
            if not isinstance(v, V):
                continue
            for b in v.bufs:
                if dma:
                    l = b.rd.setdefault(eng, [])
                    l.append(op.id)
                    if len(l) > NRING:
                        del l[0]
                else:
                    b.rc[eng] = op.id
        for v in writes:
            for b in v.bufs:
                if dma:
                    if b.rc or b.rd:
                        b.wd = {}
                    b.wc = None
                    l = b.wd.setdefault(eng, [])
                    l.append(op.id)
                    if len(l) > NRING:
                        del l[0]
                else:
                    b.wc = op.id
                    b.wd = {}
                b.rc = {}
                b.rd = {}
        ops.append(op)
        self.by_eng[eng].append(op)
        return op

    def mm(self, out, lhsT, rhs, start=True, stop=True):
        o, l, r = _ap(out), _ap(lhsT), _ap(rhs)
        rd = [lhsT, rhs] + ([] if start else [out])
        self.emit("pe", lambda e: e.matmul(o, l, r, start=start, stop=stop), rd, [out])

    def act(self, out, in_, func, bias=None, scale=None):
        o, i = _ap(out), _ap(in_)
        kw = {}
        rd = [in_]
        if bias is not None:
            kw["bias"] = _ap(bias)
            rd.append(bias)
        if scale is not None:
            kw["scale"] = _ap(scale)
            rd.append(scale)
        self.emit("act", lambda e: e.activation(o, i, func, **kw), rd, [out])

    def tt(self, out, in0, in1, op, eng="dve"):
        o, a, b = _ap(out), _ap(in0), _ap(in1)
        self.emit(eng, lambda e: e.tensor_tensor(o, a, b, op), [in0, in1], [out])

    def ts(self, out, in0, s1, op0, s2=None, op1=None, eng="dve"):
        o, a = _ap(out), _ap(in0)
        rd = [in0, s1, s2]
        a1, a2 = _ap(s1), _ap(s2)
        if op1 is None:
            self.emit(eng, lambda e: e.tensor_scalar(o, a, a1, None, op0), rd, [out])
        else:
            self.emit(eng, lambda e: e.tensor_scalar(o, a, a1, a2, op0, op1), rd, [out])

    def stt(self, out, in0, scalar, in1, op0, op1):
        o, a, s, b = _ap(out), _ap(in0), _ap(scalar), _ap(in1)
        self.emit("dve", lambda e: e.scalar_tensor_tensor(o, a, s, b, op0, op1), [in0, scalar, in1], [out])

    def scan(self, out, d0, d1, initial, op0, op1):
        o, a, b, i = _ap(out), _ap(d0), _ap(d1), _ap(initial)
        self.emit("dve", lambda e: e.tensor_tensor_scan(o, a, b, i, op0, op1), [d0, d1, initial], [out])

    def copy(self, out, in_, eng="dve"):
        o, i = _ap(out), _ap(in_)
        if eng == "act":
            self.emit("act", lambda e: e.activation(o, i, AF.Identity), [in_], [out])
        else:
            self.emit(eng, lambda e: e.tensor_copy(o, i), [in_], [out])

    def recip(self, out, in_):
        o, i = _ap(out), _ap(in_)
        self.emit("dve", lambda e: e.reciprocal(o, i), [in_], [out])

    def memset(self, out, val, eng="pool"):
        o = _ap(out)
        self.emit(eng, lambda e: e.memset(o, val), [], [out])

    def dma(self, out, in_, q="pool", **kw):
        o, i = _ap(out), _ap(in_)
        self.emit(q, lambda e: e.dma_start(o, i, **kw), [in_], [out], dma=True)

    def finalize(self, stack):
        nc = self.nc
        sems = {e: stack.enter_context(nc.semaphore("s_" + e)) for e in COMPUTE}
        rings = {q: [stack.enter_context(nc.semaphore("r_%s%d" % (q, k))) for k in range(NRING)]
                 for q in DMAQ if self.ndma[q] > 0}
        cnt = {e: 0 for e in COMPUTE}
        for op in self.ops:
            if not op.is_dma and op.sig:
                cnt[op.eng] += 1
                op.sigval = cnt[op.eng]
        ops = self.ops
        by_eng = self.by_eng
        ndma = self.ndma

        def run(eng_name, e):
            waited = {}

            def wait(s, v):
                k = id(s)
                if waited.get(k, 0) < v:
                    e.wait_ge(s, v)
                    waited[k] = v

            for op in by_eng[eng_name]:
                if op.is_dma and op.qidx >= NRING:
                    wait(rings[eng_name][op.qidx % NRING], 16 * (op.qidx // NRING))
                for d in op.deps:
                    dop = ops[d]
                    if dop.is_dma:
                        wait(rings[dop.eng][dop.qidx % NRING], 16 * (dop.qidx // NRING + 1))
                    else:
                        wait(sems[dop.eng], dop.sigval)
                inst = op.fn(e)
                if op.is_dma:
                    inst.then_inc(rings[eng_name][op.qidx % NRING], 16)
                elif op.sig:
                    inst.then_inc(sems[eng_name], 1)
            if eng_name in rings:
                n = ndma[eng_name]
                for k in range(min(n, NRING)):
                    uses = (n - 1 - k) // NRING + 1
                    wait(rings[eng_name][k], 16 * uses)

        with nc.Block() as block:
            @block.tensor
            def _(e):
                run("pe", e)

            @block.scalar
            def _(e):
                run("act", e)

            @block.vector
            def _(e):
                run("dve", e)

            @block.gpsimd
            def _(e):
                run("pool", e)

            @block.sync
            def _(e):
                run("sp", e)


D = 2048
KC = 16
DFF = 5632
PIN = 15376
BW = 1024
LAT = 2048
CTX = 256
GL = 1024
T = CTX + GL
SEQ = CTX + LAT
NKT = SEQ // 128
EPS = 1e-6
TILES_ALL = [(0, 256), (256, 512), (768, 512)]
OFF = dict(mq=0, mk=1024, mv=2048, mo=3072, mg=4096, nq=4112, nk=5136, nv=6160, lx=7184, lg=8208, gx=9232)
NEG = -30000.0
KB = 1024


class Arena:
    def __init__(self, nc, P, stack, nbytes):
        self.nc = nc
        self.P = P
        self.base = (nc.sbuf_base + 31) // 32 * 32
        stack.enter_context(nc.sbuf_tensor("arena", [128, nbytes], mybir.dt.uint8))
        assert nc.sbuf_base == self.base + nbytes, (nc.sbuf_base, self.base, nbytes)
        self.nbytes = nbytes
        self.n = 0

    def raw(self, shape, dtype, off):
        esz = 4 if dtype == F32 else 2
        sz = esz * int(np.prod(shape[1:]))
        assert off % 4 == 0 and off + sz <= self.nbytes, (shape, off, sz, self.nbytes)
        self.n += 1
        return self.nc.alloc_sbuf_tensor_at("a%d" % self.n, list(shape), dtype, offset=self.base + off)

    def v(self, shape, dtype, off):
        return self.P.new(self.raw(shape, dtype, off)[:])


class Bump:
    def __init__(self, ar, lo, hi):
        self.ar, self.lo, self.hi, self.top = ar, lo, hi, lo

    def raw(self, shape, dtype):
        esz = 4 if dtype == F32 else 2
        sz = (esz * int(np.prod(shape[1:])) + 31) // 32 * 32
        assert self.top + sz <= self.hi, ("bump overflow", shape, self.top, sz, self.hi)
        t = self.ar.raw(shape, dtype, self.top)
        self.top += sz
        return t

    def v(self, shape, dtype):
        return self.ar.P.new(self.raw(shape, dtype)[:])

    def vs(self, n, shape, dtype):
        return [self.v(shape, dtype) for _ in range(n)]


class Rot:
    def __init__(self, items):
        self.items, self.i = items, 0

    def next(self):
        v = self.items[self.i % len(self.items)]
        self.i += 1
        return v


class WStream:
    def __init__(self, P, stages, bfs):
        self.P = P
        self.stages = Rot(stages)
        self.bfs = Rot(bfs)
        self.specs = []
        self.i = 0
        self.loaded = {}
        self.ci = 0

    def begin(self, specs):
        self.specs = list(specs)
        self.i = 0
        self.loaded = {}
        self._load(0)

    def _load(self, j):
        if j >= len(self.specs) or j in self.loaded:
            return
        src, nk, nc_, cast = self.specs[j]
        P = self.P
        st = self.stages.next()
        stv = V(st.ap[:, 0:nk * nc_].rearrange("p (k c) -> p k c", k=nk), st.bufs)
        P.dma(stv, src.re("(k p) c -> p k c", p=128), q="sp")
        if not cast:
            self.loaded[j] = stv
            return
        bf = self.bfs.next()
        bfv = V(bf.ap[:, 0:nk * nc_].rearrange("p (k c) -> p k c", k=nk), bf.bufs)
        P.copy(bfv, stv, eng="act")
        self.loaded[j] = bfv

    def next(self):
        j = self.i
        self.i += 1
        self._load(j)
        self._load(j + 1)
        return self.loaded.pop(j)


def build(n_layers=2, groups=(0, 1), stop=None, dbg=(), mix_test=False):
    nc = bass.Bass("TRN2", target_bir_lowering=False)
    P = Prog(nc)
    NG = 2

    BIGW = ("w_mod", "ffn1_w_in", "ffn1_w_out", "ffn2_w_in", "ffn2_w_out", "w_in", "w_branch", "w_out", "xT", "ctxT")

    def din(name, shape, dt=F32):
        if mix_test and name in BIGW:
            return None
        return V(nc.dram_tensor(name, list(shape), dt, kind="ExternalInput").ap(), [Buf()])

    def dscr(name, shape, dt):
        return nc.dram_tensor(name, list(shape), dt, kind="Internal").ap()

    xT_d = din("xT", [NG, D, GL])
    ctxT_d = din("ctxT", [D, CTX])
    ccol_d = din("ccol", [128, KC, 2])
    w_mod_d = din("w_mod", [2, D, 9 * D])
    bmod_d = din("bmod", [2, 128, 144])
    gains_d = din("gains", [2, 3, 128, KC])
    gfin_d = din("gfin", [128, KC])
    f1in_d = din("ffn1_w_in", [2, D, 2 * DFF])
    f1out_d = din("ffn1_w_out", [2, DFF, D])
    f2in_d = din("ffn2_w_in", [2, D, 2 * DFF])
    f2out_d = din("ffn2_w_out", [2, DFF, D])
    w_in_d = din("w_in", [2, D, PIN])
    w_br_d = din("w_branch", [2, 3, BW, D])
    w_out_d = din("w_out", [2, D, D])
    mbi_d = din("mbi", [2, 64, 1])
    mbf_d = din("mbf", [2, 64, 1])
    mgn_d = din("mgn", [2, 128, 8])
    rpbe_d = din("rpbe", [2, 8, 64, 15 * 64])
    colm_d = din("colm", [128, 15 * 64])
    lcw_d = din("lcw", [2, 128, 8, 4])
    lcb_d = din("lcb", [2, 128, 8])
    lwa_d = din("lwa", [2, 2, 8, 128, 128])
    lwx_d = din("lwx", [2, 2, 8, 128, 128])
    lba_d = din("lba", [2, 128, 2, 8])
    lbx_d = din("lbx", [2, 128, 2, 8])
    llam_d = din("llam", [2, 128, 2, 8])
    perm_d = din("perm", [128, 128])
    masks_d = din("masks", [128, 8, 512])
    rope_d = din("rope", [NG, 128, 4, GL])
    sel_d = din("sel", [64, 4, 128])
    id4_d = din("id4", [64, 4])
    outT = nc.dram_tensor("outT", [NG, D, GL], F32, kind="ExternalOutput").ap()
    outT_d = [V(outT[g], [Buf()]) for g in range(NG)]
    dbg_d = {}
    for name, shape, dt in dbg:
        dbg_d[name] = V(nc.dram_tensor("dbg_" + name, list(shape), dt, kind="ExternalOutput").ap(), [Buf()])

    def scr1(name, shape, dt, ext_in=False):
        if mix_test and ext_in:
            return V(nc.dram_tensor(name, list(shape), dt, kind="ExternalInput").ap(), [Buf()])
        return V(dscr(name, shape, dt), [Buf()])

    XS = [scr1("XS%d" % g, [128, KC, T], F32) for g in range(NG)]
    MRG = [scr1("MRG%d" % g, [KC, 128, T], BF16) for g in range(NG)]
    MQ = scr1("MQ", [8, 128, SEQ], BF16, True)
    SO = scr1("SO", [8, 128, SEQ], BF16, True)
    NQ = scr1("NQ", [8, 128, SEQ], BF16, True)
    GLG = scr1("GLG", [8, 128, SEQ], BF16, True)
    MK = scr1("MK", [8, 128, SEQ], BF16, True)
    NK = scr1("NK", [8, 128, SEQ], BF16, True)
    MV = scr1("MV", [SEQ, BW], BF16, True)
    NV = scr1("NV", [SEQ, BW], BF16, True)
    MG = scr1("MG", [4, 4, SEQ], F32, True)
    LX = scr1("LX", [8, 128, SEQ], F32, True)
    YA = scr1("YA", [3, 8, 128, SEQ], BF16)

    def scol(g, t0):
        return t0 if t0 < CTX else CTX + g * GL + (t0 - CTX)

    with contextlib.ExitStack() as st:
        P.psl = [P.new(st.enter_context(nc.psum_tensor("ps%d" % i, [128, 512], F32))[:]) for i in range(8)]
        ARENA_BYTES = 206 * KB
        ar = Arena(nc, P, st, ARENA_BYTES)
        cb = Bump(ar, 197 * KB, ARENA_BYTES)
        ones32 = cb.v([128, 128], F32)
        onesb = cb.v([128, 128], BF16)
        cact = cb.v([128, KC, 2], F32)
        MOD = [cb.v([128, 144, 2], F32) for _ in range(2)]
        Amod = [[[cb.v([128, KC], F32) for w in range(2)] for n in range(3)] for L in range(2)]
        Gmod = [[[cb.v([128, KC], F32) for w in range(2)] for n in range(3)] for L in range(2)]
        gains = cb.v([128, 2, 3, KC], F32)
        gfin = cb.v([128, KC], F32)
        bmod = cb.v([128, 2, 144], F32)
        epsc = cb.v([128, 1], F32)
        P.memset(ones32, 1.0)
        P.memset(onesb, 1.0)
        P.memset(epsc, EPS)
        P.dma(gains, gains_d.re("l n p k -> p l n k"))
        P.dma(gfin, gfin_d)
        P.dma(bmod, bmod_d.re("l p c -> p l c"))

        X_OFF, HN_OFF, W_OFF = 0, 80 * KB, 120 * KB
        Xt = ar.raw([128, KC, T], F32, X_OFF)
        HNt = ar.raw([128, KC, T], BF16, HN_OFF)
        X = [[P.new(Xt[:, kc, t0:t0 + n]) for (t0, n) in TILES_ALL] for kc in range(KC)]
        HN = [[P.new(HNt[:, kc, t0:t0 + n]) for (t0, n) in TILES_ALL] for kc in range(KC)]
        Xall = V(Xt[:], [b for row in X for v in row for b in v.bufs])
        Xlat = V(Xt[:, :, CTX:T], [b for row in X for v in row[1:] for b in v.bufs])
        Xctx = V(Xt[:, :, 0:CTX], [row[0].bufs[0] for row in X])
        wst = [ar.v([128, 4096], F32, W_OFF + i * 16 * KB) for i in range(2)]
        wbf = [ar.v([128, 4096], BF16, W_OFF + 32 * KB + i * 8 * KB) for i in range(3)]
        WS = WStream(P, wst, wbf)
        S_OFF = 176 * KB

        def wq(tile_i):
            return 1 if tile_i == 0 else 0

        def dump(name, src):
            if name in dbg_d:
                P.dma(dbg_d[name], src)

        sb0 = Bump(ar, S_OFF, 197 * KB)
        cc = sb0.v([128, KC, 2], F32)
        P.dma(cc, ccol_d)
        P.act(cact, cc, AF.Silu)
        idc = cb.v([64, 4], F32)
        P.dma(idc, id4_d)
        rwr = Rot(cb.vs(2, [2, 256], F32))

        wst2 = [ar.v([128, 4096], F32, 160 * KB + i * 16 * KB) for i in range(2)]
        WS2 = WStream(P, wst2, wbf)

        def mod_begin(L, ws=None):
            (ws or WS).begin([(w_mod_d[L, :, j * 256:(j + 1) * 256], KC, 256, False) for j in range(72)])

        def mod_panel(L, j, ws=None):
            pan = (ws or WS).next()
            ps = P.ps()
            for kc in range(KC):
                P.mm(ps[0:2, 0:256], cact[:, kc, :], pan[:, kc, :], start=(kc == 0), stop=(kc == KC - 1))
            rw = rwr.next()
            P.copy(rw, ps[0:2, 0:256], eng="act")
            for o2 in range(2):
                ps2 = P.ps()
                P.mm(ps2[:, 0:2], rw[0:2, o2 * 128:(o2 + 1) * 128], idc[0:2, 0:2])
                ch = j * 2 + o2
                P.ts(MOD[L][:, ch, :], ps2[:, 0:2], bmod[:, L, ch:ch + 1], ALU.add)

        def mod_finish(L):
            for n in range(3):
                for w in range(2):
                    sc = MOD[L][:, (3 * n + 1) * KC:(3 * n + 2) * KC, w]
                    gt = MOD[L][:, (3 * n + 2) * KC:(3 * n + 3) * KC, w]
                    P.stt(Amod[L][n][w], sc, 1.0, gains[:, L, n, :], ALU.add, ALU.mult)
                    P.ts(Gmod[L][n][w], gt, 1.0 if n == 1 else 0.5, ALU.mult)

        if not mix_test:
            mod_begin(0)
            for j in range(72):
                mod_panel(0, j)
            mod_finish(0)
        dump("mod0", MOD[0])

        def shift_ap(L, n, w, kc):
            return MOD[L][:, 3 * n * KC + kc, w:w + 1]

        def adaln(L, n, tiles, sbm):
            sq = Rot(sbm.vs(3, [128, 512], BF16))
            tmp = Rot(sbm.vs(2, [128, 512], F32))
            rs = sbm.v([128, 512], F32)
            rstd = sbm.v([128, 512], F32)
            for ti in tiles:
                t0, nn = TILES_ALL[ti]
                w = wq(ti)
                ps = P.ps()
                for kc in range(KC):
                    s = sq.next()
                    P.act(s[:, :nn], X[kc][ti], AF.Square)
                    P.mm(ps[:, :nn], onesb, s[:, :nn], start=(kc == 0), stop=(kc == KC - 1))
                P.act(rs[:, :nn], ps[:, :nn], AF.Sqrt, bias=epsc, scale=1.0 / D)
                P.recip(rstd[:, :nn], rs[:, :nn])
                for kc in range(KC):
                    tm = tmp.next()
                    P.tt(tm[:, :nn], X[kc][ti], rstd[:, :nn], ALU.mult)
                    P.act(HN[kc][ti], tm[:, :nn], AF.Identity,
                          bias=shift_ap(L, n, w, kc), scale=Amod[L][n][w][:, kc:kc + 1])

        def ffn(L, n, tiles, win_d, wout_d):
            sbm = Bump(ar, S_OFF, 197 * KB)
            adaln(L, n, tiles, sbm)
            P.barrier()
            sbm = Bump(ar, S_OFF, 197 * KB)
            actg_t = sbm.raw([128, 4, T], BF16)
            ACTG = [[P.new(actg_t[:, j, t0:t0 + nn]) for (t0, nn) in TILES_ALL] for j in range(4)]
            sil = Rot(sbm.vs(2, [128, 512], F32))
            specs = []
            for grp in range(11):
                for half in range(2):
                    c0 = (grp * 4 + half * 2) * 128
                    specs.append((win_d[L, :, c0:c0 + 256], KC, 256, True))
                    specs.append((win_d[L, :, DFF + c0:DFF + c0 + 256], KC, 256, True))
                for ob in range(4):
                    specs.append((wout_d[L, grp * 512:(grp + 1) * 512, ob * 512:(ob + 1) * 512], 4, 512, True))
            WS.begin(specs)
            for grp in range(11):
                for half in range(2):
                    pg = WS.next()
                    pu = WS.next()
                    for jj in range(2):
                        jl = half * 2 + jj
                        for ti in tiles:
                            t0, nn = TILES_ALL[ti]
                            psg = P.ps()
                            psu = P.ps()
                            for kc in range(KC):
                                P.mm(psg[:, :nn], pg[:, kc, jj * 128:(jj + 1) * 128], HN[kc][ti],
                                     start=(kc == 0), stop=(kc == KC - 1))
                            for kc in range(KC):
                                P.mm(psu[:, :nn], pu[:, kc, jj * 128:(jj + 1) * 128], HN[kc][ti],
                                     start=(kc == 0), stop=(kc == KC - 1))
                            s = sil.next()
                            P.act(s[:, :nn], psg[:, :nn], AF.Silu)
                            P.tt(ACTG[jl][ti], s[:, :nn], psu[:, :nn], ALU.mult)
                for ob in range(4):
                    pw = WS.next()
                    for o4 in range(4):
                        oc = ob * 4 + o4
                        for ti in tiles:
                            t0, nn = TILES_ALL[ti]
                            ps = P.ps()
                            for j in range(4):
                                P.mm(ps[:, :nn], pw[:, j, o4 * 128:(o4 + 1) * 128], ACTG[j][ti],
                                     start=(j == 0), stop=(j == 3))
                            P.stt(X[oc][ti], ps[:, :nn], Gmod[L][n][wq(ti)][:, oc:oc + 1], X[oc][ti],
                                  ALU.mult, ALU.add)

        def inproj(L, g):
            gt = [0, 1, 2] if g == 0 else [1, 2]
            sbm = Bump(ar, S_OFF, 197 * KB)
            adaln(L, 1, gt, sbm)
            P.dma(XS[g], Xall)
            P.barrier()
            sbm = Bump(ar, X_OFF, 80 * KB)
            ropet = sbm.v([128, 4, GL], F32)
            P.dma(ropet, rope_d[g])
            perm = sbm.v([128, 128], F32)
            P.dma(perm, perm_d)
            stg32 = Rot(sbm.vs(2, [128, 512], F32))
            stgb = Rot(sbm.vs(3, [128, 512], BF16))
            t1r = Rot(sbm.vs(2, [128, 512], F32))
            t2r = Rot(sbm.vs(2, [128, 512], F32))
            wg32 = sbm.v([128, KC, 16], F32)
            wgb = sbm.v([128, KC, 16], BF16)
            g4 = Rot(sbm.vs(2, [4, 512], F32))
            P.dma(wg32, w_in_d[L, :, OFF["mg"]:OFF["mg"] + 16].re("(k p) c -> p k c", p=128))
            P.copy(wgb, wg32, eng="dve")
            for dg in range(4):
                for ti in gt:
                    t0, nn = TILES_ALL[ti]
                    ps = P.ps()
                    for kc in range(KC):
                        P.mm(ps[0:4, :nn], wgb[:, kc, dg * 4:(dg + 1) * 4], HN[kc][ti],
                             start=(kc == 0), stop=(kc == KC - 1))
                    s = g4.next()
                    P.copy(s[:, :nn], ps[0:4, :nn], eng="dve")
                    P.dma(MG[dg, :, scol(g, t0):scol(g, t0) + nn], s[:, :nn])
            fm = [("mq", MQ), ("mk", MK), ("mo", SO), ("nq", NQ), ("nk", NK), ("lx", LX), ("lg", GLG)]
            tm = [("mv", MV), ("nv", NV)]
            specs = []
            for name, _ in fm + tm:
                for pnl in range(4):
                    c0 = OFF[name] + pnl * 256
                    specs.append((w_in_d[L, :, c0:c0 + 256], KC, 256, True))
            WS.begin(specs)
            for name, dst in fm:
                for pnl in range(4):
                    pan = WS.next()
                    for o2 in range(2):
                        ch = pnl * 2 + o2
                        for ti in gt:
                            t0, nn = TILES_ALL[ti]
                            ps = P.ps()
                            for kc in range(KC):
                                P.mm(ps[:, :nn], pan[:, kc, o2 * 128:(o2 + 1) * 128], HN[kc][ti],
                                     start=(kc == 0), stop=(kc == KC - 1))
                            dd = dst[ch, :, scol(g, t0):scol(g, t0) + nn]
                            if name in ("mq", "mk"):
                                sc = 1.0 if name == "mq" else 1.0 / 16.0
                                if ti == 0:
                                    o = stgb.next()
                                    P.act(o[:, :nn], ps[:, :nn], AF.Identity, scale=sc)
                                    P.dma(dd, o[:, :nn])
                                else:
                                    qs = stg32.next()
                                    P.act(qs[:, :nn], ps[:, :nn], AF.Identity, scale=sc)
                                    ps2 = P.ps()
                                    P.mm(ps2[:, :nn], perm, qs[:, :nn])
                                    tb = 0 if ch % 2 == 0 else 2
                                    l0 = t0 - CTX
                                    a1 = t1r.next()
                                    a2 = t2r.next()
                                    P.tt(a1[:, :nn], qs[:, :nn], ropet[:, tb, l0:l0 + nn], ALU.mult, eng="pool")
                                    P.tt(a2[:, :nn], ps2[:, :nn], ropet[:, tb + 1, l0:l0 + nn], ALU.mult)
                                    o = stgb.next()
                                    P.tt(o[:, :nn], a1[:, :nn], a2[:, :nn], ALU.add, eng="pool")
                                    P.dma(dd, o[:, :nn])
                            elif name == "mo":
                                o = stgb.next()
                                P.act(o[:, :nn], ps[:, :nn], AF.Sigmoid)
                                P.dma(dd, o[:, :nn])
                            elif name in ("nq", "nk"):
                                o = stgb.next()
                                P.act(o[:, :nn], ps[:, :nn], AF.Identity)
                                P.dma(dd, o[:, :nn])
                            elif name == "lx":
                                o = stg32.next()
                                P.act(o[:, :nn], ps[:, :nn], AF.Identity)
                                P.dma(dd, o[:, :nn])
                            else:
                                a1 = t1r.next()
                                a2 = t2r.next()
                                P.act(a1[:, :nn], ps[:, :nn], AF.Square)
                                P.ts(a1[:, :nn], a1[:, :nn], 0.044715, ALU.mult, 1.0, ALU.add, eng="pool")
                                P.tt(a2[:, :nn], a1[:, :nn], ps[:, :nn], ALU.mult)
                                P.act(a2[:, :nn], a2[:, :nn], AF.Sigmoid, scale=1.5957691216057308)
                                o = stgb.next()
                                P.tt(o[:, :nn], a2[:, :nn], ps[:, :nn], ALU.mult)
                                P.dma(dd, o[:, :nn])
            for name, dst in tm:
                for pnl in range(4):
                    pan = WS.next()
                    for i in range(0 if g == 0 else 2, T // 128):
                        ti = 0 if i < 2 else (1 if i < 6 else 2)
                        t0, nn = TILES_ALL[ti]
                        o0 = i * 128 - t0
                        ps = P.ps()
                        for kc in range(KC):
                            P.mm(ps[:, 0:256], HN[kc][ti][:, o0:o0 + 128], pan[:, kc, :],
                                 start=(kc == 0), stop=(kc == KC - 1))
                        o = stgb.next()
                        P.act(o[:, 0:256], ps[:, 0:256], AF.Identity)
                        r0 = scol(g, i * 128)
                        P.dma(dst[r0:r0 + 128, pnl * 256:(pnl + 1) * 256], o[:, 0:256])

        HNall = V(HNt[:], [b for row in HN for v in row for b in v.bufs])
        Ysb_t = [ar.raw([128, 8, T], BF16, i * 20 * KB) for i in range(3)]
        Ysb = [[P.new(Ysb_t[n][:, c, :]) for c in range(8)] for n in range(3)]
        MX_HI = 197 * KB

        def qtiles_seq(L):
            q = [(0, CTX)] if L == 0 else []
            return q + [(CTX + 512 * k, 512) for k in range(4)]

        def mlstm(L):
            top = Bump(ar, 0, MX_HI)
            NBM = top.v([64, SEQ], F32)
            NM = top.v([64, SEQ], F32)
            bT = top.v([128, 2, NKT * 4], F32)
            keep = top.top
            bm = Bump(ar, keep, MX_HI)
            GI = bm.v([64, SEQ], F32)
            GF = bm.v([64, SEQ], F32)
            Ft = bm.v([64, SEQ], F32)
            Bt = bm.v([64, SEQ], F32)
            ON = bm.v([64, SEQ], F32)
            bi = bm.v([64, 1], F32)
            bf = bm.v([64, 1], F32)
            id4 = bm.v([64, 4], F32)
            P.dma(bi, mbi_d[L])
            P.dma(bf, mbf_d[L])
            P.dma(id4, id4_d)
            P.memset(ON, 1.0)
            P.ts(bf, bf, -1.0, ALU.mult)
            for d in range(2):
                r = slice(32 * d, 32 * d + 4)
                P.dma(GI[r, :], MG[2 * d])
                P.dma(GF[r, :], MG[2 * d + 1])
            for d in range(2):
                r = slice(32 * d, 32 * d + 4)
                P.ts(GI[r], GI[r], bi[r], ALU.add)
                P.act(GF[r], GF[r], AF.Exp, bias=bf[r], scale=-1.0)
                P.act(GF[r], GF[r], AF.Ln, bias=1.0)
                if d == 0:
                    segs = [(slice(0, SEQ), False, None)]
                else:
                    segs = [(slice(0, CTX), True, None), (slice(CTX, SEQ), True, 0)]
                for (sl, rev, ini) in segs:
                    def o(v):
                        w = v[r, sl]
                        return w[:, ::-1] if rev else w
                    i0 = 0.0 if ini is None else Ft[r, ini:ini + 1]
                    P.scan(o(Ft), o(ON), o(GF), i0, ALU.mult, ALU.subtract)
                P.tt(Bt[r], GI[r], Ft[r], ALU.subtract)
                for (sl, rev, ini) in segs:
                    def o(v):
                        w = v[r, sl]
                        return w[:, ::-1] if rev else w
                    i0 = -1e30 if ini is None else GI[r, ini:ini + 1]
                    P.scan(o(GI), o(Bt), o(Bt), i0, ALU.max, ALU.max)
                P.ts(NBM[r], GI[r], -1.0, ALU.mult)
                P.tt(Ft[r], Ft[r], GI[r], ALU.add)
                P.ts(NM[r], Ft[r], -1.0, ALU.mult)
                ps = P.ps()
                for kt in range(NKT):
                    P.mm(ps[:, kt * 4:(kt + 1) * 4], Bt[r, kt * 128:(kt + 1) * 128], id4[r, :])
                P.copy(bT[:, d, :], ps[:, 0:NKT * 4])
            P.barrier()
            bm = Bump(ar, keep, MX_HI)
            masks = bm.v([128, 8, 512], F32)
            sel = bm.v([64, 4, 128], F32)
            P.dma(masks, masks_d)
            P.dma(sel, sel_d)
            KT = bm.vs(2, [128, SEQ], BF16)
            VH = bm.v([128, NKT, 256], BF16)
            QT = bm.vs(2, [128, SEQ], BF16)
            NBt = bm.vs(2, [128, 512], F32)
            EMt = bm.vs(2, [128, 512], F32)
            Dt = Rot(bm.vs(4, [128, 512], F32))
            PT = Rot(bm.vs(4, [128, 512], BF16))
            HSr = Rot([bm.vs(2, [128, 512], F32) for _ in range(2)])
            tmp = Rot(bm.vs(3, [128, 512], F32))
            dntr = Rot(bm.vs(2, [128, 512], F32))
            recr = Rot(bm.vs(2, [128, 512], F32))
            sot = Rot(bm.vs(2, [128, 512], BF16))
            yo = Rot(bm.vs(3, [128, 512], BF16))
            gn = bm.v([128, 8], F32)
            P.dma(gn, mgn_d[L])
            accsets = Rot([P.psl[0:3], P.psl[3:6]])
            sbank = Rot(P.psl[6:8])
            for h in range(4):
                for c in range(2):
                    P.dma(KT[c], MK[2 * h + c])
                    P.dma(QT[c], MQ[2 * h + c])
                P.dma(VH, MV[:, h * 256:(h + 1) * 256].re("(k p) c -> p k c", p=128))
                for (s0, nn) in qtiles_seq(L):
                    for d in range(2):
                        r = slice(32 * d, 32 * d + 4)
                        ps = sbank.next()
                        P.mm(ps[:, :nn], sel[r, h, :], NBM[r, s0:s0 + nn])
                        P.copy(NBt[d][:, :nn], ps[:, :nn], eng="act")
                        ps = sbank.next()
                        P.mm(ps[:, :nn], sel[r, h, :], NM[r, s0:s0 + nn])
                        P.act(EMt[d][:, :nn], ps[:, :nn], AF.Exp)
                    HS = HSr.next()

                    def smm(kt):
                        sb_ = sbank.next()
                        for c in range(2):
                            P.mm(sb_[:, :nn], KT[c][:, kt * 128:(kt + 1) * 128], QT[c][:, s0:s0 + nn],
                                 start=(c == 0), stop=(c == 1))
                        return sb_

                    for d in range(2):
                        sched = []
                        if s0 == 0:
                            sched = [(kt, 4 * d + kt) for kt in range(2)]
                        else:
                            ql = s0 - CTX
                            sched = [(0, None), (1, None)]
                            for j in range(16):
                                o = j - ql // 128
                                if d == 0 and o <= 3:
                                    sched.append((2 + j, o if o >= 0 else None))
                                if d == 1 and o >= 0:
                                    sched.append((2 + j, 4 + o if o <= 3 else None))
                        acc = accsets.next()
                        s_cur = smm(sched[0][0])
                        for i, (kt, mi) in enumerate(sched):
                            s_next = smm(sched[i + 1][0]) if i + 1 < len(sched) else None
                            Dv = Dt.next()
                            P.act(Dv[:, :nn], NBt[d][:, :nn], AF.Exp, bias=bT[:, d, kt * 4 + h:kt * 4 + h + 1])
                            if mi is not None:
                                P.tt(Dv[:, :nn], Dv[:, :nn], masks[:, mi, :nn], ALU.mult, eng="pool")
                            Pv = PT.next()
                            P.tt(Pv[:, :nn], s_cur[:, :nn], Dv[:, :nn], ALU.mult)
                            first, last = i == 0, i == len(sched) - 1
                            P.mm(acc[0][:, :nn], VH[:, kt, 0:128], Pv[:, :nn], start=first, stop=last)
                            P.mm(acc[1][:, :nn], VH[:, kt, 128:256], Pv[:, :nn], start=first, stop=last)
                            P.mm(acc[2][:, :nn], onesb, Pv[:, :nn], start=first, stop=last)
                            s_cur = s_next
                        dnt = dntr.next()
                        rec = recr.next()
                        P.act(dnt[:, :nn], acc[2][:, :nn], AF.Abs)
                        P.tt(dnt[:, :nn], dnt[:, :nn], EMt[d][:, :nn], ALU.max)
                        P.act(dnt[:, :nn], dnt[:, :nn], AF.Ln)
                        P.act(rec[:, :nn], dnt[:, :nn], AF.Exp, scale=-1.0)
                        for c in range(2):
                            if d == 0:
                                P.tt(HS[c][:, :nn], acc[c][:, :nn], rec[:, :nn], ALU.mult)
                            else:
                                tm_ = tmp.next()
                                P.tt(tm_[:, :nn], acc[c][:, :nn], rec[:, :nn], ALU.mult)
                                P.tt(HS[c][:, :nn], HS[c][:, :nn], tm_[:, :nn], ALU.add, eng="pool")
                    ps = sbank.next()
                    for c in range(2):
                        tm_ = tmp.next()
                        P.act(tm_[:, :nn], HS[c][:, :nn], AF.Square)
                        P.mm(ps[:, :nn], ones32, tm_[:, :nn], start=(c == 0), stop=(c == 1))
                    dnt = dntr.next()
                    rec = recr.next()
                    P.act(dnt[:, :nn], ps[:, :nn], AF.Ln, bias=epsc, scale=1.0 / 256)
                    P.act(rec[:, :nn], dnt[:, :nn], AF.Exp, scale=-0.5)
                    for c in range(2):
                        ch = 2 * h + c
                        tm_ = tmp.next()
                        so_ = sot.next()
                        P.dma(so_[:, :nn], SO[ch, :, s0:s0 + nn])
                        P.tt(tm_[:, :nn], HS[c][:, :nn], rec[:, :nn], ALU.mult)
                        y_ = yo.next()
                        P.stt(y_[:, :nn], tm_[:, :nn], gn[:, ch:ch + 1], so_[:, :nn], ALU.mult, ALU.mult)
                        P.dma(YA[0, ch, :, s0:s0 + nn], y_[:, :nn])

        KTW = {0: range(0, 6), 1: range(2, 10), 2: range(6, 14), 3: range(10, 16)}

        def na(L):
            SC = 128.0 ** -0.5
            bm = Bump(ar, 0, MX_HI)
            colm = bm.v([128, 960], F32)
            Mh = bm.v([128, 960], F32)
            P.dma(colm, colm_d)
            pairs = [(qg, ktl) for qg in range(4) for ktl in KTW[qg]]
            TP = {pr: bm.v([128, 512], F32) for pr in pairs}
            MhR = Rot([Mh, bm.v([128, 960], F32)])
            NQr = Rot(bm.vs(2, [128, SEQ], BF16))
            NKr = Rot(bm.vs(2, [128, SEQ], BF16))
            NV4r = Rot(bm.vs(2, [128, NKT, 512], BF16))
            PT = Rot(bm.vs(4, [128, 512], BF16))
            lnr = Rot(bm.vs(2, [128, 512], F32))
            recr = Rot(bm.vs(2, [128, 512], F32))
            yo = Rot(bm.vs(3, [128, 512], BF16))
            for pr in pairs:
                P.memset(TP[pr], NEG)
            valid = {}
            for (qg, ktl) in pairs:
                for half in range(2):
                    j = 2 * ktl + half
                    rows = [r for r in range(8 * qg, 8 * qg + 8)
                            if min(max(r - 4, 0), 24) <= j < min(max(r - 4, 0), 24) + 8]
                    if rows:
                        valid[(qg, ktl, half)] = ((rows[0] - 8 * qg) * 64, len(rows) * 64, (rows[0] - j + 7) * 64)
            accsets = Rot([(P.psl[0], P.psl[1]), (P.psl[2], P.psl[3])])
            sbank = Rot(P.psl[4:8])
            loaded = {}

            def prep(h):
                if h >= 8:
                    return
                if h % 4 == 0:
                    nv = NV4r.next()
                    P.dma(nv, NV[:, (h // 4) * 512:(h // 4 + 1) * 512].re("(k p) c -> p k c", p=128))
                    loaded["nv"] = nv
                nq, nk, mh = NQr.next(), NKr.next(), MhR.next()
                P.dma(nq, NQ[h])
                P.dma(nk, NK[h])
                P.dma(mh[0:64, :], rpbe_d[L, h])
                P.dma(mh[64:128, :], rpbe_d[L, h])
                P.tt(mh, mh, colm, ALU.add, eng="pool")
                loaded[h] = (nq, nk, loaded["nv"], mh)

            prep(0)
            for h in range(8):
                prep(h + 1)
                NQh, NKh, NV4, mh = loaded.pop(h)
                for (s0, nn) in qtiles_seq(L):
                    keys = [(kt, None) for kt in range(2)]
                    if s0 > 0:
                        qg = (s0 - CTX) // 512
                        keys += [(2 + ktl, (qg, ktl)) for ktl in KTW[qg]]
                    pn, pd = accsets.next()

                    def smm(kt):
                        sb_ = sbank.next()
                        P.mm(sb_[:, :nn], NKh[:, kt * 128:(kt + 1) * 128], NQh[:, s0:s0 + nn])
                        return sb_

                    s_cur = smm(keys[0][0])
                    for idx, (kt, pr) in enumerate(keys):
                        s_next = smm(keys[idx + 1][0]) if idx + 1 < len(keys) else None
                        Pv = PT.next()
                        if pr is None:
                            P.act(Pv[:, :nn], s_cur[:, :nn], AF.Exp, scale=SC)
                        else:
                            for half in range(2):
                                vv = valid.get((pr[0], pr[1], half))
                                if vv is None:
                                    continue
                                a_, ln, d0 = vv
                                hp = slice(half * 64, half * 64 + 64)
                                P.stt(TP[pr][hp, a_:a_ + ln], s_cur[hp, a_:a_ + ln], SC, mh[hp, d0:d0 + ln],
                                      ALU.mult, ALU.add)
                            P.act(Pv[:, :nn], TP[pr][:, :nn], AF.Exp)
                        first, last = idx == 0, idx == len(keys) - 1
                        hh = h % 4
                        P.mm(pn[:, :nn], NV4[:, kt, hh * 128:(hh + 1) * 128], Pv[:, :nn], start=first, stop=last)
                        P.mm(pd[:, :nn], onesb, Pv[:, :nn], start=first, stop=last)
                        s_cur = s_next
                    ln_, rec = lnr.next(), recr.next()
                    P.act(ln_[:, :nn], pd[:, :nn], AF.Ln)
                    P.act(rec[:, :nn], ln_[:, :nn], AF.Exp, scale=-1.0)
                    y_ = yo.next()
                    P.tt(y_[:, :nn], pn[:, :nn], rec[:, :nn], ALU.mult)
                    P.dma(YA[1, h, :, s0:s0 + nn], y_[:, :nn])

        def lru(L, jobs=()):
            jobs = list(jobs)
            per = (len(jobs) + 15) // 16
            bm = Bump(ar, 0, 140 * KB)
            LXP = Rot(bm.vs(2, [128, SEQ + 6], F32))
            XCr = Rot(bm.vs(2, [128, SEQ], F32))
            XCbr = Rot(bm.vs(2, [128, SEQ], BF16))
            Rr = bm.vs(2, [128, SEQ], F32)
            Ii = bm.vs(2, [128, SEQ], F32)
            Tm = bm.vs(2, [128, SEQ], F32)
            H0 = bm.v([128, SEQ], F32)
            glg = Rot(bm.vs(2, [128, SEQ], BF16))
            yl = Rot(bm.vs(2, [128, SEQ], BF16))
            W32 = Rot(bm.vs(2, [128, 4, 128], F32))
            Wb = Rot(bm.vs(2, [128, 4, 128], BF16))
            cw = bm.v([128, 8, 4], F32)
            cbias = bm.v([128, 8], F32)
            ba = bm.v([128, 2, 8], F32)
            bx = bm.v([128, 2, 8], F32)
            lam = bm.v([128, 2, 8], F32)
            m8 = bm.v([128, 2, 8], F32)
            m16 = bm.v([128, 2, 8], F32)
            P.dma(cw, lcw_d[L])
            P.dma(cbias, lcb_d[L])
            P.dma(ba, lba_d[L])
            P.dma(bx, lbx_d[L])
            P.dma(lam, llam_d[L])
            P.act(lam, lam, AF.Exp, scale=-1.0)
            P.act(lam, lam, AF.Ln, bias=1.0)
            P.ts(m8, lam, -8.0, ALU.mult)
            P.ts(m16, lam, -16.0, ALU.mult)
            for lx_ in LXP.items:
                P.memset(lx_, 0.0)
            SEGS = [(1, CTX, 0), (CTX + 4, LAT, CTX)]
            STILES = [(0, 256), (256, 512), (768, 512), (1280, 512), (1792, 512)]
            staged = {}

            def stage_a(c):
                if c >= 8:
                    return
                lxp = LXP.next()
                P.dma(lxp[:, 1:1 + CTX], LX[c, :, 0:CTX])
                P.dma(lxp[:, CTX + 4:CTX + 4 + LAT], LX[c, :, CTX:SEQ])
                gl_ = glg.next()
                P.dma(gl_, GLG[c])
                XC, XCb = XCr.next(), XCbr.next()
                for (base, n, o) in SEGS:
                    P.ts(XC[:, o:o + n], lxp[:, base - 1:base - 1 + n], cw[:, c, 0:1], ALU.mult,
                         cbias[:, c:c + 1], ALU.add)
                    for j in range(1, 4):
                        P.stt(XC[:, o:o + n], lxp[:, base - 1 + j:base - 1 + j + n], cw[:, c, j:j + 1],
                              XC[:, o:o + n], ALU.mult, ALU.add)
                P.copy(XCb, XC, eng="act")
                w32 = W32.next()
                wb = Wb.next()
                for n in range(2):
                    P.dma(w32[:, n, :], lwa_d[L, n, c])
                    P.dma(w32[:, 2 + n, :], lwx_d[L, n, c])
                P.copy(wb, w32, eng="pool")
                staged[c] = (XC, XCb, wb, gl_)

            stage_a(0)
            for c in range(8):
                XC, XCb, wb, gl_ = staged.pop(c)
                for n in range(2):
                    for _ in range(per):
                        if jobs:
                            jobs.pop(0)()
                    for (s0, sn) in STILES:
                        psa = P.ps()
                        P.mm(psa[:, :sn], wb[:, n, :], XCb[:, s0:s0 + sn])
                        P.act(Rr[n][:, s0:s0 + sn], psa[:, :sn], AF.Sigmoid, bias=ba[:, n, c:c + 1])
                        psx = P.ps()
                        P.mm(psx[:, :sn], wb[:, 2 + n, :], XCb[:, s0:s0 + sn])
                        P.act(Ii[n][:, s0:s0 + sn], psx[:, :sn], AF.Sigmoid, bias=bx[:, n, c:c + 1])
                stage_a(c + 1)
                for n in range(2):
                    P.act(Tm[n], Rr[n], AF.Exp, scale=m16[:, n, c:c + 1])
                    P.act(Rr[n], Rr[n], AF.Exp, scale=m8[:, n, c:c + 1])
                for n in range(2):
                    P.ts(Tm[n], Tm[n], -1.0, ALU.mult, 1.0, ALU.add, eng="pool")
                for n in range(2):
                    P.act(Tm[n], Tm[n], AF.Sqrt)
                for n in range(2):
                    P.tt(Ii[n], Ii[n], Tm[n], ALU.mult)
                for n in range(2):
                    P.tt(Ii[n], Ii[n], XC, ALU.mult, eng="pool")
                P.scan(H0, Rr[0], Ii[0], 0.0, ALU.mult, ALU.add)
                P.scan(Tm[1][:, 0:CTX][:, ::-1], Rr[1][:, 0:CTX][:, ::-1], Ii[1][:, 0:CTX][:, ::-1], 0.0,
                       ALU.mult, ALU.add)
                P.scan(Tm[1][:, CTX:SEQ][:, ::-1], Rr[1][:, CTX:SEQ][:, ::-1], Ii[1][:, CTX:SEQ][:, ::-1],
                       Tm[1][:, 0:1], ALU.mult, ALU.add)
                P.tt(H0, H0, Tm[1], ALU.add, eng="pool")
                y_ = yl.next()
                P.tt(y_, H0, gl_, ALU.mult)
                P.dma(YA[2, c], y_)
            while jobs:
                jobs.pop(0)()

        def mixers(L):
            P.barrier()
            mlstm(L)
            P.barrier()
            na(L)
            P.barrier()
            jobs = []
            if (not mix_test) and L + 1 < n_layers:
                mod_begin(L + 1, WS2)
                jobs = [(lambda j=j: mod_panel(L + 1, j, WS2)) for j in range(72)]
            lru(L, jobs)
            if jobs:
                mod_finish(L + 1)
            P.barrier()

        def merge(L, g, qtiles):
            sbm = Bump(ar, S_OFF, 197 * KB)
            sgr = Rot(sbm.vs(2, [128, 512], F32))
            mo = Rot(sbm.vs(3, [128, 512], BF16))
            acc_t = sbm.raw([128, 2, T], F32)
            ACC = [[P.new(acc_t[:, o2, t0:t0 + nn]) for (t0, nn) in TILES_ALL] for o2 in range(2)]
            specs = []
            for ocp in range(8):
                for n in range(3):
                    c0 = OFF["gx"] + n * D + ocp * 256
                    specs.append((w_in_d[L, :, c0:c0 + 256], KC, 256, True))
                    specs.append((w_br_d[L, n, :, ocp * 256:(ocp + 1) * 256], 8, 256, True))
            WS.begin(specs)
            for ocp in range(8):
                for n in range(3):
                    pg = WS.next()
                    pb = WS.next()
                    for o2 in range(2):
                        oc = ocp * 2 + o2
                        for ti in qtiles:
                            t0, nn = TILES_ALL[ti]
                            psg = P.ps()
                            psb = P.ps()
                            for kc in range(KC):
                                P.mm(psg[:, :nn], pg[:, kc, o2 * 128:(o2 + 1) * 128], HN[kc][ti],
                                     start=(kc == 0), stop=(kc == KC - 1))
                            for kc in range(8):
                                P.mm(psb[:, :nn], pb[:, kc, o2 * 128:(o2 + 1) * 128], Ysb[n][kc][:, t0:t0 + nn],
                                     start=(kc == 0), stop=(kc == 7))
                            sg = sgr.next()
                            P.act(sg[:, :nn], psg[:, :nn], AF.Sigmoid)
                            if n == 0:
                                P.tt(ACC[o2][ti], sg[:, :nn], psb[:, :nn], ALU.mult)
                            else:
                                P.tt(sg[:, :nn], sg[:, :nn], psb[:, :nn], ALU.mult)
                                if n == 1:
                                    P.tt(ACC[o2][ti], ACC[o2][ti], sg[:, :nn], ALU.add, eng="pool")
                                else:
                                    o = mo.next()
                                    P.tt(o[:, :nn], ACC[o2][ti], sg[:, :nn], ALU.add, eng="pool")
                                    P.dma(MRG[g][oc, :, t0:t0 + nn], o[:, :nn])

        def final_out(g):
            sbm = Bump(ar, S_OFF, 197 * KB)
            sq = Rot(sbm.vs(3, [128, 512], BF16))
            og = Rot(sbm.vs(3, [128, 512], F32))
            rs = sbm.v([128, 512], F32)
            rstd = sbm.v([128, 512], F32)
            for ti in (1, 2):
                t0, nn = TILES_ALL[ti]
                ps = P.ps()
                for kc in range(KC):
                    s = sq.next()
                    P.act(s[:, :nn], X[kc][ti], AF.Square)
                    P.mm(ps[:, :nn], onesb, s[:, :nn], start=(kc == 0), stop=(kc == KC - 1))
                P.act(rs[:, :nn], ps[:, :nn], AF.Sqrt, bias=epsc, scale=1.0 / D)
                P.recip(rstd[:, :nn], rs[:, :nn])
                for kc in range(KC):
                    o = og.next()
                    P.stt(o[:, :nn], X[kc][ti], gfin[:, kc:kc + 1], rstd[:, :nn], ALU.mult, ALU.mult)
                    P.dma(outT_d[g][kc * 128:(kc + 1) * 128, t0 - CTX:t0 - CTX + nn], o[:, :nn])

        def load_X(g, tiles):
            for ti in tiles:
                t0, nn = TILES_ALL[ti]
                xv = V(Xt[:, :, t0:t0 + nn], [X[kc][ti].bufs[0] for kc in range(KC)])
                P.dma(xv, XS[g][:, :, t0:t0 + nn])

        def stageC(L, g):
            last = (L == 1)
            qtiles = [0, 1, 2] if (g == 0 and not last) else [1, 2]
            P.barrier()
            load_X(g, qtiles)
            adaln(L, 1, qtiles, Bump(ar, S_OFF, 197 * KB))
            P.barrier()
            for n in range(3):
                yv = V(Ysb_t[n][:], [b for v in Ysb[n] for b in v.bufs])
                P.dma(yv[:, :, CTX:T], YA[n, :, :, CTX + g * GL:CTX + (g + 1) * GL].re("c p t -> p c t"))
                if 0 in qtiles:
                    P.dma(yv[:, :, 0:CTX], YA[n, :, :, 0:CTX].re("c p t -> p c t"))
            merge(L, g, qtiles)
            P.barrier()
            for ti in qtiles:
                t0, nn = TILES_ALL[ti]
                load_X(g, [ti])
                hv = V(HNt[:, :, t0:t0 + nn], [HN[kc][ti].bufs[0] for kc in range(KC)])
                P.dma(hv, MRG[g][:, :, t0:t0 + nn].re("k p t -> p k t"))
            WS.begin([(w_out_d[L, :, j * 256:(j + 1) * 256], KC, 256, True) for j in range(8)])
            for j in range(8):
                pan = WS.next()
                for o2 in range(2):
                    oc = 2 * j + o2
                    for ti in qtiles:
                        t0, nn = TILES_ALL[ti]
                        ps = P.ps()
                        for kc in range(KC):
                            P.mm(ps[:, :nn], pan[:, kc, o2 * 128:(o2 + 1) * 128], HN[kc][ti],
                                 start=(kc == 0), stop=(kc == KC - 1))
                        P.stt(X[oc][ti], ps[:, :nn], Gmod[L][1][wq(ti)][:, oc:oc + 1], X[oc][ti],
                              ALU.mult, ALU.add)
            P.barrier()
            ffn(L, 2, qtiles, f2in_d, f2out_d)
            if last:
                P.barrier()
                final_out(g)
            else:
                P.dma(XS[g], Xall)
            P.barrier()

        if mix_test:
            mixers(0)
            dump("YA", YA)
        for L in range(0 if mix_test else n_layers):
            for g in groups:
                gt = [0, 1, 2] if g == 0 else [1, 2]
                P.barrier()
                if L == 0:
                    P.dma(Xlat, xT_d[g].re("(k p) t -> p k t", p=128))
                    if g == 0:
                        P.dma(Xctx, ctxT_d.re("(k p) t -> p k t", p=128))
                else:
                    load_X(g, gt)
                ffn(L, 0, gt, f1in_d, f1out_d)
                P.barrier()
                inproj(L, g)
                P.barrier()
            mixers(L)
            for g in groups:
                stageC(L, g)
        P.finalize(st)
    return nc


def _fm(v):
    v = np.asarray(v, np.float32)
    n = v.shape[-1] // 128
    return np.ascontiguousarray(np.swapaxes(v.reshape(v.shape[:-1] + (n, 128)), -1, -2))


def host_consts():
    c = {}
    perm = np.zeros((128, 128), np.float32)
    for m in range(128):
        perm[(m + 64) % 128, m] = 1.0
    c["perm"] = perm
    s = np.arange(128)[:, None]
    t = np.arange(512)[None, :]
    masks = np.zeros((128, 8, 512), np.float32)
    for o in range(4):
        masks[:, o, :] = (o * 128 + s <= t)
        masks[:, 4 + o, :] = (o * 128 + s >= t)
    c["masks"] = masks
    inv = 10000.0 ** (-np.arange(64, dtype=np.float64) / 64)
    invp = np.concatenate([inv, inv])
    sgn = np.concatenate([-np.ones(64), np.ones(64)])
    pos = np.arange(LAT)
    rope = np.zeros((2, 128, 4, GL), np.float32)
    for g in range(2):
        pg = pos[g * GL:(g + 1) * GL]
        for k, pp in enumerate((pg // 64, pg % 64)):
            ang = (pp[None, :].astype(np.float32) * invp[:, None].astype(np.float32)).astype(np.float64)
            rope[g, :, 2 * k, :] = np.cos(ang)
            rope[g, :, 2 * k + 1, :] = np.sin(ang) * sgn[:, None]
    c["rope"] = rope
    sel = np.zeros((64, 4, 128), np.float32)
    id4 = np.zeros((64, 4), np.float32)
    for base in (0, 32):
        for h in range(4):
            sel[base + h, h, :] = 1.0
            id4[base + h, h] = 1.0
    c["sel"] = sel
    c["id4"] = id4
    kc = np.arange(64)[:, None]
    qc = np.arange(64)[None, :]
    cs = np.clip(qc - 8, 0, 48)
    ok = (kc >= cs) & (kc < cs + 16)
    cm = np.where(ok, 0.0, NEG).astype(np.float32)
    cm = np.broadcast_to(cm[:, None, :], (64, 15, 64)).reshape(64, 15 * 64)
    c["colm"] = np.ascontiguousarray(np.concatenate([cm, cm], 0))
    return c


def prep_inputs(inp, n_cores=4):
    f = lambda k: np.asarray(inp[k], np.float32)
    shared = host_consts()
    for k in ("w_mod", "ffn1_w_in", "ffn1_w_out", "ffn2_w_in", "ffn2_w_out", "w_in", "w_branch", "w_out"):
        shared[k] = np.ascontiguousarray(f(k))
    shared["bmod"] = _fm(f("b_mod"))
    shared["gains"] = np.ascontiguousarray(np.stack([_fm(f("norm_ffn1")), _fm(f("norm_mix")), _fm(f("norm_ffn2"))], 1))
    shared["gfin"] = _fm(f("norm_final"))
    mb = np.zeros((2, 2, 64, 1), np.float32)
    for wi, k in enumerate(("mlstm_b_i", "mlstm_b_f")):
        v = f(k)
        for d in range(2):
            mb[wi, :, 32 * d:32 * d + 4, 0] = v[:, d, :]
    shared["mbi"] = mb[0]
    shared["mbf"] = mb[1]
    shared["mgn"] = _fm(f("mlstm_gn"))
    rpb = f("na_rpb")
    kc = np.arange(64)[:, None]
    qc = np.arange(64)[None, :]
    dc = np.clip(kc - qc, -15, 15) + 15
    e = rpb[:, :, ::-1, :][:, :, :, dc]
    shared["rpbe"] = np.ascontiguousarray(np.transpose(e, (0, 1, 3, 2, 4)).reshape(2, 8, 64, 15 * 64))
    shared["lcw"] = np.ascontiguousarray(np.transpose(_fm(f("lru_conv_w")), (0, 2, 3, 1)))
    shared["lcb"] = _fm(f("lru_conv_b"))
    shared["lwa"] = np.ascontiguousarray(f("lru_w_a"))
    shared["lwx"] = np.ascontiguousarray(f("lru_w_x"))
    for k, src in (("lba", "lru_b_a"), ("lbx", "lru_b_x"), ("llam", "lru_lambda")):
        shared[k] = np.ascontiguousarray(np.transpose(_fm(f(src)), (0, 2, 1, 3)))
    x, c, ctx, c_ctx = f("x"), f("c"), f("ctx"), f("c_ctx")
    maps = []
    for core in range(n_cores):
        b = core
        m = dict(shared)
        xt = np.ascontiguousarray(x[b].T)
        m["xT"] = np.ascontiguousarray(np.stack([xt[:, 0:GL], xt[:, GL:2 * GL]], 0))
        m["ctxT"] = np.ascontiguousarray(ctx[b].T)
        m["ccol"] = np.ascontiguousarray(np.stack([_fm(c[b]), _fm(c_ctx)], -1))
        maps.append(m)
    return maps


_NC_CACHE = {}


def kernel(**inputs):
    n_cores = 4
    if "nc" not in _NC_CACHE:
        _NC_CACHE["nc"] = build()
    nc = _NC_CACHE["nc"]
    maps = prep_inputs(inputs, n_cores)
    res = run_bass_kernel_spmd(nc, maps, core_ids=list(range(n_cores)))
    out = np.zeros((n_cores, LAT, D), np.float32)
    for b in range(n_cores):
        o = np.asarray(res.results[b]["outT"], np.float32)
        out[b] = np.concatenate([o[0].T, o[1].T], 0)
    return out
```

```python
import contextlib
import numpy as np
import concourse.bass as bass
import concourse.mybir as mybir
from concourse.bass_utils import run_bass_kernel_spmd

F32 = mybir.dt.float32
BF16 = mybir.dt.bfloat16
AF = mybir.ActivationFunctionType
ALU = mybir.AluOpType

COMPUTE = ("pe", "act", "dve", "pool")
DMAQ = ("sp", "act", "pool")
NRING = 8
RING = {"sp": 8, "act": 8, "pool": 4}


class Buf:
    __slots__ = ("wc", "wd", "rc", "rd")

    def __init__(self):
        self.wc = None
        self.wd = {}
        self.rc = {}
        self.rd = {}


class V:
    __slots__ = ("ap", "bufs")

    def __init__(self, ap, bufs):
        self.ap = ap
        self.bufs = tuple(bufs)

    def __getitem__(self, idx):
        return V(self.ap[idx], self.bufs)

    def re(self, pattern, **kw):
        return V(self.ap.rearrange(pattern, **kw), self.bufs)


def _ap(x):
    return x.ap if isinstance(x, V) else x


class Op:
    __slots__ = ("eng", "fn", "deps", "sig", "sigval", "is_dma", "qidx", "id")


class Prog:
    def __init__(self, nc):
        self.nc = nc
        self.ops = []
        self.by_eng = {e: [] for e in ("pe", "act", "dve", "pool", "sp")}
        self.ndma = {e: 0 for e in DMAQ}
        self.bar = {}
        self.psl = []
        self.psi = 0

    def new(self, ap):
        return V(ap, [Buf()])

    def ps(self):
        v = self.psl[self.psi % len(self.psl)]
        self.psi += 1
        return v

    def barrier(self):
        deps = []
        for e in ("pe", "act", "dve", "pool", "sp"):
            lst = self.by_eng[e]
            seen_c = False
            nd = 0
            for op in reversed(lst):
                if op.is_dma:
                    if nd < NRING:
                        deps.append(op.id)
                        nd += 1
                elif not seen_c:
                    deps.append(op.id)
                    seen_c = True
                if seen_c and nd >= NRING:
                    break
        self.bar = {e: list(deps) for e in ("pe", "act", "dve", "pool", "sp")}

    def emit(self, eng, fn, reads=(), writes=(), dma=False):
        op = Op()
        op.eng = eng
        op.fn = fn
        op.sig = False
        op.sigval = None
        op.is_dma = dma
        op.id = len(self.ops)
        op.qidx = None
        if dma:
            op.qidx = self.ndma[eng]
            self.ndma[eng] += 1
        ops = self.ops
        ceng = {}
        dmadeps = set()

        def add(i):
            d = ops[i]
            if d.is_dma:
                dmadeps.add(i)
            else:
                if d.eng == "pe" and eng == "pe" and not dma:
                    return
                if ceng.get(d.eng, -1) < i:
                    ceng[d.eng] = i

        pend = self.bar.pop(eng, None)
        if pend:
            for i in pend:
                add(i)
        for v in reads:
            if not isinstance(v, V):
                continue
            for b in v.bufs:
                if b.wc is not None:
                    add(b.wc)
                for l in b.wd.values():
                    for i in l:
                        add(i)
        for v in writes:
            for b in v.bufs:
                had_reads = bool(b.rc) or bool(b.rd)
                for i in b.rc.values():
                    add(i)
                for l in b.rd.values():
                    for i in l:
                        add(i)
                if b.wc is not None:
                    add(b.wc)
                if (not dma) or had_reads:
                    for l in b.wd.values():
                        for i in l:
                            add(i)
        deps = list(ceng.values()) + list(dmadeps)
        for i in deps:
            ops[i].sig = True
        op.deps = deps
        for v in reads:
            if not isinstance(v, V):
                continue
            for b in v.bufs:
                if dma:
                    l = b.rd.setdefault(eng, [])
                    l.append(op.id)
                    if len(l) > RING[eng]:
                        del l[0]
                else:
                    b.rc[eng] = op.id
        for v in writes:
            for b in v.bufs:
                if dma:
                    if b.rc or b.rd:
                        b.wd = {}
                    b.wc = None
                    l = b.wd.setdefault(eng, [])
                    l.append(op.id)
                    if len(l) > RING[eng]:
                        del l[0]
                else:
                    b.wc = op.id
                    b.wd = {}
                b.rc = {}
                b.rd = {}
        ops.append(op)
        self.by_eng[eng].append(op)
        return op

    def mm(self, out, lhsT, rhs, start=True, stop=True):
        o, l, r = _ap(out), _ap(lhsT), _ap(rhs)
        rd = [lhsT, rhs] + ([] if start else [out])
        self.emit("pe", lambda e: e.matmul(o, l, r, start=start, stop=stop), rd, [out])

    def act(self, out, in_, func, bias=None, scale=None):
        o, i = _ap(out), _ap(in_)
        kw = {}
        rd = [in_]
        if bias is not None:
            kw["bias"] = _ap(bias)
            rd.append(bias)
        if scale is not None:
            kw["scale"] = _ap(scale)
            rd.append(scale)
        self.emit("act", lambda e: e.activation(o, i, func, **kw), rd, [out])

    def tt(self, out, in0, in1, op, eng="dve"):
        o, a, b = _ap(out), _ap(in0), _ap(in1)
        self.emit(eng, lambda e: e.tensor_tensor(o, a, b, op), [in0, in1], [out])

    def ts(self, out, in0, s1, op0, s2=None, op1=None, eng="dve"):
        o, a = _ap(out), _ap(in0)
        rd = [in0, s1, s2]
        a1, a2 = _ap(s1), _ap(s2)
        if op1 is None:
            self.emit(eng, lambda e: e.tensor_scalar(o, a, a1, None, op0), rd, [out])
        else:
            self.emit(eng, lambda e: e.tensor_scalar(o, a, a1, a2, op0, op1), rd, [out])

    def stt(self, out, in0, scalar, in1, op0, op1):
        o, a, s, b = _ap(out), _ap(in0), _ap(scalar), _ap(in1)
        self.emit("dve", lambda e: e.scalar_tensor_tensor(o, a, s, b, op0, op1), [in0, scalar, in1], [out])

    def scan(self, out, d0, d1, initial, op0, op1):
        o, a, b, i = _ap(out), _ap(d0), _ap(d1), _ap(initial)
        self.emit("dve", lambda e: e.tensor_tensor_scan(o, a, b, i, op0, op1), [d0, d1, initial], [out])

    def copy(self, out, in_, eng="dve"):
        o, i = _ap(out), _ap(in_)
        if eng == "act":
            self.emit("act", lambda e: e.activation(o, i, AF.Identity), [in_], [out])
        else:
            self.emit(eng, lambda e: e.tensor_copy(o, i), [in_], [out])

    def recip(self, out, in_):
        o, i = _ap(out), _ap(in_)
        self.emit("dve", lambda e: e.reciprocal(o, i), [in_], [out])

    def memset(self, out, val, eng="pool"):
        o = _ap(out)
        self.emit(eng, lambda e: e.memset(o, val), [], [out])

    def dma(self, out, in_, q="pool", **kw):
        o, i = _ap(out), _ap(in_)
        self.emit(q, lambda e: e.dma_start(o, i, **kw), [in_], [out], dma=True)

    def finalize(self, stack):
        nc = self.nc
        sems = {e: stack.enter_context(nc.semaphore("s_" + e)) for e in COMPUTE}
        rings = {q: [stack.enter_context(nc.semaphore("r_%s%d" % (q, k))) for k in range(RING[q])]
                 for q in DMAQ if self.ndma[q] > 0}
        cnt = {e: 0 for e in COMPUTE}
        for op in self.ops:
            if not op.is_dma and op.sig:
                cnt[op.eng] += 1
                op.sigval = cnt[op.eng]
        ops = self.ops
        by_eng = self.by_eng
        ndma = self.ndma

        def run(eng_name, e):
            waited = {}

            def wait(s, v):
                k = id(s)
                if waited.get(k, 0) < v:
                    e.wait_ge(s, v)
                    waited[k] = v

            for op in by_eng[eng_name]:
                if op.is_dma and op.qidx >= RING[eng_name]:
                    nr = RING[eng_name]
                    wait(rings[eng_name][op.qidx % nr], 16 * (op.qidx // nr))
                for d in op.deps:
                    dop = ops[d]
                    if dop.is_dma:
                        nr = RING[dop.eng]
                        wait(rings[dop.eng][dop.qidx % nr], 16 * (dop.qidx // nr + 1))
                    else:
                        wait(sems[dop.eng], dop.sigval)
                inst = op.fn(e)
                if op.is_dma:
                    inst.then_inc(rings[eng_name][op.qidx % RING[eng_name]], 16)
                elif op.sig:
                    inst.then_inc(sems[eng_name], 1)
            if eng_name in rings:
                n = ndma[eng_name]
                nr = RING[eng_name]
                for k in range(min(n, nr)):
                    uses = (n - 1 - k) // nr + 1
                    wait(rings[eng_name][k], 16 * uses)

        with nc.Block() as block:
            @block.tensor
            def _(e):
                run("pe", e)

            @block.scalar
            def _(e):
                run("act", e)

            @block.vector
            def _(e):
                run("dve", e)

            @block.gpsimd
            def _(e):
                run("pool", e)

            @block.sync
            def _(e):
                run("sp", e)


D = 2048
KC = 16
DFF = 5632
PIN = 15376
BW = 1024
LAT = 2048
CTX = 256
GL = 1024
T = CTX + GL
SEQ = CTX + LAT
NKT = SEQ // 128
EPS = 1e-6
TILES_ALL = [(0, 256), (256, 512), (768, 512)]
OFF = dict(mq=0, mk=1024, mv=2048, mo=3072, mg=4096, nq=4112, nk=5136, nv=6160, lx=7184, lg=8208, gx=9232)
NEG = -30000.0
KB = 1024


class Arena:
    def __init__(self, nc, P, stack, nbytes):
        self.nc = nc
        self.P = P
        self.base = (nc.sbuf_base + 31) // 32 * 32
        stack.enter_context(nc.sbuf_tensor("arena", [128, nbytes], mybir.dt.uint8))
        assert nc.sbuf_base == self.base + nbytes, (nc.sbuf_base, self.base, nbytes)
        self.nbytes = nbytes
        self.n = 0

    def raw(self, shape, dtype, off):
        esz = 4 if dtype == F32 else 2
        sz = esz * int(np.prod(shape[1:]))
        assert off % 4 == 0 and off + sz <= self.nbytes, (shape, off, sz, self.nbytes)
        self.n += 1
        return self.nc.alloc_sbuf_tensor_at("a%d" % self.n, list(shape), dtype, offset=self.base + off)

    def v(self, shape, dtype, off):
        return self.P.new(self.raw(shape, dtype, off)[:])


class Bump:
    def __init__(self, ar, lo, hi):
        self.ar, self.lo, self.hi, self.top = ar, lo, hi, lo

    def raw(self, shape, dtype):
        esz = 4 if dtype == F32 else 2
        sz = (esz * int(np.prod(shape[1:])) + 31) // 32 * 32
        assert self.top + sz <= self.hi, ("bump overflow", shape, self.top, sz, self.hi)
        t = self.ar.raw(shape, dtype, self.top)
        self.top += sz
        return t

    def v(self, shape, dtype):
        return self.ar.P.new(self.raw(shape, dtype)[:])

    def vs(self, n, shape, dtype):
        return [self.v(shape, dtype) for _ in range(n)]


class Rot:
    def __init__(self, items):
        self.items, self.i = items, 0

    def next(self):
        v = self.items[self.i % len(self.items)]
        self.i += 1
        return v


class WStream:
    def __init__(self, P, stages, bfs):
        self.P = P
        self.stages = Rot(stages)
        self.bfs = Rot(bfs)
        self.specs = []
        self.i = 0
        self.loaded = {}
        self.ci = 0

    def begin(self, specs):
        self.specs = list(specs)
        self.i = 0
        self.loaded = {}
        self._load(0)

    def _load(self, j):
        if j >= len(self.specs) or j in self.loaded:
            return
        src, nk, nc_, cast = self.specs[j]
        P = self.P
        st = self.stages.next()
        stv = V(st.ap[:, 0:nk * nc_].rearrange("p (k c) -> p k c", k=nk), st.bufs)
        P.dma(stv, src.re("(k p) c -> p k c", p=128), q="sp")
        if not cast:
            self.loaded[j] = stv
            return
        bf = self.bfs.next()
        bfv = V(bf.ap[:, 0:nk * nc_].rearrange("p (k c) -> p k c", k=nk), bf.bufs)
        P.copy(bfv, stv, eng="act")
        self.loaded[j] = bfv

    def next(self):
        j = self.i
        self.i += 1
        self._load(j)
        self._load(j + 1)
        return self.loaded.pop(j)


def build(n_layers=2, groups=(0, 1), stop=None, dbg=(), mix_test=False):
    nc = bass.Bass("TRN2", target_bir_lowering=False)
    P = Prog(nc)
    NG = 2

    BIGW = ("w_mod", "ffn1_w_in", "ffn1_w_out", "ffn2_w_in", "ffn2_w_out", "w_in", "w_branch", "w_out", "xT", "ctxT")

    def din(name, shape, dt=F32):
        if mix_test and name in BIGW:
            return None
        return V(nc.dram_tensor(name, list(shape), dt, kind="ExternalInput").ap(), [Buf()])

    def dscr(name, shape, dt):
        return nc.dram_tensor(name, list(shape), dt, kind="Internal").ap()

    xT_d = din("xT", [NG, D, GL])
    ctxT_d = din("ctxT", [D, CTX])
    ccol_d = din("ccol", [128, KC, 2])
    w_mod_d = din("w_mod", [2, D, 9 * D])
    bmod_d = din("bmod", [2, 128, 144])
    gains_d = din("gains", [2, 3, 128, KC])
    gfin_d = din("gfin", [128, KC])
    f1in_d = din("ffn1_w_in", [2, D, 2 * DFF])
    f1out_d = din("ffn1_w_out", [2, DFF, D])
    f2in_d = din("ffn2_w_in", [2, D, 2 * DFF])
    f2out_d = din("ffn2_w_out", [2, DFF, D])
    w_in_d = din("w_in", [2, D, PIN])
    w_br_d = din("w_branch", [2, 3, BW, D])
    w_out_d = din("w_out", [2, D, D])
    mbi_d = din("mbi", [2, 64, 1])
    mbf_d = din("mbf", [2, 64, 1])
    mgn_d = din("mgn", [2, 128, 8])
    rpbe_d = din("rpbe", [2, 8, 64, 15 * 64])
    colm_d = din("colm", [128, 15 * 64])
    lcw_d = din("lcw", [2, 128, 8, 4])
    lcb_d = din("lcb", [2, 128, 8])
    lwa_d = din("lwa", [2, 2, 8, 128, 128])
    lwx_d = din("lwx", [2, 2, 8, 128, 128])
    lba_d = din("lba", [2, 128, 2, 8])
    lbx_d = din("lbx", [2, 128, 2, 8])
    llam_d = din("llam", [2, 128, 2, 8])
    perm_d = din("perm", [128, 128])
    masks_d = din("masks", [128, 8, 512])
    rope_d = din("rope", [NG, 128, 4, GL])
    sel_d = din("sel", [64, 4, 128])
    id4_d = din("id4", [64, 4])
    outT = nc.dram_tensor("outT", [NG, D, GL], F32, kind="ExternalOutput").ap()
    outT_d = [V(outT[g], [Buf()]) for g in range(NG)]
    dbg_d = {}
    for name, shape, dt in dbg:
        dbg_d[name] = V(nc.dram_tensor("dbg_" + name, list(shape), dt, kind="ExternalOutput").ap(), [Buf()])

    def scr1(name, shape, dt, ext_in=False):
        if mix_test and ext_in:
            return V(nc.dram_tensor(name, list(shape), dt, kind="ExternalInput").ap(), [Buf()])
        return V(dscr(name, shape, dt), [Buf()])

    XS = [scr1("XS%d" % g, [128, KC, T], F32) for g in range(NG)]
    MRG = [scr1("MRG%d" % g, [KC, 128, T], BF16) for g in range(NG)]
    MQ = scr1("MQ", [8, 128, SEQ], BF16, True)
    SO = scr1("SO", [8, 128, SEQ], BF16, True)
    NQ = scr1("NQ", [8, 128, SEQ], BF16, True)
    GLG = scr1("GLG", [8, 128, SEQ], BF16, True)
    MK = scr1("MK", [8, 128, SEQ], BF16, True)
    NK = scr1("NK", [8, 128, SEQ], BF16, True)
    MV = scr1("MV", [SEQ, BW], BF16, True)
    NV = scr1("NV", [SEQ, BW], BF16, True)
    MG = scr1("MG", [4, 4, SEQ], F32, True)
    LX = scr1("LX", [8, 128, SEQ], F32, True)
    YA = scr1("YA", [3, 8, 128, SEQ], BF16)

    def scol(g, t0):
        return t0 if t0 < CTX else CTX + g * GL + (t0 - CTX)

    with contextlib.ExitStack() as st:
        P.psl = [P.new(st.enter_context(nc.psum_tensor("ps%d" % i, [128, 512], F32))[:]) for i in range(8)]
        ARENA_BYTES = 206 * KB
        ar = Arena(nc, P, st, ARENA_BYTES)
        cb = Bump(ar, 197 * KB, ARENA_BYTES)
        ones32 = cb.v([128, 128], F32)
        onesb = cb.v([128, 128], BF16)
        cact = cb.v([128, KC, 2], F32)
        MOD = [cb.v([128, 144, 2], F32) for _ in range(2)]
        Amod = [[[cb.v([128, KC], F32) for w in range(2)] for n in range(3)] for L in range(2)]
        Gmod = [[[cb.v([128, KC], F32) for w in range(2)] for n in range(3)] for L in range(2)]
        gains = cb.v([128, 2, 3, KC], F32)
        gfin = cb.v([128, KC], F32)
        bmod = cb.v([128, 2, 144], F32)
        epsc = cb.v([128, 1], F32)
        P.memset(ones32, 1.0)
        P.memset(onesb, 1.0)
        P.memset(epsc, EPS)
        P.dma(gains, gains_d.re("l n p k -> p l n k"))
        P.dma(gfin, gfin_d)
        P.dma(bmod, bmod_d.re("l p c -> p l c"))

        X_OFF, HN_OFF, W_OFF = 0, 80 * KB, 120 * KB
        Xt = ar.raw([128, KC, T], F32, X_OFF)
        HNt = ar.raw([128, KC, T], BF16, HN_OFF)
        X = [[P.new(Xt[:, kc, t0:t0 + n]) for (t0, n) in TILES_ALL] for kc in range(KC)]
        HN = [[P.new(HNt[:, kc, t0:t0 + n]) for (t0, n) in TILES_ALL] for kc in range(KC)]
        Xall = V(Xt[:], [b for row in X for v in row for b in v.bufs])
        Xlat = V(Xt[:, :, CTX:T], [b for row in X for v in row[1:] for b in v.bufs])
        Xctx = V(Xt[:, :, 0:CTX], [row[0].bufs[0] for row in X])
        wst = [ar.v([128, 4096], F32, W_OFF + i * 16 * KB) for i in range(2)]
        wbf = [ar.v([128, 4096], BF16, W_OFF + 32 * KB + i * 8 * KB) for i in range(3)]
        WS = WStream(P, wst, wbf)
        S_OFF = 176 * KB

        def wq(tile_i):
            return 1 if tile_i == 0 else 0

        def dump(name, src):
            if name in dbg_d:
                P.dma(dbg_d[name], src)

        sb0 = Bump(ar, S_OFF, 197 * KB)
        cc = sb0.v([128, KC, 2], F32)
        P.dma(cc, ccol_d)
        P.act(cact, cc, AF.Silu)
        idc = cb.v([64, 4], F32)
        P.dma(idc, id4_d)
        rwr = Rot(cb.vs(2, [2, 256], F32))

        wst2 = [ar.v([128, 4096], F32, 160 * KB + i * 16 * KB) for i in range(2)]
        WS2 = WStream(P, wst2, wbf)

        def mod_begin(L, ws=None):
            (ws or WS).begin([(w_mod_d[L, :, j * 256:(j + 1) * 256], KC, 256, False) for j in range(72)])

        def mod_panel(L, j, ws=None, bank=None):
            pan = (ws or WS).next()
            ps = bank or P.ps()
            for kc in range(KC):
                P.mm(ps[0:2, 0:256], cact[:, kc, :], pan[:, kc, :], start=(kc == 0), stop=(kc == KC - 1))
            rw = rwr.next()
            P.copy(rw, ps[0:2, 0:256], eng="act")
            for o2 in range(2):
                ps2 = bank or P.ps()
                P.mm(ps2[:, 0:2], rw[0:2, o2 * 128:(o2 + 1) * 128], idc[0:2, 0:2])
                ch = j * 2 + o2
                P.ts(MOD[L][:, ch, :], ps2[:, 0:2], bmod[:, L, ch:ch + 1], ALU.add)

        def mod_finish(L):
            for n in range(3):
                for w in range(2):
                    sc = MOD[L][:, (3 * n + 1) * KC:(3 * n + 2) * KC, w]
                    gt = MOD[L][:, (3 * n + 2) * KC:(3 * n + 3) * KC, w]
                    P.stt(Amod[L][n][w], sc, 1.0, gains[:, L, n, :], ALU.add, ALU.mult)
                    P.ts(Gmod[L][n][w], gt, 1.0 if n == 1 else 0.5, ALU.mult)

        if not mix_test:
            mod_begin(0)
            for j in range(72):
                mod_panel(0, j)
            mod_finish(0)
        dump("mod0", MOD[0])

        def shift_ap(L, n, w, kc):
            return MOD[L][:, 3 * n * KC + kc, w:w + 1]

        def adaln(L, n, tiles, sbm, pre=None):
            if pre is None:
                sq = Rot(sbm.vs(3, [128, 512], BF16))
                tmp = Rot(sbm.vs(2, [128, 512], F32))
                rs = sbm.v([128, 512], F32)
                rstd = sbm.v([128, 512], F32)
            else:
                sq, tmp, rs, rstd = pre
            for ti in tiles:
                t0, nn = TILES_ALL[ti]
                w = wq(ti)
                ps = P.ps()
                for kc in range(KC):
                    s = sq.next()
                    P.act(s[:, :nn], X[kc][ti], AF.Square)
                    P.mm(ps[:, :nn], onesb, s[:, :nn], start=(kc == 0), stop=(kc == KC - 1))
                P.act(rs[:, :nn], ps[:, :nn], AF.Sqrt, bias=epsc, scale=1.0 / D)
                P.recip(rstd[:, :nn], rs[:, :nn])
                for kc in range(KC):
                    tm = tmp.next()
                    P.tt(tm[:, :nn], X[kc][ti], rstd[:, :nn], ALU.mult)
                    P.act(HN[kc][ti], tm[:, :nn], AF.Identity,
                          bias=shift_ap(L, n, w, kc), scale=Amod[L][n][w][:, kc:kc + 1])

        def ffn(L, n, tiles, win_d, wout_d):
            specs = []
            for grp in range(11):
                for half in range(2):
                    c0 = (grp * 4 + half * 2) * 128
                    specs.append((win_d[L, :, c0:c0 + 256], KC, 256, True))
                    specs.append((win_d[L, :, DFF + c0:DFF + c0 + 256], KC, 256, True))
                for ob in range(4):
                    specs.append((wout_d[L, grp * 512:(grp + 1) * 512, ob * 512:(ob + 1) * 512], 4, 512, True))
            WS.begin(specs)
            sbm = Bump(ar, S_OFF, 197 * KB)
            sq = Rot(sbm.vs(3, [128, 512], BF16))
            rs = sbm.v([128, 512], F32)
            rstd = sbm.v([128, 512], F32)
            sil = Rot(sbm.vs(2, [128, 512], F32))
            actg_t = sbm.raw([128, 4, T], BF16)
            ACTG = [[P.new(actg_t[:, j, t0:t0 + nn]) for (t0, nn) in TILES_ALL] for j in range(4)]
            adaln(L, n, tiles, None, pre=(sq, sil, rs, rstd))
            for grp in range(11):
                for half in range(2):
                    pg = WS.next()
                    pu = WS.next()
                    for jj in range(2):
                        jl = half * 2 + jj
                        for ti in tiles:
                            t0, nn = TILES_ALL[ti]
                            psg = P.ps()
                            psu = P.ps()
                            for kc in range(KC):
                                P.mm(psg[:, :nn], pg[:, kc, jj * 128:(jj + 1) * 128], HN[kc][ti],
                                     start=(kc == 0), stop=(kc == KC - 1))
                            for kc in range(KC):
                                P.mm(psu[:, :nn], pu[:, kc, jj * 128:(jj + 1) * 128], HN[kc][ti],
                                     start=(kc == 0), stop=(kc == KC - 1))
                            s = sil.next()
                            P.act(s[:, :nn], psg[:, :nn], AF.Silu)
                            P.tt(ACTG[jl][ti], s[:, :nn], psu[:, :nn], ALU.mult)
                for ob in range(4):
                    pw = WS.next()
                    for o4 in range(4):
                        oc = ob * 4 + o4
                        for ti in tiles:
                            t0, nn = TILES_ALL[ti]
                            ps = P.ps()
                            for j in range(4):
                                P.mm(ps[:, :nn], pw[:, j, o4 * 128:(o4 + 1) * 128], ACTG[j][ti],
                                     start=(j == 0), stop=(j == 3))
                            P.stt(X[oc][ti], ps[:, :nn], Gmod[L][n][wq(ti)][:, oc:oc + 1], X[oc][ti],
                                  ALU.mult, ALU.add)

        def inproj(L, g):
            gt = [0, 1, 2] if g == 0 else [1, 2]
            sbm = Bump(ar, S_OFF, 197 * KB)
            adaln(L, 1, gt, sbm)
            P.dma(XS[g], Xall)
            fm = [("mq", MQ), ("mk", MK), ("mo", SO), ("nq", NQ), ("nk", NK), ("lx", LX), ("lg", GLG)]
            tm = [("mv", MV), ("nv", NV)]
            specs = []
            for name, _ in fm + tm:
                for pnl in range(4):
                    c0 = OFF[name] + pnl * 256
                    specs.append((w_in_d[L, :, c0:c0 + 256], KC, 256, True))
            WS.begin(specs)
            P.barrier()
            sbm = Bump(ar, X_OFF, 80 * KB)
            ropet = sbm.v([128, 4, GL], F32)
            P.dma(ropet, rope_d[g])
            perm = sbm.v([128, 128], F32)
            P.dma(perm, perm_d)
            stg32 = Rot(sbm.vs(2, [128, 512], F32))
            stgb = Rot(sbm.vs(3, [128, 512], BF16))
            t1r = Rot(sbm.vs(2, [128, 512], F32))
            t2r = Rot(sbm.vs(2, [128, 512], F32))
            wg32 = sbm.v([128, KC, 16], F32)
            wgb = sbm.v([128, KC, 16], BF16)
            g4 = Rot(sbm.vs(2, [4, 512], F32))
            P.dma(wg32, w_in_d[L, :, OFF["mg"]:OFF["mg"] + 16].re("(k p) c -> p k c", p=128))
            P.copy(wgb, wg32, eng="dve")
            for dg in range(4):
                for ti in gt:
                    t0, nn = TILES_ALL[ti]
                    ps = P.ps()
                    for kc in range(KC):
                        P.mm(ps[0:4, :nn], wgb[:, kc, dg * 4:(dg + 1) * 4], HN[kc][ti],
                             start=(kc == 0), stop=(kc == KC - 1))
                    s = g4.next()
                    P.copy(s[:, :nn], ps[0:4, :nn], eng="dve")
                    P.dma(MG[dg, :, scol(g, t0):scol(g, t0) + nn], s[:, :nn])
            for name, dst in fm:
                for pnl in range(4):
                    pan = WS.next()
                    for o2 in range(2):
                        ch = pnl * 2 + o2
                        for ti in gt:
                            t0, nn = TILES_ALL[ti]
                            ps = P.ps()
                            for kc in range(KC):
                                P.mm(ps[:, :nn], pan[:, kc, o2 * 128:(o2 + 1) * 128], HN[kc][ti],
                                     start=(kc == 0), stop=(kc == KC - 1))
                            dd = dst[ch, :, scol(g, t0):scol(g, t0) + nn]
                            if name in ("mq", "mk"):
                                sc = 1.0 if name == "mq" else 1.0 / 16.0
                                if ti == 0:
                                    o = stgb.next()
                                    P.act(o[:, :nn], ps[:, :nn], AF.Identity, scale=sc)
                                    P.dma(dd, o[:, :nn])
                                else:
                                    qs = stg32.next()
                                    P.act(qs[:, :nn], ps[:, :nn], AF.Identity, scale=sc)
                                    ps2 = P.ps()
                                    P.mm(ps2[:, :nn], perm, qs[:, :nn])
                                    tb = 0 if ch % 2 == 0 else 2
                                    l0 = t0 - CTX
                                    a1 = t1r.next()
                                    a2 = t2r.next()
                                    P.tt(a1[:, :nn], qs[:, :nn], ropet[:, tb, l0:l0 + nn], ALU.mult, eng="pool")
                                    P.tt(a2[:, :nn], ps2[:, :nn], ropet[:, tb + 1, l0:l0 + nn], ALU.mult)
                                    o = stgb.next()
                                    P.tt(o[:, :nn], a1[:, :nn], a2[:, :nn], ALU.add, eng="pool")
                                    P.dma(dd, o[:, :nn])
                            elif name == "mo":
                                o = stgb.next()
                                P.act(o[:, :nn], ps[:, :nn], AF.Sigmoid)
                                P.dma(dd, o[:, :nn])
                            elif name in ("nq", "nk"):
                                o = stgb.next()
                                P.act(o[:, :nn], ps[:, :nn], AF.Identity)
                                P.dma(dd, o[:, :nn])
                            elif name == "lx":
                                o = stg32.next()
                                P.act(o[:, :nn], ps[:, :nn], AF.Identity)
                                P.dma(dd, o[:, :nn])
                            else:
                                a1 = t1r.next()
                                a2 = t2r.next()
                                P.act(a1[:, :nn], ps[:, :nn], AF.Square)
                                P.ts(a1[:, :nn], a1[:, :nn], 0.044715, ALU.mult, 1.0, ALU.add, eng="pool")
                                P.tt(a2[:, :nn], a1[:, :nn], ps[:, :nn], ALU.mult)
                                P.act(a2[:, :nn], a2[:, :nn], AF.Sigmoid, scale=1.5957691216057308)
                                o = stgb.next()
                                P.tt(o[:, :nn], a2[:, :nn], ps[:, :nn], ALU.mult)
                                P.dma(dd, o[:, :nn])
            for name, dst in tm:
                for pnl in range(4):
                    pan = WS.next()
                    for i in range(0 if g == 0 else 2, T // 128):
                        ti = 0 if i < 2 else (1 if i < 6 else 2)
                        t0, nn = TILES_ALL[ti]
                        o0 = i * 128 - t0
                        ps = P.ps()
                        for kc in range(KC):
                            P.mm(ps[:, 0:256], HN[kc][ti][:, o0:o0 + 128], pan[:, kc, :],
                                 start=(kc == 0), stop=(kc == KC - 1))
                        o = stgb.next()
                        P.act(o[:, 0:256], ps[:, 0:256], AF.Identity)
                        r0 = scol(g, i * 128)
                        P.dma(dst[r0:r0 + 128, pnl * 256:(pnl + 1) * 256], o[:, 0:256])

        HNall = V(HNt[:], [b for row in HN for v in row for b in v.bufs])
        Ysb_t = [ar.raw([128, 8, T], BF16, i * 20 * KB) for i in range(3)]
        Ysb = [[P.new(Ysb_t[n][:, c, :]) for c in range(8)] for n in range(3)]
        MX_HI = 197 * KB

        def qtiles_seq(L):
            q = [(0, CTX)] if L == 0 else []
            return q + [(CTX + 512 * k, 512) for k in range(4)]

        def mlstm(L):
            top = Bump(ar, 0, MX_HI)
            NBM = top.v([64, SEQ], F32)
            NM = top.v([64, SEQ], F32)
            bT = top.v([128, 2, NKT * 4], F32)
            keep = top.top
            bm = Bump(ar, keep, MX_HI)
            GI = bm.v([64, SEQ], F32)
            GF = bm.v([64, SEQ], F32)
            Ft = bm.v([64, SEQ], F32)
            Bt = bm.v([64, SEQ], F32)
            ON = bm.v([64, SEQ], F32)
            bi = bm.v([64, 1], F32)
            bf = bm.v([64, 1], F32)
            id4 = bm.v([64, 4], F32)
            P.dma(bi, mbi_d[L])
            P.dma(bf, mbf_d[L])
            P.dma(id4, id4_d)
            P.memset(ON, 1.0)
            P.ts(bf, bf, -1.0, ALU.mult)
            for d in range(2):
                r = slice(32 * d, 32 * d + 4)
                P.dma(GI[r, :], MG[2 * d])
                P.dma(GF[r, :], MG[2 * d + 1])
            for d in range(2):
                r = slice(32 * d, 32 * d + 4)
                P.ts(GI[r], GI[r], bi[r], ALU.add)
                P.act(GF[r], GF[r], AF.Exp, bias=bf[r], scale=-1.0)
                P.act(GF[r], GF[r], AF.Ln, bias=1.0)
                if d == 0:
                    segs = [(slice(0, SEQ), False, None)]
                else:
                    segs = [(slice(0, CTX), True, None), (slice(CTX, SEQ), True, 0)]
                for (sl, rev, ini) in segs:
                    def o(v):
                        w = v[r, sl]
                        return w[:, ::-1] if rev else w
                    i0 = 0.0 if ini is None else Ft[r, ini:ini + 1]
                    P.scan(o(Ft), o(ON), o(GF), i0, ALU.mult, ALU.subtract)
                P.tt(Bt[r], GI[r], Ft[r], ALU.subtract)
                for (sl, rev, ini) in segs:
                    def o(v):
                        w = v[r, sl]
                        return w[:, ::-1] if rev else w
                    i0 = -1e30 if ini is None else GI[r, ini:ini + 1]
                    P.scan(o(GI), o(Bt), o(Bt), i0, ALU.max, ALU.max)
                P.ts(NBM[r], GI[r], -1.0, ALU.mult)
                P.tt(Ft[r], Ft[r], GI[r], ALU.add)
                P.ts(NM[r], Ft[r], -1.0, ALU.mult)
                ps = P.ps()
                for kt in range(NKT):
                    P.mm(ps[:, kt * 4:(kt + 1) * 4], Bt[r, kt * 128:(kt + 1) * 128], id4[r, :])
                P.copy(bT[:, d, :], ps[:, 0:NKT * 4])
            P.barrier()
            bm = Bump(ar, keep, MX_HI)
            masks = bm.v([128, 8, 512], F32)
            sel = bm.v([64, 4, 128], F32)
            P.dma(masks, masks_d)
            P.dma(sel, sel_d)
            KT = bm.vs(2, [128, SEQ], BF16)
            VH = bm.v([128, NKT, 256], BF16)
            QT = bm.vs(2, [128, SEQ], BF16)
            NBt = bm.vs(2, [128, 512], F32)
            EMt = bm.vs(2, [128, 512], F32)
            Dt = Rot(bm.vs(4, [128, 512], F32))
            PT = Rot(bm.vs(4, [128, 512], BF16))
            HSr = Rot([bm.vs(2, [128, 512], F32) for _ in range(2)])
            tmp = Rot(bm.vs(3, [128, 512], F32))
            dntr = Rot(bm.vs(2, [128, 512], F32))
            recr = Rot(bm.vs(2, [128, 512], F32))
            sot = Rot(bm.vs(2, [128, 512], BF16))
            yo = Rot(bm.vs(3, [128, 512], BF16))
            gn = bm.v([128, 8], F32)
            P.dma(gn, mgn_d[L])
            accsets = Rot([P.psl[0:3], P.psl[3:6]])
            sbank = Rot(P.psl[6:8])
            for h in range(4):
                for c in range(2):
                    P.dma(KT[c], MK[2 * h + c])
                    P.dma(QT[c], MQ[2 * h + c])
                P.dma(VH, MV[:, h * 256:(h + 1) * 256].re("(k p) c -> p k c", p=128))
                for (s0, nn) in qtiles_seq(L):
                    for d in range(2):
                        r = slice(32 * d, 32 * d + 4)
                        ps = sbank.next()
                        P.mm(ps[:, :nn], sel[r, h, :], NBM[r, s0:s0 + nn])
                        P.copy(NBt[d][:, :nn], ps[:, :nn], eng="act")
                        ps = sbank.next()
                        P.mm(ps[:, :nn], sel[r, h, :], NM[r, s0:s0 + nn])
                        P.act(EMt[d][:, :nn], ps[:, :nn], AF.Exp)
                    HS = HSr.next()

                    def smm(kt):
                        sb_ = sbank.next()
                        for c in range(2):
                            P.mm(sb_[:, :nn], KT[c][:, kt * 128:(kt + 1) * 128], QT[c][:, s0:s0 + nn],
                                 start=(c == 0), stop=(c == 1))
                        return sb_

                    for d in range(2):
                        sched = []
                        if s0 == 0:
                            sched = [(kt, 4 * d + kt) for kt in range(2)]
                        else:
                            ql = s0 - CTX
                            sched = [(0, None), (1, None)]
                            for j in range(16):
                                o = j - ql // 128
                                if d == 0 and o <= 3:
                                    sched.append((2 + j, o if o >= 0 else None))
                                if d == 1 and o >= 0:
                                    sched.append((2 + j, 4 + o if o <= 3 else None))
                        acc = accsets.next()
                        s_cur = smm(sched[0][0])
                        for i, (kt, mi) in enumerate(sched):
                            s_next = smm(sched[i + 1][0]) if i + 1 < len(sched) else None
                            Dv = Dt.next()
                            P.act(Dv[:, :nn], NBt[d][:, :nn], AF.Exp, bias=bT[:, d, kt * 4 + h:kt * 4 + h + 1])
                            if mi is not None:
                                P.tt(Dv[:, :nn], Dv[:, :nn], masks[:, mi, :nn], ALU.mult, eng="pool")
                            Pv = PT.next()
                            P.tt(Pv[:, :nn], s_cur[:, :nn], Dv[:, :nn], ALU.mult)
                            first, last = i == 0, i == len(sched) - 1
                            P.mm(acc[0][:, :nn], VH[:, kt, 0:128], Pv[:, :nn], start=first, stop=last)
                            P.mm(acc[1][:, :nn], VH[:, kt, 128:256], Pv[:, :nn], start=first, stop=last)
                            P.mm(acc[2][:, :nn], onesb, Pv[:, :nn], start=first, stop=last)
                            s_cur = s_next
                        dnt = dntr.next()
                        rec = recr.next()
                        P.act(dnt[:, :nn], acc[2][:, :nn], AF.Abs)
                        P.tt(dnt[:, :nn], dnt[:, :nn], EMt[d][:, :nn], ALU.max)
                        P.act(dnt[:, :nn], dnt[:, :nn], AF.Ln)
                        P.act(rec[:, :nn], dnt[:, :nn], AF.Exp, scale=-1.0)
                        for c in range(2):
                            if d == 0:
                                P.tt(HS[c][:, :nn], acc[c][:, :nn], rec[:, :nn], ALU.mult)
                            else:
                                tm_ = tmp.next()
                                P.tt(tm_[:, :nn], acc[c][:, :nn], rec[:, :nn], ALU.mult)
                                P.tt(HS[c][:, :nn], HS[c][:, :nn], tm_[:, :nn], ALU.add, eng="pool")
                    ps = sbank.next()
                    for c in range(2):
                        tm_ = tmp.next()
                        P.act(tm_[:, :nn], HS[c][:, :nn], AF.Square)
                        P.mm(ps[:, :nn], ones32, tm_[:, :nn], start=(c == 0), stop=(c == 1))
                    dnt = dntr.next()
                    rec = recr.next()
                    P.act(dnt[:, :nn], ps[:, :nn], AF.Ln, bias=epsc, scale=1.0 / 256)
                    P.act(rec[:, :nn], dnt[:, :nn], AF.Exp, scale=-0.5)
                    for c in range(2):
                        ch = 2 * h + c
                        tm_ = tmp.next()
                        so_ = sot.next()
                        P.dma(so_[:, :nn], SO[ch, :, s0:s0 + nn])
                        P.tt(tm_[:, :nn], HS[c][:, :nn], rec[:, :nn], ALU.mult)
                        y_ = yo.next()
                        P.stt(y_[:, :nn], tm_[:, :nn], gn[:, ch:ch + 1], so_[:, :nn], ALU.mult, ALU.mult)
                        P.dma(YA[0, ch, :, s0:s0 + nn], y_[:, :nn])

        KTW = {0: range(0, 6), 1: range(2, 10), 2: range(6, 14), 3: range(10, 16)}

        def na(L, jobs=()):
            jobs = list(jobs)
            SC = 128.0 ** -0.5
            bm = Bump(ar, 0, 150 * KB)
            colm = bm.v([128, 960], F32)
            Mh = bm.v([128, 960], F32)
            P.dma(colm, colm_d)
            pairs = [(qg, ktl) for qg in range(4) for ktl in KTW[qg]]
            TP = {pr: bm.v([128, 512], F32) for pr in pairs}
            MhR = Rot([Mh, bm.v([128, 960], F32)])
            NQr = Rot(bm.vs(2, [128, SEQ], BF16))
            NKr = Rot(bm.vs(2, [128, SEQ], BF16))
            NV4r = Rot(bm.vs(2, [128, NKT, 512], BF16))
            PT = Rot(bm.vs(4, [128, 512], BF16))
            lnr = Rot(bm.vs(2, [128, 512], F32))
            recr = Rot(bm.vs(2, [128, 512], F32))
            yo = Rot(bm.vs(3, [128, 512], BF16))
            for pr in pairs:
                P.memset(TP[pr], NEG)
            valid = {}
            for (qg, ktl) in pairs:
                for half in range(2):
                    j = 2 * ktl + half
                    rows = [r for r in range(8 * qg, 8 * qg + 8)
                            if min(max(r - 4, 0), 24) <= j < min(max(r - 4, 0), 24) + 8]
                    if rows:
                        valid[(qg, ktl, half)] = ((rows[0] - 8 * qg) * 64, len(rows) * 64, (rows[0] - j + 7) * 64)
            accsets = Rot([(P.psl[0], P.psl[1]), (P.psl[2], P.psl[3])])
            sbank = Rot(P.psl[4:7])
            loaded = {}

            def prep(h):
                if h >= 8:
                    return
                if h % 4 == 0:
                    nv = NV4r.next()
                    P.dma(nv, NV[:, (h // 4) * 512:(h // 4 + 1) * 512].re("(k p) c -> p k c", p=128))
                    loaded["nv"] = nv
                nq, nk, mh = NQr.next(), NKr.next(), MhR.next()
                P.dma(nq, NQ[h])
                P.dma(nk, NK[h])
                P.dma(mh[0:64, :], rpbe_d[L, h])
                P.dma(mh[64:128, :], rpbe_d[L, h])
                P.tt(mh, mh, colm, ALU.add, eng="pool")
                loaded[h] = (nq, nk, loaded["nv"], mh)

            prep(0)
            for h in range(8):
                prep(h + 1)
                NQh, NKh, NV4, mh = loaded.pop(h)
                for (s0, nn) in qtiles_seq(L):
                    keys = [(kt, None) for kt in range(2)]
                    if s0 > 0:
                        qg = (s0 - CTX) // 512
                        keys += [(2 + ktl, (qg, ktl)) for ktl in KTW[qg]]
                    pn, pd = accsets.next()
                    if jobs:
                        jobs.pop(0)(P.psl[7])

                    def smm(kt):
                        sb_ = sbank.next()
                        P.mm(sb_[:, :nn], NKh[:, kt * 128:(kt + 1) * 128], NQh[:, s0:s0 + nn])
                        return sb_

                    s_cur = smm(keys[0][0])
                    for idx, (kt, pr) in enumerate(keys):
                        s_next = smm(keys[idx + 1][0]) if idx + 1 < len(keys) else None
                        Pv = PT.next()
                        if pr is None:
                            P.act(Pv[:, :nn], s_cur[:, :nn], AF.Exp, scale=SC)
                        else:
                            for half in range(2):
                                vv = valid.get((pr[0], pr[1], half))
                                if vv is None:
                                    continue
                                a_, ln, d0 = vv
                                hp = slice(half * 64, half * 64 + 64)
                                P.stt(TP[pr][hp, a_:a_ + ln], s_cur[hp, a_:a_ + ln], SC, mh[hp, d0:d0 + ln],
                                      ALU.mult, ALU.add)
                            P.act(Pv[:, :nn], TP[pr][:, :nn], AF.Exp)
                        first, last = idx == 0, idx == len(keys) - 1
                        hh = h % 4
                        P.mm(pn[:, :nn], NV4[:, kt, hh * 128:(hh + 1) * 128], Pv[:, :nn], start=first, stop=last)
                        P.mm(pd[:, :nn], onesb, Pv[:, :nn], start=first, stop=last)
                        s_cur = s_next
                    ln_, rec = lnr.next(), recr.next()
                    P.act(ln_[:, :nn], pd[:, :nn], AF.Ln)
                    P.act(rec[:, :nn], ln_[:, :nn], AF.Exp, scale=-1.0)
                    y_ = yo.next()
                    P.tt(y_[:, :nn], pn[:, :nn], rec[:, :nn], ALU.mult)
                    P.dma(YA[1, h, :, s0:s0 + nn], y_[:, :nn])
            while jobs:
                jobs.pop(0)(P.psl[7])

        def lru(L, jobs=()):
            jobs = list(jobs)
            per = (len(jobs) + 15) // 16
            bm = Bump(ar, 0, 140 * KB)
            LXP = Rot(bm.vs(2, [128, SEQ + 6], F32))
            XCr = Rot(bm.vs(2, [128, SEQ], F32))
            XCbr = Rot(bm.vs(2, [128, SEQ], BF16))
            Rr = bm.vs(2, [128, SEQ], F32)
            Ii = bm.vs(2, [128, SEQ], F32)
            Tm = bm.vs(2, [128, SEQ], F32)
            H0 = bm.v([128, SEQ], F32)
            glg = Rot(bm.vs(2, [128, SEQ], BF16))
            yl = Rot(bm.vs(2, [128, SEQ], BF16))
            W32 = Rot(bm.vs(2, [128, 4, 128], F32))
            Wb = Rot(bm.vs(2, [128, 4, 128], BF16))
            cw = bm.v([128, 8, 4], F32)
            cbias = bm.v([128, 8], F32)
            ba = bm.v([128, 2, 8], F32)
            bx = bm.v([128, 2, 8], F32)
            lam = bm.v([128, 2, 8], F32)
            m8 = bm.v([128, 2, 8], F32)
            m16 = bm.v([128, 2, 8], F32)
            P.dma(cw, lcw_d[L])
            P.dma(cbias, lcb_d[L])
            P.dma(ba, lba_d[L])
            P.dma(bx, lbx_d[L])
            P.dma(lam, llam_d[L])
            P.act(lam, lam, AF.Exp, scale=-1.0)
            P.act(lam, lam, AF.Ln, bias=1.0)
            P.ts(m8, lam, -8.0, ALU.mult)
            P.ts(m16, lam, -16.0, ALU.mult)
            for lx_ in LXP.items:
                P.memset(lx_, 0.0)
            SEGS = [(1, CTX, 0), (CTX + 4, LAT, CTX)]
            STILES = [(0, 256), (256, 512), (768, 512), (1280, 512), (1792, 512)]
            staged = {}

            def stage_a(c):
                if c >= 8:
                    return
                lxp = LXP.next()
                P.dma(lxp[:, 1:1 + CTX], LX[c, :, 0:CTX])
                P.dma(lxp[:, CTX + 4:CTX + 4 + LAT], LX[c, :, CTX:SEQ])
                gl_ = glg.next()
                P.dma(gl_, GLG[c])
                XC, XCb = XCr.next(), XCbr.next()
                for (base, n, o) in SEGS:
                    P.ts(XC[:, o:o + n], lxp[:, base - 1:base - 1 + n], cw[:, c, 0:1], ALU.mult,
                         cbias[:, c:c + 1], ALU.add)
                    for j in range(1, 4):
                        P.stt(XC[:, o:o + n], lxp[:, base - 1 + j:base - 1 + j + n], cw[:, c, j:j + 1],
                              XC[:, o:o + n], ALU.mult, ALU.add)
                P.copy(XCb, XC, eng="act")
                w32 = W32.next()
                wb = Wb.next()
                for n in range(2):
                    P.dma(w32[:, n, :], lwa_d[L, n, c])
                    P.dma(w32[:, 2 + n, :], lwx_d[L, n, c])
                P.copy(wb, w32, eng="pool")
                staged[c] = (XC, XCb, wb, gl_)

            stage_a(0)
            for c in range(8):
                XC, XCb, wb, gl_ = staged.pop(c)
                for n in range(2):
                    for _ in range(per):
                        if jobs:
                            jobs.pop(0)(None)
                    for (s0, sn) in STILES:
                        psa = P.ps()
                        P.mm(psa[:, :sn], wb[:, n, :], XCb[:, s0:s0 + sn])
                        P.act(Rr[n][:, s0:s0 + sn], psa[:, :sn], AF.Sigmoid, bias=ba[:, n, c:c + 1])
                        psx = P.ps()
                        P.mm(psx[:, :sn], wb[:, 2 + n, :], XCb[:, s0:s0 + sn])
                        P.act(Ii[n][:, s0:s0 + sn], psx[:, :sn], AF.Sigmoid, bias=bx[:, n, c:c + 1])
                stage_a(c + 1)
                for n in range(2):
                    P.act(Tm[n], Rr[n], AF.Exp, scale=m16[:, n, c:c + 1])
                    P.act(Rr[n], Rr[n], AF.Exp, scale=m8[:, n, c:c + 1])
                for n in range(2):
                    P.ts(Tm[n], Tm[n], -1.0, ALU.mult, 1.0, ALU.add, eng="pool")
                for n in range(2):
                    P.act(Tm[n], Tm[n], AF.Sqrt)
                for n in range(2):
                    P.tt(Ii[n], Ii[n], Tm[n], ALU.mult)
                for n in range(2):
                    P.tt(Ii[n], Ii[n], XC, ALU.mult, eng="pool")
                P.scan(H0, Rr[0], Ii[0], 0.0, ALU.mult, ALU.add)
                P.scan(Tm[1][:, 0:CTX][:, ::-1], Rr[1][:, 0:CTX][:, ::-1], Ii[1][:, 0:CTX][:, ::-1], 0.0,
                       ALU.mult, ALU.add)
                P.scan(Tm[1][:, CTX:SEQ][:, ::-1], Rr[1][:, CTX:SEQ][:, ::-1], Ii[1][:, CTX:SEQ][:, ::-1],
                       Tm[1][:, 0:1], ALU.mult, ALU.add)
                P.tt(H0, H0, Tm[1], ALU.add, eng="pool")
                y_ = yl.next()
                P.tt(y_, H0, gl_, ALU.mult)
                P.dma(YA[2, c], y_)
            while jobs:
                jobs.pop(0)(None)

        def mixers(L):
            P.barrier()
            mlstm(L)
            P.barrier()
            jobs = []
            if (not mix_test) and L + 1 < n_layers:
                mod_begin(L + 1, WS2)
                jobs = [(lambda bank, j=j: mod_panel(L + 1, j, WS2, bank)) for j in range(72)]
            na(L, jobs[:36])
            P.barrier()
            lru(L, jobs[36:])
            if jobs:
                mod_finish(L + 1)
            P.barrier()

        def merge(L, g, qtiles):
            sbm = Bump(ar, S_OFF, 197 * KB)
            sgr = Rot(sbm.vs(2, [128, 512], F32))
            mo = Rot(sbm.vs(3, [128, 512], BF16))
            acc_t = sbm.raw([128, 2, T], F32)
            ACC = [[P.new(acc_t[:, o2, t0:t0 + nn]) for (t0, nn) in TILES_ALL] for o2 in range(2)]
            for ocp in range(8):
                for n in range(3):
                    pg = WS.next()
                    pb = WS.next()
                    for o2 in range(2):
                        oc = ocp * 2 + o2
                        for ti in qtiles:
                            t0, nn = TILES_ALL[ti]
                            psg = P.ps()
                            psb = P.ps()
                            for kc in range(KC):
                                P.mm(psg[:, :nn], pg[:, kc, o2 * 128:(o2 + 1) * 128], HN[kc][ti],
                                     start=(kc == 0), stop=(kc == KC - 1))
                            for kc in range(8):
                                P.mm(psb[:, :nn], pb[:, kc, o2 * 128:(o2 + 1) * 128], Ysb[n][kc][:, t0:t0 + nn],
                                     start=(kc == 0), stop=(kc == 7))
                            sg = sgr.next()
                            P.act(sg[:, :nn], psg[:, :nn], AF.Sigmoid)
                            if n == 0:
                                P.tt(ACC[o2][ti], sg[:, :nn], psb[:, :nn], ALU.mult)
                            else:
                                P.tt(sg[:, :nn], sg[:, :nn], psb[:, :nn], ALU.mult)
                                if n == 1:
                                    P.tt(ACC[o2][ti], ACC[o2][ti], sg[:, :nn], ALU.add, eng="pool")
                                else:
                                    o = mo.next()
                                    P.tt(o[:, :nn], ACC[o2][ti], sg[:, :nn], ALU.add, eng="pool")
                                    P.dma(MRG[g][oc, :, t0:t0 + nn], o[:, :nn])

        def final_out(g):
            sbm = Bump(ar, S_OFF, 197 * KB)
            sq = Rot(sbm.vs(3, [128, 512], BF16))
            og = Rot(sbm.vs(3, [128, 512], F32))
            rs = sbm.v([128, 512], F32)
            rstd = sbm.v([128, 512], F32)
            for ti in (1, 2):
                t0, nn = TILES_ALL[ti]
                ps = P.ps()
                for kc in range(KC):
                    s = sq.next()
                    P.act(s[:, :nn], X[kc][ti], AF.Square)
                    P.mm(ps[:, :nn], onesb, s[:, :nn], start=(kc == 0), stop=(kc == KC - 1))
                P.act(rs[:, :nn], ps[:, :nn], AF.Sqrt, bias=epsc, scale=1.0 / D)
                P.recip(rstd[:, :nn], rs[:, :nn])
                for kc in range(KC):
                    o = og.next()
                    P.stt(o[:, :nn], X[kc][ti], gfin[:, kc:kc + 1], rstd[:, :nn], ALU.mult, ALU.mult)
                    P.dma(outT_d[g][kc * 128:(kc + 1) * 128, t0 - CTX:t0 - CTX + nn], o[:, :nn])

        def load_X(g, tiles):
            for ti in tiles:
                t0, nn = TILES_ALL[ti]
                xv = V(Xt[:, :, t0:t0 + nn], [X[kc][ti].bufs[0] for kc in range(KC)])
                P.dma(xv, XS[g][:, :, t0:t0 + nn])

        def stageC(L, g):
            last = (L == 1)
            qtiles = [0, 1, 2] if (g == 0 and not last) else [1, 2]
            P.barrier()
            load_X(g, qtiles)
            adaln(L, 1, qtiles, Bump(ar, S_OFF, 197 * KB))
            specs = []
            for ocp in range(8):
                for n in range(3):
                    c0 = OFF["gx"] + n * D + ocp * 256
                    specs.append((w_in_d[L, :, c0:c0 + 256], KC, 256, True))
                    specs.append((w_br_d[L, n, :, ocp * 256:(ocp + 1) * 256], 8, 256, True))
            WS.begin(specs)
            P.barrier()
            for n in range(3):
                yv = V(Ysb_t[n][:], [b for v in Ysb[n] for b in v.bufs])
                P.dma(yv[:, :, CTX:T], YA[n, :, :, CTX + g * GL:CTX + (g + 1) * GL].re("c p t -> p c t"))
                if 0 in qtiles:
                    P.dma(yv[:, :, 0:CTX], YA[n, :, :, 0:CTX].re("c p t -> p c t"))
            merge(L, g, qtiles)
            WS.begin([(w_out_d[L, :, j * 256:(j + 1) * 256], KC, 256, True) for j in range(8)])
            P.barrier()
            for ti in qtiles:
                t0, nn = TILES_ALL[ti]
                load_X(g, [ti])
                hv = V(HNt[:, :, t0:t0 + nn], [HN[kc][ti].bufs[0] for kc in range(KC)])
                P.dma(hv, MRG[g][:, :, t0:t0 + nn].re("k p t -> p k t"))
            for j in range(8):
                pan = WS.next()
                for o2 in range(2):
                    oc = 2 * j + o2
                    for ti in qtiles:
                        t0, nn = TILES_ALL[ti]
                        ps = P.ps()
                        for kc in range(KC):
                            P.mm(ps[:, :nn], pan[:, kc, o2 * 128:(o2 + 1) * 128], HN[kc][ti],
                                 start=(kc == 0), stop=(kc == KC - 1))
                        P.stt(X[oc][ti], ps[:, :nn], Gmod[L][1][wq(ti)][:, oc:oc + 1], X[oc][ti],
                              ALU.mult, ALU.add)
            P.barrier()
            ffn(L, 2, qtiles, f2in_d, f2out_d)
            if last:
                P.barrier()
                final_out(g)
            else:
                P.dma(XS[g], Xall)
            P.barrier()

        if mix_test:
            mixers(0)
            dump("YA", YA)
        for L in range(0 if mix_test else n_layers):
            for g in groups:
                gt = [0, 1, 2] if g == 0 else [1, 2]
                P.barrier()
                if L == 0:
                    P.dma(Xlat, xT_d[g].re("(k p) t -> p k t", p=128))
                    if g == 0:
                        P.dma(Xctx, ctxT_d.re("(k p) t -> p k t", p=128))
                else:
                    load_X(g, gt)
                ffn(L, 0, gt, f1in_d, f1out_d)
                P.barrier()
                inproj(L, g)
                P.barrier()
            mixers(L)
            for g in groups:
                stageC(L, g)
        P.finalize(st)
    return nc


def _fm(v):
    v = np.asarray(v, np.float32)
    n = v.shape[-1] // 128
    return np.ascontiguousarray(np.swapaxes(v.reshape(v.shape[:-1] + (n, 128)), -1, -2))


def host_consts():
    c = {}
    perm = np.zeros((128, 128), np.float32)
    for m in range(128):
        perm[(m + 64) % 128, m] = 1.0
    c["perm"] = perm
    s = np.arange(128)[:, None]
    t = np.arange(512)[None, :]
    masks = np.zeros((128, 8, 512), np.float32)
    for o in range(4):
        masks[:, o, :] = (o * 128 + s <= t)
        masks[:, 4 + o, :] = (o * 128 + s >= t)
    c["masks"] = masks
    inv = 10000.0 ** (-np.arange(64, dtype=np.float64) / 64)
    invp = np.concatenate([inv, inv])
    sgn = np.concatenate([-np.ones(64), np.ones(64)])
    pos = np.arange(LAT)
    rope = np.zeros((2, 128, 4, GL), np.float32)
    for g in range(2):
        pg = pos[g * GL:(g + 1) * GL]
        for k, pp in enumerate((pg // 64, pg % 64)):
            ang = (pp[None, :].astype(np.float32) * invp[:, None].astype(np.float32)).astype(np.float64)
            rope[g, :, 2 * k, :] = np.cos(ang)
            rope[g, :, 2 * k + 1, :] = np.sin(ang) * sgn[:, None]
    c["rope"] = rope
    sel = np.zeros((64, 4, 128), np.float32)
    id4 = np.zeros((64, 4), np.float32)
    for base in (0, 32):
        for h in range(4):
            sel[base + h, h, :] = 1.0
            id4[base + h, h] = 1.0
    c["sel"] = sel
    c["id4"] = id4
    kc = np.arange(64)[:, None]
    qc = np.arange(64)[None, :]
    cs = np.clip(qc - 8, 0, 48)
    ok = (kc >= cs) & (kc < cs + 16)
    cm = np.where(ok, 0.0, NEG).astype(np.float32)
    cm = np.broadcast_to(cm[:, None, :], (64, 15, 64)).reshape(64, 15 * 64)
    c["colm"] = np.ascontiguousarray(np.concatenate([cm, cm], 0))
    return c


def prep_inputs(inp, n_cores=4):
    f = lambda k: np.asarray(inp[k], np.float32)
    shared = host_consts()
    for k in ("w_mod", "ffn1_w_in", "ffn1_w_out", "ffn2_w_in", "ffn2_w_out", "w_in", "w_branch", "w_out"):
        shared[k] = np.ascontiguousarray(f(k))
    shared["bmod"] = _fm(f("b_mod"))
    shared["gains"] = np.ascontiguousarray(np.stack([_fm(f("norm_ffn1")), _fm(f("norm_mix")), _fm(f("norm_ffn2"))], 1))
    shared["gfin"] = _fm(f("norm_final"))
    mb = np.zeros((2, 2, 64, 1), np.float32)
    for wi, k in enumerate(("mlstm_b_i", "mlstm_b_f")):
        v = f(k)
        for d in range(2):
            mb[wi, :, 32 * d:32 * d + 4, 0] = v[:, d, :]
    shared["mbi"] = mb[0]
    shared["mbf"] = mb[1]
    shared["mgn"] = _fm(f("mlstm_gn"))
    rpb = f("na_rpb")
    kc = np.arange(64)[:, None]
    qc = np.arange(64)[None, :]
    dc = np.clip(kc - qc, -15, 15) + 15
    e = rpb[:, :, ::-1, :][:, :, :, dc]
    shared["rpbe"] = np.ascontiguousarray(np.transpose(e, (0, 1, 3, 2, 4)).reshape(2, 8, 64, 15 * 64))
    shared["lcw"] = np.ascontiguousarray(np.transpose(_fm(f("lru_conv_w")), (0, 2, 3, 1)))
    shared["lcb"] = _fm(f("lru_conv_b"))
    shared["lwa"] = np.ascontiguousarray(f("lru_w_a"))
    shared["lwx"] = np.ascontiguousarray(f("lru_w_x"))
    for k, src in (("lba", "lru_b_a"), ("lbx", "lru_b_x"), ("llam", "lru_lambda")):
        shared[k] = np.ascontiguousarray(np.transpose(_fm(f(src)), (0, 2, 1, 3)))
    x, c, ctx, c_ctx = f("x"), f("c"), f("ctx"), f("c_ctx")
    maps = []
    for core in range(n_cores):
        b = core
        m = dict(shared)
        xt = np.ascontiguousarray(x[b].T)
        m["xT"] = np.ascontiguousarray(np.stack([xt[:, 0:GL], xt[:, GL:2 * GL]], 0))
        m["ctxT"] = np.ascontiguousarray(ctx[b].T)
        m["ccol"] = np.ascontiguousarray(np.stack([_fm(c[b]), _fm(c_ctx)], -1))
        maps.append(m)
    return maps


_NC_CACHE = {}


ACTIVE = (0, 1, 4, 5)


def kernel(**inputs):
    n_cores = 8
    if "nc" not in _NC_CACHE:
        _NC_CACHE["nc"] = build()
    nc = _NC_CACHE["nc"]
    real = prep_inputs(inputs, len(ACTIVE))
    zero = {k: np.zeros_like(v) for k, v in real[0].items()}
    maps = [zero] * n_cores
    for b, core in enumerate(ACTIVE):
        maps[core] = real[b]
    res = run_bass_kernel_spmd(nc, maps, core_ids=list(range(n_cores)))
    out = np.zeros((len(ACTIVE), LAT, D), np.float32)
    for b, core in enumerate(ACTIVE):
        o = np.asarray(res.results[core]["outT"], np.float32)
        out[b] = np.concatenate([o[0].T, o[1].T], 0)
    return out
```

```python
import contextlib
import numpy as np
import concourse.bass as bass
import concourse.mybir as mybir
from concourse.bass_utils import run_bass_kernel_spmd

F32 = mybir.dt.float32
BF16 = mybir.dt.bfloat16
AF = mybir.ActivationFunctionType
ALU = mybir.AluOpType

COMPUTE = ("pe", "act", "dve", "pool")
DMAQ = ("sp", "act", "pool")
NRING = 8
RING = {"sp": 8, "act": 8, "pool": 4}


class Buf:
    __slots__ = ("wc", "wd", "rc", "rd")

    def __init__(self):
        self.wc = None
        self.wd = {}
        self.rc = {}
        self.rd = {}


class V:
    __slots__ = ("ap", "bufs")

    def __init__(self, ap, bufs):
        self.ap = ap
        self.bufs = tuple(bufs)

    def __getitem__(self, idx):
        return V(self.ap[idx], self.bufs)

    def re(self, pattern, **kw):
        return V(self.ap.rearrange(pattern, **kw), self.bufs)


def _ap(x):
    return x.ap if isinstance(x, V) else x


class Op:
    __slots__ = ("eng", "fn", "deps", "sig", "sigval", "is_dma", "qidx", "id")


class Prog:
    def __init__(self, nc):
        self.nc = nc
        self.ops = []
        self.by_eng = {e: [] for e in ("pe", "act", "dve", "pool", "sp")}
        self.ndma = {e: 0 for e in DMAQ}
        self.bar = {}
        self.psl = []
        self.psi = 0

    def new(self, ap):
        return V(ap, [Buf()])

    def ps(self):
        v = self.psl[self.psi % len(self.psl)]
        self.psi += 1
        return v

    def barrier(self):
        deps = []
        for e in ("pe", "act", "dve", "pool", "sp"):
            lst = self.by_eng[e]
            seen_c = False
            nd = 0
            for op in reversed(lst):
                if op.is_dma:
                    if nd < NRING:
                        deps.append(op.id)
                        nd += 1
                elif not seen_c:
                    deps.append(op.id)
                    seen_c = True
                if seen_c and nd >= NRING:
                    break
        self.bar = {e: list(deps) for e in ("pe", "act", "dve", "pool", "sp")}

    def emit(self, eng, fn, reads=(), writes=(), dma=False):
        op = Op()
        op.eng = eng
        op.fn = fn
        op.sig = False
        op.sigval = None
        op.is_dma = dma
        op.id = len(self.ops)
        op.qidx = None
        if dma:
            op.qidx = self.ndma[eng]
            self.ndma[eng] += 1
        ops = self.ops
        ceng = {}
        dmadeps = set()

        def add(i):
            d = ops[i]
            if d.is_dma:
                dmadeps.add(i)
            else:
                if d.eng == "pe" and eng == "pe" and not dma:
                    return
                if ceng.get(d.eng, -1) < i:
                    ceng[d.eng] = i

        pend = self.bar.pop(eng, None)
        if pend:
            for i in pend:
                add(i)
        for v in reads:
            if not isinstance(v, V):
                continue
            for b in v.bufs:
                if b.wc is not None:
                    add(b.wc)
                for l in b.wd.values():
                    for i in l:
                        add(i)
        for v in writes:
            for b in v.bufs:
                had_reads = bool(b.rc) or bool(b.rd)
                for i in b.rc.values():
                    add(i)
                for l in b.rd.values():
                    for i in l:
                        add(i)
                if b.wc is not None:
                    add(b.wc)
                if (not dma) or had_reads:
                    for l in b.wd.values():
                        for i in l:
                            add(i)
        deps = list(ceng.values()) + list(dmadeps)
        for i in deps:
            ops[i].sig = True
        op.deps = deps
        for v in reads:
            if not isinstance(v, V):
                continue
            for b in v.bufs:
                if dma:
                    l = b.rd.setdefault(eng, [])
                    l.append(op.id)
                    if len(l) > RING[eng]:
                        del l[0]
                else:
                    b.rc[eng] = op.id
        for v in writes:
            for b in v.bufs:
                if dma:
                    if b.rc or b.rd:
                        b.wd = {}
                    b.wc = None
                    l = b.wd.setdefault(eng, [])
                    l.append(op.id)
                    if len(l) > RING[eng]:
                        del l[0]
                else:
                    b.wc = op.id
                    b.wd = {}
                b.rc = {}
                b.rd = {}
        ops.append(op)
        self.by_eng[eng].append(op)
        return op

    def mm(self, out, lhsT, rhs, start=True, stop=True):
        o, l, r = _ap(out), _ap(lhsT), _ap(rhs)
        rd = [lhsT, rhs] + ([] if start else [out])
        self.emit("pe", lambda e: e.matmul(o, l, r, start=start, stop=stop), rd, [out])

    def act(self, out, in_, func, bias=None, scale=None):
        o, i = _ap(out), _ap(in_)
        kw = {}
        rd = [in_]
        if bias is not None:
            kw["bias"] = _ap(bias)
            rd.append(bias)
        if scale is not None:
            kw["scale"] = _ap(scale)
            rd.append(scale)
        self.emit("act", lambda e: e.activation(o, i, func, **kw), rd, [out])

    def tt(self, out, in0, in1, op, eng="dve"):
        o, a, b = _ap(out), _ap(in0), _ap(in1)
        self.emit(eng, lambda e: e.tensor_tensor(o, a, b, op), [in0, in1], [out])

    def ts(self, out, in0, s1, op0, s2=None, op1=None, eng="dve"):
        o, a = _ap(out), _ap(in0)
        rd = [in0, s1, s2]
        a1, a2 = _ap(s1), _ap(s2)
        if op1 is None:
            self.emit(eng, lambda e: e.tensor_scalar(o, a, a1, None, op0), rd, [out])
        else:
            self.emit(eng, lambda e: e.tensor_scalar(o, a, a1, a2, op0, op1), rd, [out])

    def stt(self, out, in0, scalar, in1, op0, op1):
        o, a, s, b = _ap(out), _ap(in0), _ap(scalar), _ap(in1)
        self.emit("dve", lambda e: e.scalar_tensor_tensor(o, a, s, b, op0, op1), [in0, scalar, in1], [out])

    def scan(self, out, d0, d1, initial, op0, op1):
        o, a, b, i = _ap(out), _ap(d0), _ap(d1), _ap(initial)
        self.emit("dve", lambda e: e.tensor_tensor_scan(o, a, b, i, op0, op1), [d0, d1, initial], [out])

    def copy(self, out, in_, eng="dve"):
        o, i = _ap(out), _ap(in_)
        if eng == "act":
            self.emit("act", lambda e: e.activation(o, i, AF.Identity), [in_], [out])
        else:
            self.emit(eng, lambda e: e.tensor_copy(o, i), [in_], [out])

    def recip(self, out, in_):
        o, i = _ap(out), _ap(in_)
        self.emit("dve", lambda e: e.reciprocal(o, i), [in_], [out])

    def memset(self, out, val, eng="pool"):
        o = _ap(out)
        self.emit(eng, lambda e: e.memset(o, val), [], [out])

    def dma(self, out, in_, q="pool", **kw):
        o, i = _ap(out), _ap(in_)
        self.emit(q, lambda e: e.dma_start(o, i, **kw), [in_], [out], dma=True)

    def finalize(self, stack):
        nc = self.nc
        sems = {e: stack.enter_context(nc.semaphore("s_" + e)) for e in COMPUTE}
        rings = {q: [stack.enter_context(nc.semaphore("r_%s%d" % (q, k))) for k in range(RING[q])]
                 for q in DMAQ if self.ndma[q] > 0}
        cnt = {e: 0 for e in COMPUTE}
        for op in self.ops:
            if not op.is_dma and op.sig:
                cnt[op.eng] += 1
                op.sigval = cnt[op.eng]
        ops = self.ops
        by_eng = self.by_eng
        ndma = self.ndma

        def run(eng_name, e):
            waited = {}

            def wait(s, v):
                k = id(s)
                if waited.get(k, 0) < v:
                    e.wait_ge(s, v)
                    waited[k] = v

            for op in by_eng[eng_name]:
                if op.is_dma and op.qidx >= RING[eng_name]:
                    nr = RING[eng_name]
                    wait(rings[eng_name][op.qidx % nr], 16 * (op.qidx // nr))
                for d in op.deps:
                    dop = ops[d]
                    if dop.is_dma:
                        nr = RING[dop.eng]
                        wait(rings[dop.eng][dop.qidx % nr], 16 * (dop.qidx // nr + 1))
                    else:
                        wait(sems[dop.eng], dop.sigval)
                inst = op.fn(e)
                if op.is_dma:
                    inst.then_inc(rings[eng_name][op.qidx % RING[eng_name]], 16)
                elif op.sig:
                    inst.then_inc(sems[eng_name], 1)
            if eng_name in rings:
                n = ndma[eng_name]
                nr = RING[eng_name]
                for k in range(min(n, nr)):
                    uses = (n - 1 - k) // nr + 1
                    wait(rings[eng_name][k], 16 * uses)

        with nc.Block() as block:
            @block.tensor
            def _(e):
                run("pe", e)

            @block.scalar
            def _(e):
                run("act", e)

            @block.vector
            def _(e):
                run("dve", e)

            @block.gpsimd
            def _(e):
                run("pool", e)

            @block.sync
            def _(e):
                run("sp", e)


D = 2048
KC = 16
DFF = 5632
PIN = 15376
BW = 1024
LAT = 2048
CTX = 256
GL = 1024
T = CTX + GL
SEQ = CTX + LAT
NKT = SEQ // 128
EPS = 1e-6
TILES_ALL = [(0, 256), (256, 512), (768, 512)]
OFF = dict(mq=0, mk=1024, mv=2048, mo=3072, mg=4096, nq=4112, nk=5136, nv=6160, lx=7184, lg=8208, gx=9232)
NEG = -30000.0
KB = 1024


class Arena:
    def __init__(self, nc, P, stack, nbytes):
        self.nc = nc
        self.P = P
        self.base = (nc.sbuf_base + 31) // 32 * 32
        stack.enter_context(nc.sbuf_tensor("arena", [128, nbytes], mybir.dt.uint8))
        assert nc.sbuf_base == self.base + nbytes, (nc.sbuf_base, self.base, nbytes)
        self.nbytes = nbytes
        self.n = 0

    def raw(self, shape, dtype, off):
        esz = 4 if dtype == F32 else 2
        sz = esz * int(np.prod(shape[1:]))
        assert off % 4 == 0 and off + sz <= self.nbytes, (shape, off, sz, self.nbytes)
        self.n += 1
        return self.nc.alloc_sbuf_tensor_at("a%d" % self.n, list(shape), dtype, offset=self.base + off)

    def v(self, shape, dtype, off):
        return self.P.new(self.raw(shape, dtype, off)[:])


class Bump:
    def __init__(self, ar, lo, hi):
        self.ar, self.lo, self.hi, self.top = ar, lo, hi, lo

    def raw(self, shape, dtype):
        esz = 4 if dtype == F32 else 2
        sz = (esz * int(np.prod(shape[1:])) + 31) // 32 * 32
        assert self.top + sz <= self.hi, ("bump overflow", shape, self.top, sz, self.hi)
        t = self.ar.raw(shape, dtype, self.top)
        self.top += sz
        return t

    def v(self, shape, dtype):
        return self.ar.P.new(self.raw(shape, dtype)[:])

    def vs(self, n, shape, dtype):
        return [self.v(shape, dtype) for _ in range(n)]


class Rot:
    def __init__(self, items):
        self.items, self.i = items, 0

    def next(self):
        v = self.items[self.i % len(self.items)]
        self.i += 1
        return v


class WStream:
    def __init__(self, P, stages, bfs):
        self.P = P
        self.stages = Rot(stages)
        self.bfs = Rot(bfs)
        self.specs = []
        self.i = 0
        self.loaded = {}
        self.ci = 0

    def begin(self, specs):
        self.specs = list(specs)
        self.i = 0
        self.loaded = {}
        self._load(0)

    def _load(self, j):
        if j >= len(self.specs) or j in self.loaded:
            return
        src, nk, nc_, cast = self.specs[j]
        P = self.P
        st = self.stages.next()
        stv = V(st.ap[:, 0:nk * nc_].rearrange("p (k c) -> p k c", k=nk), st.bufs)
        P.dma(stv, src.re("(k p) c -> p k c", p=128), q="sp")
        if not cast:
            self.loaded[j] = stv
            return
        bf = self.bfs.next()
        bfv = V(bf.ap[:, 0:nk * nc_].rearrange("p (k c) -> p k c", k=nk), bf.bufs)
        P.copy(bfv, stv, eng="act")
        self.loaded[j] = bfv

    def next(self):
        j = self.i
        self.i += 1
        self._load(j)
        self._load(j + 1)
        return self.loaded.pop(j)


def build(n_layers=2, groups=(0, 1), stop=None, dbg=(), mix_test=False):
    nc = bass.Bass("TRN2", target_bir_lowering=False)
    P = Prog(nc)
    NG = 2

    BIGW = ("w_mod", "ffn1_w_in", "ffn1_w_out", "ffn2_w_in", "ffn2_w_out", "w_in", "w_branch", "w_out", "xT", "ctxT")

    def din(name, shape, dt=F32):
        if mix_test and name in BIGW:
            return None
        return V(nc.dram_tensor(name, list(shape), dt, kind="ExternalInput").ap(), [Buf()])

    def dscr(name, shape, dt):
        return nc.dram_tensor(name, list(shape), dt, kind="Internal").ap()

    xT_d = din("xT", [NG, D, GL])
    ctxT_d = din("ctxT", [D, CTX])
    ccol_d = din("ccol", [128, KC, 2])
    w_mod_d = din("w_mod", [2, D, 9 * D])
    bmod_d = din("bmod", [2, 128, 144])
    gains_d = din("gains", [2, 3, 128, KC])
    gfin_d = din("gfin", [128, KC])
    f1in_d = din("ffn1_w_in", [2, D, 2 * DFF])
    f1out_d = din("ffn1_w_out", [2, DFF, D])
    f2in_d = din("ffn2_w_in", [2, D, 2 * DFF])
    f2out_d = din("ffn2_w_out", [2, DFF, D])
    w_in_d = din("w_in", [2, D, PIN])
    w_br_d = din("w_branch", [2, 3, BW, D])
    w_out_d = din("w_out", [2, D, D])
    mbi_d = din("mbi", [2, 64, 1])
    mbf_d = din("mbf", [2, 64, 1])
    mgn_d = din("mgn", [2, 128, 8])
    rpbe_d = din("rpbe", [2, 8, 64, 15 * 64])
    colm_d = din("colm", [128, 15 * 64])
    lcw_d = din("lcw", [2, 128, 8, 4])
    lcb_d = din("lcb", [2, 128, 8])
    lwa_d = din("lwa", [2, 2, 8, 128, 128])
    lwx_d = din("lwx", [2, 2, 8, 128, 128])
    lba_d = din("lba", [2, 128, 2, 8])
    lbx_d = din("lbx", [2, 128, 2, 8])
    llam_d = din("llam", [2, 128, 2, 8])
    perm_d = din("perm", [128, 128])
    masks_d = din("masks", [128, 8, 512])
    rope_d = din("rope", [NG, 128, 4, GL])
    sel_d = din("sel", [64, 4, 128])
    id4_d = din("id4", [64, 4])
    outT = nc.dram_tensor("outT", [NG, D, GL], F32, kind="ExternalOutput").ap()
    outT_d = [V(outT[g], [Buf()]) for g in range(NG)]
    dbg_d = {}
    for name, shape, dt in dbg:
        dbg_d[name] = V(nc.dram_tensor("dbg_" + name, list(shape), dt, kind="ExternalOutput").ap(), [Buf()])

    def scr1(name, shape, dt, ext_in=False):
        if mix_test and ext_in:
            return V(nc.dram_tensor(name, list(shape), dt, kind="ExternalInput").ap(), [Buf()])
        return V(dscr(name, shape, dt), [Buf()])

    XS = [scr1("XS%d" % g, [128, KC, T], F32) for g in range(NG)]
    MRG = [scr1("MRG%d" % g, [KC, 128, T], BF16) for g in range(NG)]
    HS = [scr1("HS%d" % g, [128, KC, T], BF16) for g in range(NG)]
    MQ = scr1("MQ", [8, 128, SEQ], BF16, True)
    SO = scr1("SO", [8, 128, SEQ], BF16, True)
    NQ = scr1("NQ", [8, 128, SEQ], BF16, True)
    GLG = scr1("GLG", [8, 128, SEQ], BF16, True)
    MK = scr1("MK", [8, 128, SEQ], BF16, True)
    NK = scr1("NK", [8, 128, SEQ], BF16, True)
    MV = scr1("MV", [SEQ, BW], BF16, True)
    NV = scr1("NV", [SEQ, BW], BF16, True)
    MG = scr1("MG", [4, 4, SEQ], F32, True)
    LX = scr1("LX", [8, 128, SEQ], F32, True)
    YA = scr1("YA", [3, 8, 128, SEQ], BF16)

    def scol(g, t0):
        return t0 if t0 < CTX else CTX + g * GL + (t0 - CTX)

    with contextlib.ExitStack() as st:
        P.psl = [P.new(st.enter_context(nc.psum_tensor("ps%d" % i, [128, 512], F32))[:]) for i in range(8)]
        ARENA_BYTES = 206 * KB
        ar = Arena(nc, P, st, ARENA_BYTES)
        cb = Bump(ar, 197 * KB, ARENA_BYTES)
        ones32 = cb.v([128, 128], F32)
        onesb = cb.v([128, 128], BF16)
        cact = cb.v([128, KC, 2], F32)
        MOD = [cb.v([128, 144, 2], F32) for _ in range(2)]
        Amod = [[[cb.v([128, KC], F32) for w in range(2)] for n in range(3)] for L in range(2)]
        Gmod = [[[cb.v([128, KC], F32) for w in range(2)] for n in range(3)] for L in range(2)]
        gains = cb.v([128, 2, 3, KC], F32)
        gfin = cb.v([128, KC], F32)
        bmod = cb.v([128, 2, 144], F32)
        epsc = cb.v([128, 1], F32)
        P.memset(ones32, 1.0)
        P.memset(onesb, 1.0)
        P.memset(epsc, EPS)
        P.dma(gains, gains_d.re("l n p k -> p l n k"))
        P.dma(gfin, gfin_d)
        P.dma(bmod, bmod_d.re("l p c -> p l c"))

        X_OFF, HN_OFF, W_OFF = 0, 80 * KB, 120 * KB
        Xt = ar.raw([128, KC, T], F32, X_OFF)
        HNt = ar.raw([128, KC, T], BF16, HN_OFF)
        X = [[P.new(Xt[:, kc, t0:t0 + n]) for (t0, n) in TILES_ALL] for kc in range(KC)]
        HN = [[P.new(HNt[:, kc, t0:t0 + n]) for (t0, n) in TILES_ALL] for kc in range(KC)]
        Xall = V(Xt[:], [b for row in X for v in row for b in v.bufs])
        Xlat = V(Xt[:, :, CTX:T], [b for row in X for v in row[1:] for b in v.bufs])
        Xctx = V(Xt[:, :, 0:CTX], [row[0].bufs[0] for row in X])
        wst = [ar.v([128, 4096], F32, W_OFF + i * 16 * KB) for i in range(2)]
        wbf = [ar.v([128, 4096], BF16, W_OFF + 32 * KB + i * 8 * KB) for i in range(3)]
        WS = WStream(P, wst, wbf)
        S_OFF = 176 * KB

        def wq(tile_i):
            return 1 if tile_i == 0 else 0

        def dump(name, src):
            if name in dbg_d:
                P.dma(dbg_d[name], src)

        sb0 = Bump(ar, S_OFF, 197 * KB)
        cc = sb0.v([128, KC, 2], F32)
        P.dma(cc, ccol_d)
        P.act(cact, cc, AF.Silu)
        idc = cb.v([64, 4], F32)
        P.dma(idc, id4_d)
        rwr = Rot(cb.vs(2, [2, 256], F32))

        wst2 = [ar.v([128, 4096], F32, 160 * KB + i * 16 * KB) for i in range(2)]
        WS2 = WStream(P, wst2, wbf)

        def mod_begin(L, ws=None):
            (ws or WS).begin([(w_mod_d[L, :, j * 256:(j + 1) * 256], KC, 256, False) for j in range(72)])

        def mod_panel(L, j, ws=None, bank=None):
            pan = (ws or WS).next()
            ps = bank or P.ps()
            for kc in range(KC):
                P.mm(ps[0:2, 0:256], cact[:, kc, :], pan[:, kc, :], start=(kc == 0), stop=(kc == KC - 1))
            rw = rwr.next()
            P.copy(rw, ps[0:2, 0:256], eng="act")
            for o2 in range(2):
                ps2 = bank or P.ps()
                P.mm(ps2[:, 0:2], rw[0:2, o2 * 128:(o2 + 1) * 128], idc[0:2, 0:2])
                ch = j * 2 + o2
                P.ts(MOD[L][:, ch, :], ps2[:, 0:2], bmod[:, L, ch:ch + 1], ALU.add)

        def mod_finish(L):
            for n in range(3):
                for w in range(2):
                    sc = MOD[L][:, (3 * n + 1) * KC:(3 * n + 2) * KC, w]
                    gt = MOD[L][:, (3 * n + 2) * KC:(3 * n + 3) * KC, w]
                    P.stt(Amod[L][n][w], sc, 1.0, gains[:, L, n, :], ALU.add, ALU.mult)
                    P.ts(Gmod[L][n][w], gt, 1.0 if n == 1 else 0.5, ALU.mult)

        if not mix_test:
            mod_begin(0)
            for j in range(72):
                mod_panel(0, j)
            mod_finish(0)
        dump("mod0", MOD[0])

        def shift_ap(L, n, w, kc):
            return MOD[L][:, 3 * n * KC + kc, w:w + 1]

        def adaln(L, n, tiles, sbm, pre=None):
            if pre is None:
                sq = Rot(sbm.vs(3, [128, 512], BF16))
                tmp = Rot(sbm.vs(2, [128, 512], F32))
                rs = sbm.v([128, 512], F32)
                rstd = sbm.v([128, 512], F32)
            else:
                sq, tmp, rs, rstd = pre
            for ti in tiles:
                t0, nn = TILES_ALL[ti]
                w = wq(ti)
                ps = P.ps()
                for kc in range(KC):
                    s = sq.next()
                    P.act(s[:, :nn], X[kc][ti], AF.Square)
                    P.mm(ps[:, :nn], onesb, s[:, :nn], start=(kc == 0), stop=(kc == KC - 1))
                P.act(rs[:, :nn], ps[:, :nn], AF.Ln, bias=epsc, scale=1.0 / D)
                P.act(rstd[:, :nn], rs[:, :nn], AF.Exp, scale=-0.5)
                for kc in range(KC):
                    tm = tmp.next()
                    P.tt(tm[:, :nn], X[kc][ti], rstd[:, :nn], ALU.mult)
                    P.act(HN[kc][ti], tm[:, :nn], AF.Identity,
                          bias=shift_ap(L, n, w, kc), scale=Amod[L][n][w][:, kc:kc + 1])

        def ffn(L, n, tiles, win_d, wout_d):
            specs = []
            for grp in range(11):
                for half in range(2):
                    c0 = (grp * 4 + half * 2) * 128
                    specs.append((win_d[L, :, c0:c0 + 256], KC, 256, True))
                    specs.append((win_d[L, :, DFF + c0:DFF + c0 + 256], KC, 256, True))
                for ob in range(4):
                    specs.append((wout_d[L, grp * 512:(grp + 1) * 512, ob * 512:(ob + 1) * 512], 4, 512, True))
            WS.begin(specs)
            sbm = Bump(ar, S_OFF, 197 * KB)
            sq = Rot(sbm.vs(3, [128, 512], BF16))
            rs = sbm.v([128, 512], F32)
            rstd = sbm.v([128, 512], F32)
            sil = Rot(sbm.vs(2, [128, 512], F32))
            actg_t = sbm.raw([128, 4, T], BF16)
            ACTG = [[P.new(actg_t[:, j, t0:t0 + nn]) for (t0, nn) in TILES_ALL] for j in range(4)]
            adaln(L, n, tiles, None, pre=(sq, sil, rs, rstd))
            for grp in range(11):
                for half in range(2):
                    pg = WS.next()
                    pu = WS.next()
                    for jj in range(2):
                        jl = half * 2 + jj
                        for ti in tiles:
                            t0, nn = TILES_ALL[ti]
                            psg = P.ps()
                            psu = P.ps()
                            for kc in range(KC):
                                P.mm(psg[:, :nn], pg[:, kc, jj * 128:(jj + 1) * 128], HN[kc][ti],
                                     start=(kc == 0), stop=(kc == KC - 1))
                            for kc in range(KC):
                                P.mm(psu[:, :nn], pu[:, kc, jj * 128:(jj + 1) * 128], HN[kc][ti],
                                     start=(kc == 0), stop=(kc == KC - 1))
                            s = sil.next()
                            P.act(s[:, :nn], psg[:, :nn], AF.Silu)
                            P.tt(ACTG[jl][ti], s[:, :nn], psu[:, :nn], ALU.mult)
                for ob in range(4):
                    pw = WS.next()
                    for o4 in range(4):
                        oc = ob * 4 + o4
                        for ti in tiles:
                            t0, nn = TILES_ALL[ti]
                            ps = P.ps()
                            for j in range(4):
                                P.mm(ps[:, :nn], pw[:, j, o4 * 128:(o4 + 1) * 128], ACTG[j][ti],
                                     start=(j == 0), stop=(j == 3))
                            P.stt(X[oc][ti], ps[:, :nn], Gmod[L][n][wq(ti)][:, oc:oc + 1], X[oc][ti],
                                  ALU.mult, ALU.add)

        def inproj(L, g):
            gt = [0, 1, 2] if g == 0 else [1, 2]
            sbm = Bump(ar, S_OFF, 197 * KB)
            adaln(L, 1, gt, sbm)
            P.dma(XS[g], Xall)
            P.dma(HS[g], V(HNt[:], [b for row in HN for v in row for b in v.bufs]))
            fm = [("mq", MQ), ("mk", MK), ("mo", SO), ("nq", NQ), ("nk", NK), ("lx", LX), ("lg", GLG)]
            tm = [("mv", MV), ("nv", NV)]
            specs = []
            for name, _ in fm + tm:
                for pnl in range(4):
                    c0 = OFF[name] + pnl * 256
                    specs.append((w_in_d[L, :, c0:c0 + 256], KC, 256, True))
            WS.begin(specs)
            P.barrier()
            sbm = Bump(ar, X_OFF, 80 * KB)
            ropet = sbm.v([128, 4, GL], F32)
            P.dma(ropet, rope_d[g])
            perm = sbm.v([128, 128], F32)
            P.dma(perm, perm_d)
            stg32 = Rot(sbm.vs(2, [128, 512], F32))
            stgb = Rot(sbm.vs(3, [128, 512], BF16))
            t1r = Rot(sbm.vs(2, [128, 512], F32))
            t2r = Rot(sbm.vs(2, [128, 512], F32))
            wg32 = sbm.v([128, KC, 16], F32)
            wgb = sbm.v([128, KC, 16], BF16)
            g4 = Rot(sbm.vs(2, [4, 512], F32))
            P.dma(wg32, w_in_d[L, :, OFF["mg"]:OFF["mg"] + 16].re("(k p) c -> p k c", p=128))
            P.copy(wgb, wg32, eng="dve")
            for dg in range(4):
                for ti in gt:
                    t0, nn = TILES_ALL[ti]
                    ps = P.ps()
                    for kc in range(KC):
                        P.mm(ps[0:4, :nn], wgb[:, kc, dg * 4:(dg + 1) * 4], HN[kc][ti],
                             start=(kc == 0), stop=(kc == KC - 1))
                    s = g4.next()
                    P.copy(s[:, :nn], ps[0:4, :nn], eng="dve")
                    P.dma(MG[dg, :, scol(g, t0):scol(g, t0) + nn], s[:, :nn])
            for name, dst in fm:
                for pnl in range(4):
                    pan = WS.next()
                    for o2 in range(2):
                        ch = pnl * 2 + o2
                        for ti in gt:
                            t0, nn = TILES_ALL[ti]
                            ps = P.ps()
                            for kc in range(KC):
                                P.mm(ps[:, :nn], pan[:, kc, o2 * 128:(o2 + 1) * 128], HN[kc][ti],
                                     start=(kc == 0), stop=(kc == KC - 1))
                            dd = dst[ch, :, scol(g, t0):scol(g, t0) + nn]
                            if name in ("mq", "mk"):
                                sc = 1.0 if name == "mq" else 1.0 / 16.0
                                if ti == 0:
                                    o = stgb.next()
                                    P.act(o[:, :nn], ps[:, :nn], AF.Identity, scale=sc)
                                    P.dma(dd, o[:, :nn])
                                else:
                                    qs = stg32.next()
                                    P.act(qs[:, :nn], ps[:, :nn], AF.Identity, scale=sc)
                                    ps2 = P.ps()
                                    P.mm(ps2[:, :nn], perm, qs[:, :nn])
                                    tb = 0 if ch % 2 == 0 else 2
                                    l0 = t0 - CTX
                                    a1 = t1r.next()
                                    a2 = t2r.next()
                                    P.tt(a1[:, :nn], qs[:, :nn], ropet[:, tb, l0:l0 + nn], ALU.mult, eng="pool")
                                    P.tt(a2[:, :nn], ps2[:, :nn], ropet[:, tb + 1, l0:l0 + nn], ALU.mult)
                                    o = stgb.next()
                                    P.tt(o[:, :nn], a1[:, :nn], a2[:, :nn], ALU.add, eng="pool")
                                    P.dma(dd, o[:, :nn])
                            elif name == "mo":
                                o = stgb.next()
                                P.act(o[:, :nn], ps[:, :nn], AF.Sigmoid)
                                P.dma(dd, o[:, :nn])
                            elif name in ("nq", "nk"):
                                o = stgb.next()
                                P.act(o[:, :nn], ps[:, :nn], AF.Identity)
                                P.dma(dd, o[:, :nn])
                            elif name == "lx":
                                o = stg32.next()
                                P.act(o[:, :nn], ps[:, :nn], AF.Identity)
                                P.dma(dd, o[:, :nn])
                            else:
                                a1 = t1r.next()
                                a2 = t2r.next()
                                P.act(a1[:, :nn], ps[:, :nn], AF.Square)
                                P.ts(a1[:, :nn], a1[:, :nn], 0.044715, ALU.mult, 1.0, ALU.add, eng="pool")
                                P.tt(a2[:, :nn], a1[:, :nn], ps[:, :nn], ALU.mult)
                                P.act(a2[:, :nn], a2[:, :nn], AF.Sigmoid, scale=1.5957691216057308)
                                o = stgb.next()
                                P.tt(o[:, :nn], a2[:, :nn], ps[:, :nn], ALU.mult)
                                P.dma(dd, o[:, :nn])
            for name, dst in tm:
                for pnl in range(4):
                    pan = WS.next()
                    for i in range(0 if g == 0 else 2, T // 128):
                        ti = 0 if i < 2 else (1 if i < 6 else 2)
                        t0, nn = TILES_ALL[ti]
                        o0 = i * 128 - t0
                        ps = P.ps()
                        for kc in range(KC):
                            P.mm(ps[:, 0:256], HN[kc][ti][:, o0:o0 + 128], pan[:, kc, :],
                                 start=(kc == 0), stop=(kc == KC - 1))
                        o = stgb.next()
                        P.act(o[:, 0:256], ps[:, 0:256], AF.Identity)
                        r0 = scol(g, i * 128)
                        P.dma(dst[r0:r0 + 128, pnl * 256:(pnl + 1) * 256], o[:, 0:256])

        HNall = V(HNt[:], [b for row in HN for v in row for b in v.bufs])
        Ysb_t = [ar.raw([128, 8, T], BF16, i * 20 * KB) for i in range(3)]
        Ysb = [[P.new(Ysb_t[n][:, c, :]) for c in range(8)] for n in range(3)]
        MX_HI = 197 * KB

        def qtiles_seq(L):
            q = [(0, CTX)] if L == 0 else []
            return q + [(CTX + 512 * k, 512) for k in range(4)]

        def mlstm(L):
            top = Bump(ar, 0, MX_HI)
            NBM = top.v([64, SEQ], F32)
            NM = top.v([64, SEQ], F32)
            bT = top.v([128, 2, NKT * 4], F32)
            keep = top.top
            bm = Bump(ar, keep, MX_HI)
            GI = bm.v([64, SEQ], F32)
            GF = bm.v([64, SEQ], F32)
            Ft = bm.v([64, SEQ], F32)
            Bt = bm.v([64, SEQ], F32)
            ON = bm.v([64, SEQ], F32)
            bi = bm.v([64, 1], F32)
            bf = bm.v([64, 1], F32)
            id4 = bm.v([64, 4], F32)
            P.dma(bi, mbi_d[L])
            P.dma(bf, mbf_d[L])
            P.dma(id4, id4_d)
            P.memset(ON, 1.0)
            P.ts(bf, bf, -1.0, ALU.mult)
            for d in range(2):
                r = slice(32 * d, 32 * d + 4)
                P.dma(GI[r, :], MG[2 * d])
                P.dma(GF[r, :], MG[2 * d + 1])
            for d in range(2):
                r = slice(32 * d, 32 * d + 4)
                P.ts(GI[r], GI[r], bi[r], ALU.add)
                P.act(GF[r], GF[r], AF.Exp, bias=bf[r], scale=-1.0)
                P.act(GF[r], GF[r], AF.Ln, bias=1.0)
                if d == 0:
                    segs = [(slice(0, SEQ), False, None)]
                else:
                    segs = [(slice(0, CTX), True, None), (slice(CTX, SEQ), True, 0)]
                for (sl, rev, ini) in segs:
                    def o(v):
                        w = v[r, sl]
                        return w[:, ::-1] if rev else w
                    i0 = 0.0 if ini is None else Ft[r, ini:ini + 1]
                    P.scan(o(Ft), o(ON), o(GF), i0, ALU.mult, ALU.subtract)
                P.tt(Bt[r], GI[r], Ft[r], ALU.subtract)
                for (sl, rev, ini) in segs:
                    def o(v):
                        w = v[r, sl]
                        return w[:, ::-1] if rev else w
                    i0 = -1e30 if ini is None else GI[r, ini:ini + 1]
                    P.scan(o(GI), o(Bt), o(Bt), i0, ALU.max, ALU.max)
                P.ts(NBM[r], GI[r], -1.0, ALU.mult)
                P.tt(Ft[r], Ft[r], GI[r], ALU.add)
                P.ts(NM[r], Ft[r], -1.0, ALU.mult)
                ps = P.ps()
                for kt in range(NKT):
                    P.mm(ps[:, kt * 4:(kt + 1) * 4], Bt[r, kt * 128:(kt + 1) * 128], id4[r, :])
                P.copy(bT[:, d, :], ps[:, 0:NKT * 4])
            P.barrier()
            bm = Bump(ar, keep, MX_HI)
            masks = bm.v([128, 8, 512], F32)
            sel = bm.v([64, 4, 128], F32)
            P.dma(masks, masks_d)
            P.dma(sel, sel_d)
            KT = bm.vs(2, [128, SEQ], BF16)
            VH = bm.v([128, NKT, 256], BF16)
            QT = bm.vs(2, [128, SEQ], BF16)
            NBt = bm.vs(2, [128, 512], F32)
            EMt = bm.vs(2, [128, 512], F32)
            Dt = Rot(bm.vs(4, [128, 512], F32))
            PT = Rot(bm.vs(4, [128, 512], BF16))
            HSr = Rot([bm.vs(2, [128, 512], F32) for _ in range(2)])
            tmp = Rot(bm.vs(3, [128, 512], F32))
            dntr = Rot(bm.vs(2, [128, 512], F32))
            recr = Rot(bm.vs(2, [128, 512], F32))
            sot = Rot(bm.vs(2, [128, 512], BF16))
            yo = Rot(bm.vs(3, [128, 512], BF16))
            gn = bm.v([128, 8], F32)
            P.dma(gn, mgn_d[L])
            accsets = Rot([P.psl[0:3], P.psl[3:6]])
            sbank = Rot(P.psl[6:8])
            for h in range(4):
                for c in range(2):
                    P.dma(KT[c], MK[2 * h + c])
                    P.dma(QT[c], MQ[2 * h + c])
                P.dma(VH, MV[:, h * 256:(h + 1) * 256].re("(k p) c -> p k c", p=128))
                for (s0, nn) in qtiles_seq(L):
                    for d in range(2):
                        r = slice(32 * d, 32 * d + 4)
                        ps = sbank.next()
                        P.mm(ps[:, :nn], sel[r, h, :], NBM[r, s0:s0 + nn])
                        P.copy(NBt[d][:, :nn], ps[:, :nn], eng="act")
                        ps = sbank.next()
                        P.mm(ps[:, :nn], sel[r, h, :], NM[r, s0:s0 + nn])
                        P.act(EMt[d][:, :nn], ps[:, :nn], AF.Exp)
                    HS = HSr.next()

                    def smm(kt):
                        sb_ = sbank.next()
                        for c in range(2):
                            P.mm(sb_[:, :nn], KT[c][:, kt * 128:(kt + 1) * 128], QT[c][:, s0:s0 + nn],
                                 start=(c == 0), stop=(c == 1))
                        return sb_

                    for d in range(2):
                        sched = []
                        if s0 == 0:
                            sched = [(kt, 4 * d + kt) for kt in range(2)]
                        else:
                            ql = s0 - CTX
                            sched = [(0, None), (1, None)]
                            for j in range(16):
                                o = j - ql // 128
                                if d == 0 and o <= 3:
                                    sched.append((2 + j, o if o >= 0 else None))
                                if d == 1 and o >= 0:
                                    sched.append((2 + j, 4 + o if o <= 3 else None))
                        acc = accsets.next()
                        s_cur = smm(sched[0][0])
                        for i, (kt, mi) in enumerate(sched):
                            s_next = smm(sched[i + 1][0]) if i + 1 < len(sched) else None
                            Dv = Dt.next()
                            P.act(Dv[:, :nn], NBt[d][:, :nn], AF.Exp, bias=bT[:, d, kt * 4 + h:kt * 4 + h + 1])
                            if mi is not None:
                                P.tt(Dv[:, :nn], Dv[:, :nn], masks[:, mi, :nn], ALU.mult, eng="pool")
                            Pv = PT.next()
                            P.tt(Pv[:, :nn], s_cur[:, :nn], Dv[:, :nn], ALU.mult)
                            first, last = i == 0, i == len(sched) - 1
                            P.mm(acc[0][:, :nn], VH[:, kt, 0:128], Pv[:, :nn], start=first, stop=last)
                            P.mm(acc[1][:, :nn], VH[:, kt, 128:256], Pv[:, :nn], start=first, stop=last)
                            P.mm(acc[2][:, :nn], onesb, Pv[:, :nn], start=first, stop=last)
                            s_cur = s_next
                        dnt = dntr.next()
                        rec = recr.next()
                        P.act(dnt[:, :nn], acc[2][:, :nn], AF.Abs)
                        P.tt(dnt[:, :nn], dnt[:, :nn], EMt[d][:, :nn], ALU.max)
                        P.act(dnt[:, :nn], dnt[:, :nn], AF.Ln)
                        P.act(rec[:, :nn], dnt[:, :nn], AF.Exp, scale=-1.0)
                        for c in range(2):
                            if d == 0:
                                P.tt(HS[c][:, :nn], acc[c][:, :nn], rec[:, :nn], ALU.mult)
                            else:
                                tm_ = tmp.next()
                                P.tt(tm_[:, :nn], acc[c][:, :nn], rec[:, :nn], ALU.mult)
                                P.tt(HS[c][:, :nn], HS[c][:, :nn], tm_[:, :nn], ALU.add, eng="pool")
                    ps = sbank.next()
                    for c in range(2):
                        tm_ = tmp.next()
                        P.act(tm_[:, :nn], HS[c][:, :nn], AF.Square)
                        P.mm(ps[:, :nn], ones32, tm_[:, :nn], start=(c == 0), stop=(c == 1))
                    dnt = dntr.next()
                    rec = recr.next()
                    P.act(dnt[:, :nn], ps[:, :nn], AF.Ln, bias=epsc, scale=1.0 / 256)
                    P.act(rec[:, :nn], dnt[:, :nn], AF.Exp, scale=-0.5)
                    for c in range(2):
                        ch = 2 * h + c
                        tm_ = tmp.next()
                        so_ = sot.next()
                        P.dma(so_[:, :nn], SO[ch, :, s0:s0 + nn])
                        P.tt(tm_[:, :nn], HS[c][:, :nn], rec[:, :nn], ALU.mult)
                        y_ = yo.next()
                        P.stt(y_[:, :nn], tm_[:, :nn], gn[:, ch:ch + 1], so_[:, :nn], ALU.mult, ALU.mult)
                        P.dma(YA[0, ch, :, s0:s0 + nn], y_[:, :nn])

        KTW = {0: range(0, 6), 1: range(2, 10), 2: range(6, 14), 3: range(10, 16)}

        def na(L, jobs=()):
            jobs = list(jobs)
            SC = 128.0 ** -0.5
            bm = Bump(ar, 0, 150 * KB)
            colm = bm.v([128, 960], F32)
            Mh = bm.v([128, 960], F32)
            P.dma(colm, colm_d)
            pairs = [(qg, ktl) for qg in range(4) for ktl in KTW[qg]]
            TP = {pr: bm.v([128, 512], F32) for pr in pairs}
            MhR = Rot([Mh, bm.v([128, 960], F32)])
            NQr = Rot(bm.vs(2, [128, SEQ], BF16))
            NKr = Rot(bm.vs(2, [128, SEQ], BF16))
            NV4r = Rot(bm.vs(2, [128, NKT, 512], BF16))
            PT = Rot(bm.vs(4, [128, 512], BF16))
            lnr = Rot(bm.vs(2, [128, 512], F32))
            recr = Rot(bm.vs(2, [128, 512], F32))
            yo = Rot(bm.vs(3, [128, 512], BF16))
            for pr in pairs:
                P.memset(TP[pr], NEG)
            valid = {}
            for (qg, ktl) in pairs:
                for half in range(2):
                    j = 2 * ktl + half
                    rows = [r for r in range(8 * qg, 8 * qg + 8)
                            if min(max(r - 4, 0), 24) <= j < min(max(r - 4, 0), 24) + 8]
                    if rows:
                        valid[(qg, ktl, half)] = ((rows[0] - 8 * qg) * 64, len(rows) * 64, (rows[0] - j + 7) * 64)
            accsets = Rot([(P.psl[0], P.psl[1]), (P.psl[2], P.psl[3])])
            sbank = Rot(P.psl[4:7])
            loaded = {}

            def prep(h):
                if h >= 8:
                    return
                if h % 4 == 0:
                    nv = NV4r.next()
                    P.dma(nv, NV[:, (h // 4) * 512:(h // 4 + 1) * 512].re("(k p) c -> p k c", p=128))
                    loaded["nv"] = nv
                nq, nk, mh = NQr.next(), NKr.next(), MhR.next()
                P.dma(nq, NQ[h])
                P.dma(nk, NK[h])
                P.dma(mh[0:64, :], rpbe_d[L, h])
                P.dma(mh[64:128, :], rpbe_d[L, h])
                P.tt(mh, mh, colm, ALU.add, eng="pool")
                loaded[h] = (nq, nk, loaded["nv"], mh)

            prep(0)
            for h in range(8):
                prep(h + 1)
                NQh, NKh, NV4, mh = loaded.pop(h)
                for (s0, nn) in qtiles_seq(L):
                    keys = [(kt, None) for kt in range(2)]
                    if s0 > 0:
                        qg = (s0 - CTX) // 512
                        keys += [(2 + ktl, (qg, ktl)) for ktl in KTW[qg]]
                    pn, pd = accsets.next()
                    if jobs:
                        jobs.pop(0)(P.psl[7])

                    def smm(kt):
                        sb_ = sbank.next()
                        P.mm(sb_[:, :nn], NKh[:, kt * 128:(kt + 1) * 128], NQh[:, s0:s0 + nn])
                        return sb_

                    s_cur = smm(keys[0][0])
                    for idx, (kt, pr) in enumerate(keys):
                        s_next = smm(keys[idx + 1][0]) if idx + 1 < len(keys) else None
                        Pv = PT.next()
                        if pr is None:
                            P.act(Pv[:, :nn], s_cur[:, :nn], AF.Exp, scale=SC)
                        else:
                            for half in range(2):
                                vv = valid.get((pr[0], pr[1], half))
                                if vv is None:
                                    continue
                                a_, ln, d0 = vv
                                hp = slice(half * 64, half * 64 + 64)
                                P.stt(TP[pr][hp, a_:a_ + ln], s_cur[hp, a_:a_ + ln], SC, mh[hp, d0:d0 + ln],
                                      ALU.mult, ALU.add)
                            P.act(Pv[:, :nn], TP[pr][:, :nn], AF.Exp)
                        first, last = idx == 0, idx == len(keys) - 1
                        hh = h % 4
                        P.mm(pn[:, :nn], NV4[:, kt, hh * 128:(hh + 1) * 128], Pv[:, :nn], start=first, stop=last)
                        P.mm(pd[:, :nn], onesb, Pv[:, :nn], start=first, stop=last)
                        s_cur = s_next
                    ln_, rec = lnr.next(), recr.next()
                    P.act(ln_[:, :nn], pd[:, :nn], AF.Ln)
                    P.act(rec[:, :nn], ln_[:, :nn], AF.Exp, scale=-1.0)
                    y_ = yo.next()
                    P.tt(y_[:, :nn], pn[:, :nn], rec[:, :nn], ALU.mult)
                    P.dma(YA[1, h, :, s0:s0 + nn], y_[:, :nn])
            while jobs:
                jobs.pop(0)(P.psl[7])

        def lru(L, jobs=()):
            jobs = list(jobs)
            per = (len(jobs) + 15) // 16
            bm = Bump(ar, 0, 140 * KB)
            LXP = Rot(bm.vs(2, [128, SEQ + 6], F32))
            XCr = Rot(bm.vs(2, [128, SEQ], F32))
            XCbr = Rot(bm.vs(2, [128, SEQ], BF16))
            Rr = bm.vs(2, [128, SEQ], F32)
            Ii = bm.vs(2, [128, SEQ], F32)
            Tm = bm.vs(2, [128, SEQ], F32)
            H0 = bm.v([128, SEQ], F32)
            glg = Rot(bm.vs(2, [128, SEQ], BF16))
            yl = Rot(bm.vs(2, [128, SEQ], BF16))
            W32 = Rot(bm.vs(2, [128, 4, 128], F32))
            Wb = Rot(bm.vs(2, [128, 4, 128], BF16))
            cw = bm.v([128, 8, 4], F32)
            cbias = bm.v([128, 8], F32)
            ba = bm.v([128, 2, 8], F32)
            bx = bm.v([128, 2, 8], F32)
            lam = bm.v([128, 2, 8], F32)
            m8 = bm.v([128, 2, 8], F32)
            m16 = bm.v([128, 2, 8], F32)
            P.dma(cw, lcw_d[L])
            P.dma(cbias, lcb_d[L])
            P.dma(ba, lba_d[L])
            P.dma(bx, lbx_d[L])
            P.dma(lam, llam_d[L])
            P.act(lam, lam, AF.Exp, scale=-1.0)
            P.act(lam, lam, AF.Ln, bias=1.0)
            P.ts(m8, lam, -8.0, ALU.mult)
            P.ts(m16, lam, -16.0, ALU.mult)
            for lx_ in LXP.items:
                P.memset(lx_, 0.0)
            SEGS = [(1, CTX, 0), (CTX + 4, LAT, CTX)]
            STILES = [(0, 256), (256, 512), (768, 512), (1280, 512), (1792, 512)]
            staged = {}

            def stage_a(c):
                if c >= 8:
                    return
                lxp = LXP.next()
                P.dma(lxp[:, 1:1 + CTX], LX[c, :, 0:CTX])
                P.dma(lxp[:, CTX + 4:CTX + 4 + LAT], LX[c, :, CTX:SEQ])
                gl_ = glg.next()
                P.dma(gl_, GLG[c])
                XC, XCb = XCr.next(), XCbr.next()
                for (base, n, o) in SEGS:
                    P.ts(XC[:, o:o + n], lxp[:, base - 1:base - 1 + n], cw[:, c, 0:1], ALU.mult,
                         cbias[:, c:c + 1], ALU.add)
                    for j in range(1, 4):
                        P.stt(XC[:, o:o + n], lxp[:, base - 1 + j:base - 1 + j + n], cw[:, c, j:j + 1],
                              XC[:, o:o + n], ALU.mult, ALU.add)
                P.copy(XCb, XC, eng="act")
                w32 = W32.next()
                wb = Wb.next()
                for n in range(2):
                    P.dma(w32[:, n, :], lwa_d[L, n, c])
                    P.dma(w32[:, 2 + n, :], lwx_d[L, n, c])
                P.copy(wb, w32, eng="pool")
                staged[c] = (XC, XCb, wb, gl_)

            stage_a(0)
            for c in range(8):
                XC, XCb, wb, gl_ = staged.pop(c)
                for n in range(2):
                    for _ in range(per):
                        if jobs:
                            jobs.pop(0)(None)
                    for (s0, sn) in STILES:
                        psa = P.ps()
                        P.mm(psa[:, :sn], wb[:, n, :], XCb[:, s0:s0 + sn])
                        P.act(Rr[n][:, s0:s0 + sn], psa[:, :sn], AF.Sigmoid, bias=ba[:, n, c:c + 1])
                        psx = P.ps()
                        P.mm(psx[:, :sn], wb[:, 2 + n, :], XCb[:, s0:s0 + sn])
                        P.act(Ii[n][:, s0:s0 + sn], psx[:, :sn], AF.Sigmoid, bias=bx[:, n, c:c + 1])
                stage_a(c + 1)
                for n in range(2):
                    P.act(Tm[n], Rr[n], AF.Exp, scale=m16[:, n, c:c + 1])
                    P.act(Rr[n], Rr[n], AF.Exp, scale=m8[:, n, c:c + 1])
                for n in range(2):
                    P.ts(Tm[n], Tm[n], -1.0, ALU.mult, 1.0, ALU.add, eng="pool")
                for n in range(2):
                    P.act(Tm[n], Tm[n], AF.Sqrt)
                for n in range(2):
                    P.tt(Ii[n], Ii[n], Tm[n], ALU.mult)
                for n in range(2):
                    P.tt(Ii[n], Ii[n], XC, ALU.mult, eng="pool")
                P.scan(H0, Rr[0], Ii[0], 0.0, ALU.mult, ALU.add)
                P.scan(Tm[1][:, 0:CTX][:, ::-1], Rr[1][:, 0:CTX][:, ::-1], Ii[1][:, 0:CTX][:, ::-1], 0.0,
                       ALU.mult, ALU.add)
                P.scan(Tm[1][:, CTX:SEQ][:, ::-1], Rr[1][:, CTX:SEQ][:, ::-1], Ii[1][:, CTX:SEQ][:, ::-1],
                       Tm[1][:, 0:1], ALU.mult, ALU.add)
                P.tt(H0, H0, Tm[1], ALU.add, eng="pool")
                y_ = yl.next()
                P.tt(y_, H0, gl_, ALU.mult)
                P.dma(YA[2, c], y_)
            while jobs:
                jobs.pop(0)(None)

        def mixers(L):
            P.barrier()
            mlstm(L)
            P.barrier()
            jobs = []
            if (not mix_test) and L + 1 < n_layers:
                mod_begin(L + 1, WS2)
                jobs = [(lambda bank, j=j: mod_panel(L + 1, j, WS2, bank)) for j in range(72)]
            na(L, jobs[:36])
            P.barrier()
            lru(L, jobs[36:])
            if jobs:
                mod_finish(L + 1)
            P.barrier()

        def merge(L, g, qtiles):
            sbm = Bump(ar, S_OFF, 197 * KB)
            sgr = Rot(sbm.vs(2, [128, 512], F32))
            mo = Rot(sbm.vs(3, [128, 512], BF16))
            acc_t = sbm.raw([128, 2, T], F32)
            ACC = [[P.new(acc_t[:, o2, t0:t0 + nn]) for (t0, nn) in TILES_ALL] for o2 in range(2)]
            for ocp in range(8):
                for n in range(3):
                    pg = WS.next()
                    pb = WS.next()
                    for o2 in range(2):
                        oc = ocp * 2 + o2
                        for ti in qtiles:
                            t0, nn = TILES_ALL[ti]
                            psg = P.ps()
                            psb = P.ps()
                            for kc in range(KC):
                                P.mm(psg[:, :nn], pg[:, kc, o2 * 128:(o2 + 1) * 128], HN[kc][ti],
                                     start=(kc == 0), stop=(kc == KC - 1))
                            for kc in range(8):
                                P.mm(psb[:, :nn], pb[:, kc, o2 * 128:(o2 + 1) * 128], Ysb[n][kc][:, t0:t0 + nn],
                                     start=(kc == 0), stop=(kc == 7))
                            sg = sgr.next()
                            P.act(sg[:, :nn], psg[:, :nn], AF.Sigmoid)
                            if n == 0:
                                P.tt(ACC[o2][ti], sg[:, :nn], psb[:, :nn], ALU.mult)
                            else:
                                P.tt(sg[:, :nn], sg[:, :nn], psb[:, :nn], ALU.mult)
                                if n == 1:
                                    P.tt(ACC[o2][ti], ACC[o2][ti], sg[:, :nn], ALU.add, eng="pool")
                                else:
                                    o = mo.next()
                                    P.tt(o[:, :nn], ACC[o2][ti], sg[:, :nn], ALU.add, eng="pool")
                                    P.dma(MRG[g][oc, :, t0:t0 + nn], o[:, :nn])

        def final_out(g):
            sbm = Bump(ar, S_OFF, 197 * KB)
            sq = Rot(sbm.vs(3, [128, 512], BF16))
            og = Rot(sbm.vs(3, [128, 512], F32))
            rs = sbm.v([128, 512], F32)
            rstd = sbm.v([128, 512], F32)
            for ti in (1, 2):
                t0, nn = TILES_ALL[ti]
                ps = P.ps()
                for kc in range(KC):
                    s = sq.next()
                    P.act(s[:, :nn], X[kc][ti], AF.Square)
                    P.mm(ps[:, :nn], onesb, s[:, :nn], start=(kc == 0), stop=(kc == KC - 1))
                P.act(rs[:, :nn], ps[:, :nn], AF.Ln, bias=epsc, scale=1.0 / D)
                P.act(rstd[:, :nn], rs[:, :nn], AF.Exp, scale=-0.5)
                for kc in range(KC):
                    o = og.next()
                    P.stt(o[:, :nn], X[kc][ti], gfin[:, kc:kc + 1], rstd[:, :nn], ALU.mult, ALU.mult)
                    P.dma(outT_d[g][kc * 128:(kc + 1) * 128, t0 - CTX:t0 - CTX + nn], o[:, :nn])

        def load_X(g, tiles):
            for ti in tiles:
                t0, nn = TILES_ALL[ti]
                xv = V(Xt[:, :, t0:t0 + nn], [X[kc][ti].bufs[0] for kc in range(KC)])
                P.dma(xv, XS[g][:, :, t0:t0 + nn])

        def stageC(L, g):
            last = (L == 1)
            qtiles = [0, 1, 2] if (g == 0 and not last) else [1, 2]
            P.barrier()
            for ti in qtiles:
                t0, nn = TILES_ALL[ti]
                hv = V(HNt[:, :, t0:t0 + nn], [HN[kc][ti].bufs[0] for kc in range(KC)])
                P.dma(hv, HS[g][:, :, t0:t0 + nn])
            specs = []
            for ocp in range(8):
                for n in range(3):
                    c0 = OFF["gx"] + n * D + ocp * 256
                    specs.append((w_in_d[L, :, c0:c0 + 256], KC, 256, True))
                    specs.append((w_br_d[L, n, :, ocp * 256:(ocp + 1) * 256], 8, 256, True))
            WS.begin(specs)
            for n in range(3):
                yv = V(Ysb_t[n][:], [b for v in Ysb[n] for b in v.bufs])
                P.dma(yv[:, :, CTX:T], YA[n, :, :, CTX + g * GL:CTX + (g + 1) * GL].re("c p t -> p c t"))
                if 0 in qtiles:
                    P.dma(yv[:, :, 0:CTX], YA[n, :, :, 0:CTX].re("c p t -> p c t"))
            merge(L, g, qtiles)
            WS.begin([(w_out_d[L, :, j * 256:(j + 1) * 256], KC, 256, True) for j in range(8)])
            P.barrier()
            for ti in qtiles:
                t0, nn = TILES_ALL[ti]
                load_X(g, [ti])
                hv = V(HNt[:, :, t0:t0 + nn], [HN[kc][ti].bufs[0] for kc in range(KC)])
                P.dma(hv, MRG[g][:, :, t0:t0 + nn].re("k p t -> p k t"))
            for j in range(8):
                pan = WS.next()
                for o2 in range(2):
                    oc = 2 * j + o2
                    for ti in qtiles:
                        t0, nn = TILES_ALL[ti]
                        ps = P.ps()
                        for kc in range(KC):
                            P.mm(ps[:, :nn], pan[:, kc, o2 * 128:(o2 + 1) * 128], HN[kc][ti],
                                 start=(kc == 0), stop=(kc == KC - 1))
                        P.stt(X[oc][ti], ps[:, :nn], Gmod[L][1][wq(ti)][:, oc:oc + 1], X[oc][ti],
                              ALU.mult, ALU.add)
            P.barrier()
            ffn(L, 2, qtiles, f2in_d, f2out_d)
            if last:
                P.barrier()
                final_out(g)
            else:
                P.dma(XS[g], Xall)
            P.barrier()

        if mix_test:
            mixers(0)
            dump("YA", YA)
        for L in range(0 if mix_test else n_layers):
            for g in groups:
                gt = [0, 1, 2] if g == 0 else [1, 2]
                P.barrier()
                if L == 0:
                    P.dma(Xlat, xT_d[g].re("(k p) t -> p k t", p=128))
                    if g == 0:
                        P.dma(Xctx, ctxT_d.re("(k p) t -> p k t", p=128))
                else:
                    load_X(g, gt)
                ffn(L, 0, gt, f1in_d, f1out_d)
                P.barrier()
                inproj(L, g)
                P.barrier()
            mixers(L)
            for g in groups:
                stageC(L, g)
        P.finalize(st)
    return nc


def _fm(v):
    v = np.asarray(v, np.float32)
    n = v.shape[-1] // 128
    return np.ascontiguousarray(np.swapaxes(v.reshape(v.shape[:-1] + (n, 128)), -1, -2))


def host_consts():
    c = {}
    perm = np.zeros((128, 128), np.float32)
    for m in range(128):
        perm[(m + 64) % 128, m] = 1.0
    c["perm"] = perm
    s = np.arange(128)[:, None]
    t = np.arange(512)[None, :]
    masks = np.zeros((128, 8, 512), np.float32)
    for o in range(4):
        masks[:, o, :] = (o * 128 + s <= t)
        masks[:, 4 + o, :] = (o * 128 + s >= t)
    c["masks"] = masks
    inv = 10000.0 ** (-np.arange(64, dtype=np.float64) / 64)
    invp = np.concatenate([inv, inv])
    sgn = np.concatenate([-np.ones(64), np.ones(64)])
    pos = np.arange(LAT)
    rope = np.zeros((2, 128, 4, GL), np.float32)
    for g in range(2):
        pg = pos[g * GL:(g + 1) * GL]
        for k, pp in enumerate((pg // 64, pg % 64)):
            ang = (pp[None, :].astype(np.float32) * invp[:, None].astype(np.float32)).astype(np.float64)
            rope[g, :, 2 * k, :] = np.cos(ang)
            rope[g, :, 2 * k + 1, :] = np.sin(ang) * sgn[:, None]
    c["rope"] = rope
    sel = np.zeros((64, 4, 128), np.float32)
    id4 = np.zeros((64, 4), np.float32)
    for base in (0, 32):
        for h in range(4):
            sel[base + h, h, :] = 1.0
            id4[base + h, h] = 1.0
    c["sel"] = sel
    c["id4"] = id4
    kc = np.arange(64)[:, None]
    qc = np.arange(64)[None, :]
    cs = np.clip(qc - 8, 0, 48)
    ok = (kc >= cs) & (kc < cs + 16)
    cm = np.where(ok, 0.0, NEG).astype(np.float32)
    cm = np.broadcast_to(cm[:, None, :], (64, 15, 64)).reshape(64, 15 * 64)
    c["colm"] = np.ascontiguousarray(np.concatenate([cm, cm], 0))
    return c


def prep_inputs(inp, n_cores=4):
    f = lambda k: np.asarray(inp[k], np.float32)
    shared = host_consts()
    for k in ("w_mod", "ffn1_w_in", "ffn1_w_out", "ffn2_w_in", "ffn2_w_out", "w_in", "w_branch", "w_out"):
        shared[k] = np.ascontiguousarray(f(k))
    shared["bmod"] = _fm(f("b_mod"))
    shared["gains"] = np.ascontiguousarray(np.stack([_fm(f("norm_ffn1")), _fm(f("norm_mix")), _fm(f("norm_ffn2"))], 1))
    shared["gfin"] = _fm(f("norm_final"))
    mb = np.zeros((2, 2, 64, 1), np.float32)
    for wi, k in enumerate(("mlstm_b_i", "mlstm_b_f")):
        v = f(k)
        for d in range(2):
            mb[wi, :, 32 * d:32 * d + 4, 0] = v[:, d, :]
    shared["mbi"] = mb[0]
    shared["mbf"] = mb[1]
    shared["mgn"] = _fm(f("mlstm_gn"))
    rpb = f("na_rpb")
    kc = np.arange(64)[:, None]
    qc = np.arange(64)[None, :]
    dc = np.clip(kc - qc, -15, 15) + 15
    e = rpb[:, :, ::-1, :][:, :, :, dc]
    shared["rpbe"] = np.ascontiguousarray(np.transpose(e, (0, 1, 3, 2, 4)).reshape(2, 8, 64, 15 * 64))
    shared["lcw"] = np.ascontiguousarray(np.transpose(_fm(f("lru_conv_w")), (0, 2, 3, 1)))
    shared["lcb"] = _fm(f("lru_conv_b"))
    shared["lwa"] = np.ascontiguousarray(f("lru_w_a"))
    shared["lwx"] = np.ascontiguousarray(f("lru_w_x"))
    for k, src in (("lba", "lru_b_a"), ("lbx", "lru_b_x"), ("llam", "lru_lambda")):
        shared[k] = np.ascontiguousarray(np.transpose(_fm(f(src)), (0, 2, 1, 3)))
    x, c, ctx, c_ctx = f("x"), f("c"), f("ctx"), f("c_ctx")
    maps = []
    for core in range(n_cores):
        b = core
        m = dict(shared)
        xt = np.ascontiguousarray(x[b].T)
        m["xT"] = np.ascontiguousarray(np.stack([xt[:, 0:GL], xt[:, GL:2 * GL]], 0))
        m["ctxT"] = np.ascontiguousarray(ctx[b].T)
        m["ccol"] = np.ascontiguousarray(np.stack([_fm(c[b]), _fm(c_ctx)], -1))
        maps.append(m)
    return maps


_NC_CACHE = {}


ACTIVE = (0, 1, 4, 5)


def kernel(**inputs):
    n_cores = 8
    if "nc" not in _NC_CACHE:
        _NC_CACHE["nc"] = build()
    nc = _NC_CACHE["nc"]
    real = prep_inputs(inputs, len(ACTIVE))
    zero = {k: np.zeros_like(v) for k, v in real[0].items()}
    maps = [zero] * n_cores
    for b, core in enumerate(ACTIVE):
        maps[core] = real[b]
    res = run_bass_kernel_spmd(nc, maps, core_ids=list(range(n_cores)))
    out = np.zeros((len(ACTIVE), LAT, D), np.float32)
    for b, core in enumerate(ACTIVE):
        o = np.asarray(res.results[core]["outT"], np.float32)
        out[b] = np.concatenate([o[0].T, o[1].T], 0)
    return out
```

```python
import contextlib
import numpy as np
import concourse.bass as bass
import concourse.mybir as mybir
from concourse.bass_utils import run_bass_kernel_spmd

F32 = mybir.dt.float32
BF16 = mybir.dt.bfloat16
AF = mybir.ActivationFunctionType
ALU = mybir.AluOpType

COMPUTE = ("pe", "act", "dve", "pool")
DMAQ = ("sp", "act", "pool")
NRING = 8
RING = {"sp": 8, "act": 8, "pool": 4}


class Buf:
    __slots__ = ("wc", "wd", "rc", "rd")

    def __init__(self):
        self.wc = None
        self.wd = {}
        self.rc = {}
        self.rd = {}


class V:
    __slots__ = ("ap", "bufs")

    def __init__(self, ap, bufs):
        self.ap = ap
        self.bufs = tuple(bufs)

    def __getitem__(self, idx):
        return V(self.ap[idx], self.bufs)

    def re(self, pattern, **kw):
        return V(self.ap.rearrange(pattern, **kw), self.bufs)


def _ap(x):
    return x.ap if isinstance(x, V) else x


class Op:
    __slots__ = ("eng", "fn", "deps", "sig", "sigval", "is_dma", "qidx", "id")


class Prog:
    def __init__(self, nc):
        self.nc = nc
        self.ops = []
        self.by_eng = {e: [] for e in ("pe", "act", "dve", "pool", "sp")}
        self.ndma = {e: 0 for e in DMAQ}
        self.bar = {}
        self.psl = []
        self.psi = 0

    def new(self, ap):
        return V(ap, [Buf()])

    def ps(self):
        v = self.psl[self.psi % len(self.psl)]
        self.psi += 1
        return v

    def barrier(self):
        deps = []
        for e in ("pe", "act", "dve", "pool", "sp"):
            lst = self.by_eng[e]
            seen_c = False
            nd = 0
            for op in reversed(lst):
                if op.is_dma:
                    if nd < NRING:
                        deps.append(op.id)
                        nd += 1
                elif not seen_c:
                    deps.append(op.id)
                    seen_c = True
                if seen_c and nd >= NRING:
                    break
        self.bar = {e: list(deps) for e in ("pe", "act", "dve", "pool", "sp")}

    def emit(self, eng, fn, reads=(), writes=(), dma=False):
        op = Op()
        op.eng = eng
        op.fn = fn
        op.sig = False
        op.sigval = None
        op.is_dma = dma
        op.id = len(self.ops)
        op.qidx = None
        if dma:
            op.qidx = self.ndma[eng]
            self.ndma[eng] += 1
        ops = self.ops
        ceng = {}
        dmadeps = set()

        def add(i):
            d = ops[i]
            if d.is_dma:
                dmadeps.add(i)
            else:
                if d.eng == "pe" and eng == "pe" and not dma:
                    return
                if ceng.get(d.eng, -1) < i:
                    ceng[d.eng] = i

        pend = self.bar.pop(eng, None)
        if pend:
            for i in pend:
                add(i)
        for v in reads:
            if not isinstance(v, V):
                continue
            for b in v.bufs:
                if b.wc is not None:
                    add(b.wc)
                for l in b.wd.values():
                    for i in l:
                        add(i)
        for v in writes:
            for b in v.bufs:
                had_reads = bool(b.rc) or bool(b.rd)
                for i in b.rc.values():
                    add(i)
                for l in b.rd.values():
                    for i in l:
                        add(i)
                if b.wc is not None:
                    add(b.wc)
                if (not dma) or had_reads:
                    for l in b.wd.values():
                        for i in l:
                            add(i)
        deps = list(ceng.values()) + list(dmadeps)
        for i in deps:
            ops[i].sig = True
        op.deps = deps
        for v in reads:
            if not isinstance(v, V):
                continue
            for b in v.bufs:
                if dma:
                    l = b.rd.setdefault(eng, [])
                    l.append(op.id)
                    if len(l) > RING[eng]:
                        del l[0]
                else:
                    b.rc[eng] = op.id
        for v in writes:
            for b in v.bufs:
                if dma:
                    if b.rc or b.rd:
                        b.wd = {}
                    b.wc = None
                    l = b.wd.setdefault(eng, [])
                    l.append(op.id)
                    if len(l) > RING[eng]:
                        del l[0]
                else:
                    b.wc = op.id
                    b.wd = {}
                b.rc = {}
                b.rd = {}
        ops.append(op)
        self.by_eng[eng].append(op)
        return op

    def mm(self, out, lhsT, rhs, start=True, stop=True):
        o, l, r = _ap(out), _ap(lhsT), _ap(rhs)
        rd = [lhsT, rhs] + ([] if start else [out])
        self.emit("pe", lambda e: e.matmul(o, l, r, start=start, stop=stop), rd, [out])

    def act(self, out, in_, func, bias=None, scale=None):
        o, i = _ap(out), _ap(in_)
        kw = {}
        rd = [in_]
        if bias is not None:
            kw["bias"] = _ap(bias)
            rd.append(bias)
        if scale is not None:
            kw["scale"] = _ap(scale)
            rd.append(scale)
        self.emit("act", lambda e: e.activation(o, i, func, **kw), rd, [out])

    def tt(self, out, in0, in1, op, eng="dve"):
        o, a, b = _ap(out), _ap(in0), _ap(in1)
        self.emit(eng, lambda e: e.tensor_tensor(o, a, b, op), [in0, in1], [out])

    def ts(self, out, in0, s1, op0, s2=None, op1=None, eng="dve"):
        o, a = _ap(out), _ap(in0)
        rd = [in0, s1, s2]
        a1, a2 = _ap(s1), _ap(s2)
        if op1 is None:
            self.emit(eng, lambda e: e.tensor_scalar(o, a, a1, None, op0), rd, [out])
        else:
            self.emit(eng, lambda e: e.tensor_scalar(o, a, a1, a2, op0, op1), rd, [out])

    def stt(self, out, in0, scalar, in1, op0, op1):
        o, a, s, b = _ap(out), _ap(in0), _ap(scalar), _ap(in1)
        self.emit("dve", lambda e: e.scalar_tensor_tensor(o, a, s, b, op0, op1), [in0, scalar, in1], [out])

    def scan(self, out, d0, d1, initial, op0, op1):
        o, a, b, i = _ap(out), _ap(d0), _ap(d1), _ap(initial)
        self.emit("dve", lambda e: e.tensor_tensor_scan(o, a, b, i, op0, op1), [d0, d1, initial], [out])

    def copy(self, out, in_, eng="dve"):
        o, i = _ap(out), _ap(in_)
        if eng == "act":
            self.emit("act", lambda e: e.activation(o, i, AF.Identity), [in_], [out])
        else:
            self.emit(eng, lambda e: e.tensor_copy(o, i), [in_], [out])

    def recip(self, out, in_):
        o, i = _ap(out), _ap(in_)
        self.emit("dve", lambda e: e.reciprocal(o, i), [in_], [out])

    def memset(self, out, val, eng="pool"):
        o = _ap(out)
        self.emit(eng, lambda e: e.memset(o, val), [], [out])

    def dma(self, out, in_, q="pool", **kw):
        o, i = _ap(out), _ap(in_)
        self.emit(q, lambda e: e.dma_start(o, i, **kw), [in_], [out], dma=True)

    def finalize(self, stack):
        nc = self.nc
        sems = {e: stack.enter_context(nc.semaphore("s_" + e)) for e in COMPUTE}
        rings = {q: [stack.enter_context(nc.semaphore("r_%s%d" % (q, k))) for k in range(RING[q])]
                 for q in DMAQ if self.ndma[q] > 0}
        cnt = {e: 0 for e in COMPUTE}
        for op in self.ops:
            if not op.is_dma and op.sig:
                cnt[op.eng] += 1
                op.sigval = cnt[op.eng]
        ops = self.ops
        by_eng = self.by_eng
        ndma = self.ndma

        def run(eng_name, e):
            waited = {}

            def wait(s, v):
                k = id(s)
                if waited.get(k, 0) < v:
                    e.wait_ge(s, v)
                    waited[k] = v

            for op in by_eng[eng_name]:
                if op.is_dma and op.qidx >= RING[eng_name]:
                    nr = RING[eng_name]
                    wait(rings[eng_name][op.qidx % nr], 16 * (op.qidx // nr))
                for d in op.deps:
                    dop = ops[d]
                    if dop.is_dma:
                        nr = RING[dop.eng]
                        wait(rings[dop.eng][dop.qidx % nr], 16 * (dop.qidx // nr + 1))
                    else:
                        wait(sems[dop.eng], dop.sigval)
                inst = op.fn(e)
                if op.is_dma:
                    inst.then_inc(rings[eng_name][op.qidx % RING[eng_name]], 16)
                elif op.sig:
                    inst.then_inc(sems[eng_name], 1)
            if eng_name in rings:
                n = ndma[eng_name]
                nr = RING[eng_name]
                for k in range(min(n, nr)):
                    uses = (n - 1 - k) // nr + 1
                    wait(rings[eng_name][k], 16 * uses)

        with nc.Block() as block:
            @block.tensor
            def _(e):
                run("pe", e)

            @block.scalar
            def _(e):
                run("act", e)

            @block.vector
            def _(e):
                run("dve", e)

            @block.gpsimd
            def _(e):
                run("pool", e)

            @block.sync
            def _(e):
                run("sp", e)


D = 2048
KC = 16
DFF = 5632
PIN = 15376
BW = 1024
LAT = 2048
CTX = 256
GL = 1024
T = CTX + GL
SEQ = CTX + LAT
NKT = SEQ // 128
EPS = 1e-6
TILES_ALL = [(0, 256), (256, 512), (768, 512)]
OFF = dict(mq=0, mk=1024, mv=2048, mo=3072, mg=4096, nq=4112, nk=5136, nv=6160, lx=7184, lg=8208, gx=9232)
NEG = -30000.0
KB = 1024


class Arena:
    def __init__(self, nc, P, stack, nbytes):
        self.nc = nc
        self.P = P
        self.base = (nc.sbuf_base + 31) // 32 * 32
        stack.enter_context(nc.sbuf_tensor("arena", [128, nbytes], mybir.dt.uint8))
        assert nc.sbuf_base == self.base + nbytes, (nc.sbuf_base, self.base, nbytes)
        self.nbytes = nbytes
        self.n = 0

    def raw(self, shape, dtype, off):
        esz = 4 if dtype == F32 else 2
        sz = esz * int(np.prod(shape[1:]))
        assert off % 4 == 0 and off + sz <= self.nbytes, (shape, off, sz, self.nbytes)
        self.n += 1
        return self.nc.alloc_sbuf_tensor_at("a%d" % self.n, list(shape), dtype, offset=self.base + off)

    def v(self, shape, dtype, off):
        return self.P.new(self.raw(shape, dtype, off)[:])


class Bump:
    def __init__(self, ar, lo, hi):
        self.ar, self.lo, self.hi, self.top = ar, lo, hi, lo

    def raw(self, shape, dtype):
        esz = 4 if dtype == F32 else 2
        sz = (esz * int(np.prod(shape[1:])) + 31) // 32 * 32
        assert self.top + sz <= self.hi, ("bump overflow", shape, self.top, sz, self.hi)
        t = self.ar.raw(shape, dtype, self.top)
        self.top += sz
        return t

    def v(self, shape, dtype):
        return self.ar.P.new(self.raw(shape, dtype)[:])

    def vs(self, n, shape, dtype):
        return [self.v(shape, dtype) for _ in range(n)]


class Rot:
    def __init__(self, items):
        self.items, self.i = items, 0

    def next(self):
        v = self.items[self.i % len(self.items)]
        self.i += 1
        return v


class WStream:
    def __init__(self, P, stages, bfs):
        self.P = P
        self.stages = Rot(stages)
        self.bfs = Rot(bfs)
        self.specs = []
        self.i = 0
        self.loaded = {}
        self.ci = 0

    def begin(self, specs):
        self.specs = list(specs)
        self.i = 0
        self.loaded = {}
        self._load(0)

    def _load(self, j):
        if j >= len(self.specs) or j in self.loaded:
            return
        src, nk, nc_, cast = self.specs[j]
        P = self.P
        st = self.stages.next()
        stv = V(st.ap[:, 0:nk * nc_].rearrange("p (k c) -> p k c", k=nk), st.bufs)
        P.dma(stv, src.re("(k p) c -> p k c", p=128), q="sp")
        if not cast:
            self.loaded[j] = stv
            return
        bf = self.bfs.next()
        bfv = V(bf.ap[:, 0:nk * nc_].rearrange("p (k c) -> p k c", k=nk), bf.bufs)
        P.copy(bfv, stv, eng="act")
        self.loaded[j] = bfv

    def next(self):
        j = self.i
        self.i += 1
        self._load(j)
        self._load(j + 1)
        return self.loaded.pop(j)


def build(n_layers=2, groups=(0, 1), stop=None, dbg=(), mix_test=False):
    nc = bass.Bass("TRN2", target_bir_lowering=False)
    P = Prog(nc)
    NG = 2

    BIGW = ("w_mod", "ffn1_w_in", "ffn1_w_out", "ffn2_w_in", "ffn2_w_out", "w_in", "w_branch", "w_out", "xT", "ctxT")

    def din(name, shape, dt=F32):
        if mix_test and name in BIGW:
            return None
        return V(nc.dram_tensor(name, list(shape), dt, kind="ExternalInput").ap(), [Buf()])

    def dscr(name, shape, dt):
        return nc.dram_tensor(name, list(shape), dt, kind="Internal").ap()

    xT_d = din("xT", [NG, D, GL])
    ctxT_d = din("ctxT", [D, CTX])
    ccol_d = din("ccol", [128, KC, 2])
    w_mod_d = din("w_mod", [2, D, 9 * D])
    bmod_d = din("bmod", [2, 128, 144])
    gains_d = din("gains", [2, 3, 128, KC])
    gfin_d = din("gfin", [128, KC])
    f1in_d = din("ffn1_w_in", [2, D, 2 * DFF])
    f1out_d = din("ffn1_w_out", [2, DFF, D])
    f2in_d = din("ffn2_w_in", [2, D, 2 * DFF])
    f2out_d = din("ffn2_w_out", [2, DFF, D])
    w_in_d = din("w_in", [2, D, PIN])
    w_br_d = din("w_branch", [2, 3, BW, D])
    w_out_d = din("w_out", [2, D, D])
    mbi_d = din("mbi", [2, 64, 1])
    mbf_d = din("mbf", [2, 64, 1])
    mgn_d = din("mgn", [2, 128, 8])
    rpbe_d = din("rpbe", [2, 8, 64, 15 * 64])
    colm_d = din("colm", [128, 15 * 64])
    lcw_d = din("lcw", [2, 128, 8, 4])
    lcb_d = din("lcb", [2, 128, 8])
    lwa_d = din("lwa", [2, 2, 8, 128, 128])
    lwx_d = din("lwx", [2, 2, 8, 128, 128])
    lba_d = din("lba", [2, 128, 2, 8])
    lbx_d = din("lbx", [2, 128, 2, 8])
    llam_d = din("llam", [2, 128, 2, 8])
    perm_d = din("perm", [128, 128])
    masks_d = din("masks", [128, 8, 512])
    rope_d = din("rope", [NG, 128, 4, GL])
    sel_d = din("sel", [64, 4, 128])
    id4_d = din("id4", [64, 4])
    outT = nc.dram_tensor("outT", [NG, D, GL], F32, kind="ExternalOutput").ap()
    outT_d = [V(outT[g], [Buf()]) for g in range(NG)]
    dbg_d = {}
    for name, shape, dt in dbg:
        dbg_d[name] = V(nc.dram_tensor("dbg_" + name, list(shape), dt, kind="ExternalOutput").ap(), [Buf()])

    def scr1(name, shape, dt, ext_in=False):
        if mix_test and ext_in:
            return V(nc.dram_tensor(name, list(shape), dt, kind="ExternalInput").ap(), [Buf()])
        return V(dscr(name, shape, dt), [Buf()])

    XS = [scr1("XS%d" % g, [128, KC, T], F32) for g in range(NG)]
    MRG = [scr1("MRG%d" % g, [KC, 128, T], BF16) for g in range(NG)]
    HS = [scr1("HS%d" % g, [128, KC, T], BF16) for g in range(NG)]
    MQ = scr1("MQ", [8, 128, SEQ], BF16, True)
    SO = scr1("SO", [8, 128, SEQ], BF16, True)
    NQ = scr1("NQ", [8, 128, SEQ], BF16, True)
    GLG = scr1("GLG", [8, 128, SEQ], BF16, True)
    MK = scr1("MK", [8, 128, SEQ], BF16, True)
    NK = scr1("NK", [8, 128, SEQ], BF16, True)
    MV = scr1("MV", [SEQ, BW], BF16, True)
    NV = scr1("NV", [SEQ, BW], BF16, True)
    MG = scr1("MG", [4, 4, SEQ], F32, True)
    LX = scr1("LX", [8, 128, SEQ], F32, True)
    YA = scr1("YA", [3, 8, 128, SEQ], BF16)

    def scol(g, t0):
        return t0 if t0 < CTX else CTX + g * GL + (t0 - CTX)

    with contextlib.ExitStack() as st:
        P.psl = [P.new(st.enter_context(nc.psum_tensor("ps%d" % i, [128, 512], F32))[:]) for i in range(8)]
        ARENA_BYTES = 206 * KB
        ar = Arena(nc, P, st, ARENA_BYTES)
        cb = Bump(ar, 197 * KB, ARENA_BYTES)
        ones32 = cb.v([128, 128], F32)
        onesb = cb.v([128, 128], BF16)
        cact = cb.v([128, KC, 2], F32)
        MOD = [cb.v([128, 144, 2], F32) for _ in range(2)]
        Amod = [[[cb.v([128, KC], F32) for w in range(2)] for n in range(3)] for L in range(2)]
        Gmod = [[[cb.v([128, KC], F32) for w in range(2)] for n in range(3)] for L in range(2)]
        gains = cb.v([128, 2, 3, KC], F32)
        gfin = cb.v([128, KC], F32)
        bmod = cb.v([128, 2, 144], F32)
        epsc = cb.v([128, 1], F32)
        P.memset(ones32, 1.0)
        P.memset(onesb, 1.0)
        P.memset(epsc, EPS)
        P.dma(gains, gains_d.re("l n p k -> p l n k"))
        P.dma(gfin, gfin_d)
        P.dma(bmod, bmod_d.re("l p c -> p l c"))

        X_OFF, HN_OFF, W_OFF = 0, 80 * KB, 120 * KB
        Xt = ar.raw([128, KC, T], F32, X_OFF)
        HNt = ar.raw([128, KC, T], BF16, HN_OFF)
        X = [[P.new(Xt[:, kc, t0:t0 + n]) for (t0, n) in TILES_ALL] for kc in range(KC)]
        HN = [[P.new(HNt[:, kc, t0:t0 + n]) for (t0, n) in TILES_ALL] for kc in range(KC)]
        Xall = V(Xt[:], [b for row in X for v in row for b in v.bufs])
        Xlat = V(Xt[:, :, CTX:T], [b for row in X for v in row[1:] for b in v.bufs])
        Xctx = V(Xt[:, :, 0:CTX], [row[0].bufs[0] for row in X])
        wst = [ar.v([128, 4096], F32, W_OFF + i * 16 * KB) for i in range(2)]
        wbf = [ar.v([128, 4096], BF16, W_OFF + 32 * KB + i * 8 * KB) for i in range(3)]
        WS = WStream(P, wst, wbf)
        S_OFF = 176 * KB

        def wq(tile_i):
            return 1 if tile_i == 0 else 0

        def dump(name, src):
            if name in dbg_d:
                P.dma(dbg_d[name], src)

        sb0 = Bump(ar, S_OFF, 197 * KB)
        cc = sb0.v([128, KC, 2], F32)
        P.dma(cc, ccol_d)
        P.act(cact, cc, AF.Silu)
        idc = cb.v([64, 4], F32)
        P.dma(idc, id4_d)
        rwr = Rot(cb.vs(2, [2, 256], F32))

        wst2 = [ar.v([128, 4096], F32, 160 * KB + i * 16 * KB) for i in range(2)]
        WS2 = WStream(P, wst2, wbf)

        def mod_begin(L, ws=None):
            (ws or WS).begin([(w_mod_d[L, :, j * 256:(j + 1) * 256], KC, 256, False) for j in range(72)])

        def mod_panel(L, j, ws=None, bank=None):
            pan = (ws or WS).next()
            ps = bank or P.ps()
            for kc in range(KC):
                P.mm(ps[0:2, 0:256], cact[:, kc, :], pan[:, kc, :], start=(kc == 0), stop=(kc == KC - 1))
            rw = rwr.next()
            P.copy(rw, ps[0:2, 0:256], eng="act")
            for o2 in range(2):
                ps2 = bank or P.ps()
                P.mm(ps2[:, 0:2], rw[0:2, o2 * 128:(o2 + 1) * 128], idc[0:2, 0:2])
                ch = j * 2 + o2
                P.ts(MOD[L][:, ch, :], ps2[:, 0:2], bmod[:, L, ch:ch + 1], ALU.add)

        def mod_finish(L):
            for n in range(3):
                for w in range(2):
                    sc = MOD[L][:, (3 * n + 1) * KC:(3 * n + 2) * KC, w]
                    gt = MOD[L][:, (3 * n + 2) * KC:(3 * n + 3) * KC, w]
                    P.stt(Amod[L][n][w], sc, 1.0, gains[:, L, n, :], ALU.add, ALU.mult)
                    P.ts(Gmod[L][n][w], gt, 1.0 if n == 1 else 0.5, ALU.mult)

        if not mix_test:
            mod_begin(0)
            for j in range(72):
                mod_panel(0, j)
            mod_finish(0)
        dump("mod0", MOD[0])

        def shift_ap(L, n, w, kc):
            return MOD[L][:, 3 * n * KC + kc, w:w + 1]

        def adaln(L, n, tiles, sbm, pre=None):
            if pre is None:
                sq = Rot(sbm.vs(3, [128, 512], BF16))
                tmp = Rot(sbm.vs(2, [128, 512], F32))
                rs = sbm.v([128, 512], F32)
                rstd = sbm.v([128, 512], F32)
            else:
                sq, tmp, rs, rstd = pre
            for ti in tiles:
                t0, nn = TILES_ALL[ti]
                w = wq(ti)
                ps = P.ps()
                for kc in range(KC):
                    s = sq.next()
                    P.act(s[:, :nn], X[kc][ti], AF.Square)
                    P.mm(ps[:, :nn], onesb, s[:, :nn], start=(kc == 0), stop=(kc == KC - 1))
                P.act(rs[:, :nn], ps[:, :nn], AF.Ln, bias=epsc, scale=1.0 / D)
                P.act(rstd[:, :nn], rs[:, :nn], AF.Exp, scale=-0.5)
                for kc in range(KC):
                    tm = tmp.next()
                    P.tt(tm[:, :nn], X[kc][ti], rstd[:, :nn], ALU.mult)
                    P.act(HN[kc][ti], tm[:, :nn], AF.Identity,
                          bias=shift_ap(L, n, w, kc), scale=Amod[L][n][w][:, kc:kc + 1])

        def ffn(L, n, tiles, win_d, wout_d):
            specs = []
            for grp in range(11):
                for half in range(2):
                    c0 = (grp * 4 + half * 2) * 128
                    specs.append((win_d[L, :, c0:c0 + 256], KC, 256, True))
                    specs.append((win_d[L, :, DFF + c0:DFF + c0 + 256], KC, 256, True))
                for ob in range(4):
                    specs.append((wout_d[L, grp * 512:(grp + 1) * 512, ob * 512:(ob + 1) * 512], 4, 512, True))
            WS.begin(specs)
            sbm = Bump(ar, S_OFF, 197 * KB)
            sq = Rot(sbm.vs(3, [128, 512], BF16))
            rs = sbm.v([128, 512], F32)
            rstd = sbm.v([128, 512], F32)
            sil = Rot(sbm.vs(2, [128, 512], F32))
            actg_t = sbm.raw([128, 4, T], BF16)
            ACTG = [[P.new(actg_t[:, j, t0:t0 + nn]) for (t0, nn) in TILES_ALL] for j in range(4)]
            adaln(L, n, tiles, None, pre=(sq, sil, rs, rstd))
            for grp in range(11):
                for half in range(2):
                    pg = WS.next()
                    pu = WS.next()
                    for jj in range(2):
                        jl = half * 2 + jj
                        for ti in tiles:
                            t0, nn = TILES_ALL[ti]
                            psg = P.ps()
                            psu = P.ps()
                            for kc in range(KC):
                                P.mm(psg[:, :nn], pg[:, kc, jj * 128:(jj + 1) * 128], HN[kc][ti],
                                     start=(kc == 0), stop=(kc == KC - 1))
                            for kc in range(KC):
                                P.mm(psu[:, :nn], pu[:, kc, jj * 128:(jj + 1) * 128], HN[kc][ti],
                                     start=(kc == 0), stop=(kc == KC - 1))
                            s = sil.next()
                            P.act(s[:, :nn], psg[:, :nn], AF.Silu)
                            P.tt(ACTG[jl][ti], s[:, :nn], psu[:, :nn], ALU.mult)
                for ob in range(4):
                    pw = WS.next()
                    for o4 in range(4):
                        oc = ob * 4 + o4
                        for ti in tiles:
                            t0, nn = TILES_ALL[ti]
                            ps = P.ps()
                            for j in range(4):
                                P.mm(ps[:, :nn], pw[:, j, o4 * 128:(o4 + 1) * 128], ACTG[j][ti],
                                     start=(j == 0), stop=(j == 3))
                            P.stt(X[oc][ti], ps[:, :nn], Gmod[L][n][wq(ti)][:, oc:oc + 1], X[oc][ti],
                                  ALU.mult, ALU.add)

        def inproj(L, g):
            gt = [0, 1, 2] if g == 0 else [1, 2]
            sbm = Bump(ar, S_OFF, 197 * KB)
            adaln(L, 1, gt, sbm)
            P.dma(XS[g], Xall)
            fm = [("mq", MQ), ("mk", MK), ("mo", SO), ("nq", NQ), ("nk", NK), ("lx", LX), ("lg", GLG)]
            tm = [("mv", MV), ("nv", NV)]
            specs = []
            for name, _ in fm + tm:
                for pnl in range(4):
                    c0 = OFF[name] + pnl * 256
                    specs.append((w_in_d[L, :, c0:c0 + 256], KC, 256, True))
            WS.begin(specs)
            P.barrier()
            P.dma(HS[g], V(HNt[:], [b for row in HN for v in row for b in v.bufs]))
            sbm = Bump(ar, X_OFF, 80 * KB)
            ropet = sbm.v([128, 4, GL], F32)
            P.dma(ropet, rope_d[g])
            perm = sbm.v([128, 128], F32)
            P.dma(perm, perm_d)
            stg32 = Rot(sbm.vs(2, [128, 512], F32))
            stgb = Rot(sbm.vs(3, [128, 512], BF16))
            t1r = Rot(sbm.vs(2, [128, 512], F32))
            t2r = Rot(sbm.vs(2, [128, 512], F32))
            wg32 = sbm.v([128, KC, 16], F32)
            wgb = sbm.v([128, KC, 16], BF16)
            g4 = Rot(sbm.vs(2, [4, 512], F32))
            P.dma(wg32, w_in_d[L, :, OFF["mg"]:OFF["mg"] + 16].re("(k p) c -> p k c", p=128))
            P.copy(wgb, wg32, eng="dve")
            for dg in range(4):
                for ti in gt:
                    t0, nn = TILES_ALL[ti]
                    ps = P.ps()
                    for kc in range(KC):
                        P.mm(ps[0:4, :nn], wgb[:, kc, dg * 4:(dg + 1) * 4], HN[kc][ti],
                             start=(kc == 0), stop=(kc == KC - 1))
                    s = g4.next()
                    P.copy(s[:, :nn], ps[0:4, :nn], eng="dve")
                    P.dma(MG[dg, :, scol(g, t0):scol(g, t0) + nn], s[:, :nn])
            for name, dst in fm:
                for pnl in range(4):
                    pan = WS.next()
                    for o2 in range(2):
                        ch = pnl * 2 + o2
                        for ti in gt:
                            t0, nn = TILES_ALL[ti]
                            ps = P.ps()
                            for kc in range(KC):
                                P.mm(ps[:, :nn], pan[:, kc, o2 * 128:(o2 + 1) * 128], HN[kc][ti],
                                     start=(kc == 0), stop=(kc == KC - 1))
                            dd = dst[ch, :, scol(g, t0):scol(g, t0) + nn]
                            if name in ("mq", "mk"):
                                sc = 1.0 if name == "mq" else 1.0 / 16.0
                                if ti == 0:
                                    o = stgb.next()
                                    P.act(o[:, :nn], ps[:, :nn], AF.Identity, scale=sc)
                                    P.dma(dd, o[:, :nn])
                                else:
                                    qs = stg32.next()
                                    P.act(qs[:, :nn], ps[:, :nn], AF.Identity, scale=sc)
                                    ps2 = P.ps()
                                    P.mm(ps2[:, :nn], perm, qs[:, :nn])
                                    tb = 0 if ch % 2 == 0 else 2
                                    l0 = t0 - CTX
                                    a1 = t1r.next()
                                    a2 = t2r.next()
                                    P.tt(a1[:, :nn], qs[:, :nn], ropet[:, tb, l0:l0 + nn], ALU.mult, eng="pool")
                                    P.tt(a2[:, :nn], ps2[:, :nn], ropet[:, tb + 1, l0:l0 + nn], ALU.mult)
                                    o = stgb.next()
                                    P.tt(o[:, :nn], a1[:, :nn], a2[:, :nn], ALU.add, eng="pool")
                                    P.dma(dd, o[:, :nn])
                            elif name == "mo":
                                o = stgb.next()
                                P.act(o[:, :nn], ps[:, :nn], AF.Sigmoid)
                                P.dma(dd, o[:, :nn])
                            elif name in ("nq", "nk"):
                                o = stgb.next()
                                P.act(o[:, :nn], ps[:, :nn], AF.Identity)
                                P.dma(dd, o[:, :nn])
                            elif name == "lx":
                                o = stg32.next()
                                P.act(o[:, :nn], ps[:, :nn], AF.Identity)
                                P.dma(dd, o[:, :nn])
                            else:
                                a1 = t1r.next()
                                a2 = t2r.next()
                                P.act(a1[:, :nn], ps[:, :nn], AF.Square)
                                P.ts(a1[:, :nn], a1[:, :nn], 0.044715, ALU.mult, 1.0, ALU.add, eng="pool")
                                P.tt(a2[:, :nn], a1[:, :nn], ps[:, :nn], ALU.mult)
                                P.act(a2[:, :nn], a2[:, :nn], AF.Sigmoid, scale=1.5957691216057308)
                                o = stgb.next()
                                P.tt(o[:, :nn], a2[:, :nn], ps[:, :nn], ALU.mult)
                                P.dma(dd, o[:, :nn])
            for name, dst in tm:
                for pnl in range(4):
                    pan = WS.next()
                    for i in range(0 if g == 0 else 2, T // 128):
                        ti = 0 if i < 2 else (1 if i < 6 else 2)
                        t0, nn = TILES_ALL[ti]
                        o0 = i * 128 - t0
                        ps = P.ps()
                        for kc in range(KC):
                            P.mm(ps[:, 0:256], HN[kc][ti][:, o0:o0 + 128], pan[:, kc, :],
                                 start=(kc == 0), stop=(kc == KC - 1))
                        o = stgb.next()
                        P.act(o[:, 0:256], ps[:, 0:256], AF.Identity)
                        r0 = scol(g, i * 128)
                        P.dma(dst[r0:r0 + 128, pnl * 256:(pnl + 1) * 256], o[:, 0:256])

        HNall = V(HNt[:], [b for row in HN for v in row for b in v.bufs])
        Ysb_t = [ar.raw([128, 8, T], BF16, i * 20 * KB) for i in range(3)]
        Ysb = [[P.new(Ysb_t[n][:, c, :]) for c in range(8)] for n in range(3)]
        MX_HI = 197 * KB

        def qtiles_seq(L):
            q = [(0, CTX)] if L == 0 else []
            return q + [(CTX + 512 * k, 512) for k in range(4)]

        def mlstm(L):
            top = Bump(ar, 0, MX_HI)
            NBM = top.v([64, SEQ], F32)
            NM = top.v([64, SEQ], F32)
            bT = top.v([128, 2, NKT * 4], F32)
            keep = top.top
            bm = Bump(ar, keep, MX_HI)
            GI = bm.v([64, SEQ], F32)
            GF = bm.v([64, SEQ], F32)
            Ft = bm.v([64, SEQ], F32)
            Bt = bm.v([64, SEQ], F32)
            ON = bm.v([64, SEQ], F32)
            bi = bm.v([64, 1], F32)
            bf = bm.v([64, 1], F32)
            id4 = bm.v([64, 4], F32)
            P.dma(bi, mbi_d[L])
            P.dma(bf, mbf_d[L])
            P.dma(id4, id4_d)
            P.memset(ON, 1.0)
            P.ts(bf, bf, -1.0, ALU.mult)
            for d in range(2):
                r = slice(32 * d, 32 * d + 4)
                P.dma(GI[r, :], MG[2 * d])
                P.dma(GF[r, :], MG[2 * d + 1])
            for d in range(2):
                r = slice(32 * d, 32 * d + 4)
                P.ts(GI[r], GI[r], bi[r], ALU.add)
                P.act(GF[r], GF[r], AF.Exp, bias=bf[r], scale=-1.0)
                P.act(GF[r], GF[r], AF.Ln, bias=1.0)
                if d == 0:
                    segs = [(slice(0, SEQ), False, None)]
                else:
                    segs = [(slice(0, CTX), True, None), (slice(CTX, SEQ), True, 0)]
                for (sl, rev, ini) in segs:
                    def o(v):
                        w = v[r, sl]
                        return w[:, ::-1] if rev else w
                    i0 = 0.0 if ini is None else Ft[r, ini:ini + 1]
                    P.scan(o(Ft), o(ON), o(GF), i0, ALU.mult, ALU.subtract)
                P.tt(Bt[r], GI[r], Ft[r], ALU.subtract)
                for (sl, rev, ini) in segs:
                    def o(v):
                        w = v[r, sl]
                        return w[:, ::-1] if rev else w
                    i0 = -1e30 if ini is None else GI[r, ini:ini + 1]
                    P.scan(o(GI), o(Bt), o(Bt), i0, ALU.max, ALU.max)
                P.ts(NBM[r], GI[r], -1.0, ALU.mult)
                P.tt(Ft[r], Ft[r], GI[r], ALU.add)
                P.ts(NM[r], Ft[r], -1.0, ALU.mult)
                ps = P.ps()
                for kt in range(NKT):
                    P.mm(ps[:, kt * 4:(kt + 1) * 4], Bt[r, kt * 128:(kt + 1) * 128], id4[r, :])
                P.copy(bT[:, d, :], ps[:, 0:NKT * 4])
            P.barrier()
            bm = Bump(ar, keep, MX_HI)
            masks = bm.v([128, 8, 512], F32)
            sel = bm.v([64, 4, 128], F32)
            P.dma(masks, masks_d)
            P.dma(sel, sel_d)
            KT = bm.vs(2, [128, SEQ], BF16)
            VH = bm.v([128, NKT, 256], BF16)
            QT = bm.vs(2, [128, SEQ], BF16)
            NBt = bm.vs(2, [128, 512], F32)
            EMt = bm.vs(2, [128, 512], F32)
            Dt = Rot(bm.vs(4, [128, 512], F32))
            PT = Rot(bm.vs(4, [128, 512], BF16))
            HSr = Rot([bm.vs(2, [128, 512], F32) for _ in range(2)])
            tmp = Rot(bm.vs(3, [128, 512], F32))
            dntr = Rot(bm.vs(2, [128, 512], F32))
            recr = Rot(bm.vs(2, [128, 512], F32))
            sot = Rot(bm.vs(2, [128, 512], BF16))
            yo = Rot(bm.vs(3, [128, 512], BF16))
            gn = bm.v([128, 8], F32)
            P.dma(gn, mgn_d[L])
            accsets = Rot([P.psl[0:3], P.psl[3:6]])
            sbank = Rot(P.psl[6:8])
            for h in range(4):
                for c in range(2):
                    P.dma(KT[c], MK[2 * h + c])
                    P.dma(QT[c], MQ[2 * h + c])
                P.dma(VH, MV[:, h * 256:(h + 1) * 256].re("(k p) c -> p k c", p=128))
                for (s0, nn) in qtiles_seq(L):
                    for d in range(2):
                        r = slice(32 * d, 32 * d + 4)
                        ps = sbank.next()
                        P.mm(ps[:, :nn], sel[r, h, :], NBM[r, s0:s0 + nn])
                        P.copy(NBt[d][:, :nn], ps[:, :nn], eng="act")
                        ps = sbank.next()
                        P.mm(ps[:, :nn], sel[r, h, :], NM[r, s0:s0 + nn])
                        P.act(EMt[d][:, :nn], ps[:, :nn], AF.Exp)
                    HS = HSr.next()

                    def smm(kt):
                        sb_ = sbank.next()
                        for c in range(2):
                            P.mm(sb_[:, :nn], KT[c][:, kt * 128:(kt + 1) * 128], QT[c][:, s0:s0 + nn],
                                 start=(c == 0), stop=(c == 1))
                        return sb_

                    for d in range(2):
                        sched = []
                        if s0 == 0:
                            sched = [(kt, 4 * d + kt) for kt in range(2)]
                        else:
                            ql = s0 - CTX
                            sched = [(0, None), (1, None)]
                            for j in range(16):
                                o = j - ql // 128
                                if d == 0 and o <= 3:
                                    sched.append((2 + j, o if o >= 0 else None))
                                if d == 1 and o >= 0:
                                    sched.append((2 + j, 4 + o if o <= 3 else None))
                        acc = accsets.next()
                        s_cur = smm(sched[0][0])
                        for i, (kt, mi) in enumerate(sched):
                            s_next = smm(sched[i + 1][0]) if i + 1 < len(sched) else None
                            Dv = Dt.next()
                            P.act(Dv[:, :nn], NBt[d][:, :nn], AF.Exp, bias=bT[:, d, kt * 4 + h:kt * 4 + h + 1])
                            if mi is not None:
                                P.tt(Dv[:, :nn], Dv[:, :nn], masks[:, mi, :nn], ALU.mult, eng="pool")
                            Pv = PT.next()
                            P.tt(Pv[:, :nn], s_cur[:, :nn], Dv[:, :nn], ALU.mult)
                            first, last = i == 0, i == len(sched) - 1
                            P.mm(acc[0][:, :nn], VH[:, kt, 0:128], Pv[:, :nn], start=first, stop=last)
                            P.mm(acc[1][:, :nn], VH[:, kt, 128:256], Pv[:, :nn], start=first, stop=last)
                            P.mm(acc[2][:, :nn], onesb, Pv[:, :nn], start=first, stop=last)
                            s_cur = s_next
                        dnt = dntr.next()
                        rec = recr.next()
                        P.act(dnt[:, :nn], acc[2][:, :nn], AF.Abs)
                        P.tt(dnt[:, :nn], dnt[:, :nn], EMt[d][:, :nn], ALU.max)
                        P.act(dnt[:, :nn], dnt[:, :nn], AF.Ln)
                        P.act(rec[:, :nn], dnt[:, :nn], AF.Exp, scale=-1.0)
                        for c in range(2):
                            if d == 0:
                                P.tt(HS[c][:, :nn], acc[c][:, :nn], rec[:, :nn], ALU.mult)
                            else:
                                tm_ = tmp.next()
                                P.tt(tm_[:, :nn], acc[c][:, :nn], rec[:, :nn], ALU.mult)
                                P.tt(HS[c][:, :nn], HS[c][:, :nn], tm_[:, :nn], ALU.add, eng="pool")
                    ps = sbank.next()
                    for c in range(2):
                        tm_ = tmp.next()
                        P.act(tm_[:, :nn], HS[c][:, :nn], AF.Square)
                        P.mm(ps[:, :nn], ones32, tm_[:, :nn], start=(c == 0), stop=(c == 1))
                    dnt = dntr.next()
                    rec = recr.next()
                    P.act(dnt[:, :nn], ps[:, :nn], AF.Ln, bias=epsc, scale=1.0 / 256)
                    P.act(rec[:, :nn], dnt[:, :nn], AF.Exp, scale=-0.5)
                    for c in range(2):
                        ch = 2 * h + c
                        tm_ = tmp.next()
                        so_ = sot.next()
                        P.dma(so_[:, :nn], SO[ch, :, s0:s0 + nn])
                        P.tt(tm_[:, :nn], HS[c][:, :nn], rec[:, :nn], ALU.mult)
                        y_ = yo.next()
                        P.stt(y_[:, :nn], tm_[:, :nn], gn[:, ch:ch + 1], so_[:, :nn], ALU.mult, ALU.mult)
                        P.dma(YA[0, ch, :, s0:s0 + nn], y_[:, :nn])

        KTW = {0: range(0, 6), 1: range(2, 10), 2: range(6, 14), 3: range(10, 16)}

        def na(L, jobs=()):
            jobs = list(jobs)
            SC = 128.0 ** -0.5
            bm = Bump(ar, 0, 150 * KB)
            colm = bm.v([128, 960], F32)
            Mh = bm.v([128, 960], F32)
            P.dma(colm, colm_d)
            pairs = [(qg, ktl) for qg in range(4) for ktl in KTW[qg]]
            TP = {pr: bm.v([128, 512], F32) for pr in pairs}
            MhR = Rot([Mh, bm.v([128, 960], F32)])
            NQr = Rot(bm.vs(2, [128, SEQ], BF16))
            NKr = Rot(bm.vs(2, [128, SEQ], BF16))
            NV4r = Rot(bm.vs(2, [128, NKT, 512], BF16))
            PT = Rot(bm.vs(4, [128, 512], BF16))
            lnr = Rot(bm.vs(2, [128, 512], F32))
            recr = Rot(bm.vs(2, [128, 512], F32))
            yo = Rot(bm.vs(3, [128, 512], BF16))
            for pr in pairs:
                P.memset(TP[pr], NEG)
            valid = {}
            for (qg, ktl) in pairs:
                for half in range(2):
                    j = 2 * ktl + half
                    rows = [r for r in range(8 * qg, 8 * qg + 8)
                            if min(max(r - 4, 0), 24) <= j < min(max(r - 4, 0), 24) + 8]
                    if rows:
                        valid[(qg, ktl, half)] = ((rows[0] - 8 * qg) * 64, len(rows) * 64, (rows[0] - j + 7) * 64)
            accsets = Rot([(P.psl[0], P.psl[1]), (P.psl[2], P.psl[3])])
            sbank = Rot(P.psl[4:7])
            loaded = {}

            def prep(h):
                if h >= 8:
                    return
                if h % 4 == 0:
                    nv = NV4r.next()
                    P.dma(nv, NV[:, (h // 4) * 512:(h // 4 + 1) * 512].re("(k p) c -> p k c", p=128))
                    loaded["nv"] = nv
                nq, nk, mh = NQr.next(), NKr.next(), MhR.next()
                P.dma(nq, NQ[h])
                P.dma(nk, NK[h])
                P.dma(mh[0:64, :], rpbe_d[L, h])
                P.dma(mh[64:128, :], rpbe_d[L, h])
                P.tt(mh, mh, colm, ALU.add, eng="pool")
                loaded[h] = (nq, nk, loaded["nv"], mh)

            prep(0)
            for h in range(8):
                prep(h + 1)
                NQh, NKh, NV4, mh = loaded.pop(h)
                for (s0, nn) in qtiles_seq(L):
                    keys = [(kt, None) for kt in range(2)]
                    if s0 > 0:
                        qg = (s0 - CTX) // 512
                        keys += [(2 + ktl, (qg, ktl)) for ktl in KTW[qg]]
                    pn, pd = accsets.next()
                    if jobs:
                        jobs.pop(0)(P.psl[7])

                    def smm(kt):
                        sb_ = sbank.next()
                        P.mm(sb_[:, :nn], NKh[:, kt * 128:(kt + 1) * 128], NQh[:, s0:s0 + nn])
                        return sb_

                    s_cur = smm(keys[0][0])
                    for idx, (kt, pr) in enumerate(keys):
                        s_next = smm(keys[idx + 1][0]) if idx + 1 < len(keys) else None
                        Pv = PT.next()
                        if pr is None:
                            P.act(Pv[:, :nn], s_cur[:, :nn], AF.Exp, scale=SC)
                        else:
                            for half in range(2):
                                vv = valid.get((pr[0], pr[1], half))
                                if vv is None:
                                    continue
                                a_, ln, d0 = vv
                                hp = slice(half * 64, half * 64 + 64)
                                P.stt(TP[pr][hp, a_:a_ + ln], s_cur[hp, a_:a_ + ln], SC, mh[hp, d0:d0 + ln],
                                      ALU.mult, ALU.add)
                            P.act(Pv[:, :nn], TP[pr][:, :nn], AF.Exp)
                        first, last = idx == 0, idx == len(keys) - 1
                        hh = h % 4
                        P.mm(pn[:, :nn], NV4[:, kt, hh * 128:(hh + 1) * 128], Pv[:, :nn], start=first, stop=last)
                        P.mm(pd[:, :nn], onesb, Pv[:, :nn], start=first, stop=last)
                        s_cur = s_next
                    ln_, rec = lnr.next(), recr.next()
                    P.act(ln_[:, :nn], pd[:, :nn], AF.Ln)
                    P.act(rec[:, :nn], ln_[:, :nn], AF.Exp, scale=-1.0)
                    y_ = yo.next()
                    P.tt(y_[:, :nn], pn[:, :nn], rec[:, :nn], ALU.mult)
                    P.dma(YA[1, h, :, s0:s0 + nn], y_[:, :nn])
            while jobs:
                jobs.pop(0)(P.psl[7])

        def lru(L, jobs=()):
            jobs = list(jobs)
            per = (len(jobs) + 15) // 16
            bm = Bump(ar, 0, 140 * KB)
            LXP = Rot(bm.vs(2, [128, SEQ + 6], F32))
            XCr = Rot(bm.vs(2, [128, SEQ], F32))
            XCbr = Rot(bm.vs(2, [128, SEQ], BF16))
            Rr = bm.vs(2, [128, SEQ], F32)
            Ii = bm.vs(2, [128, SEQ], F32)
            Tm = bm.vs(2, [128, SEQ], F32)
            H0 = bm.v([128, SEQ], F32)
            glg = Rot(bm.vs(2, [128, SEQ], BF16))
            yl = Rot(bm.vs(2, [128, SEQ], BF16))
            W32 = Rot(bm.vs(2, [128, 4, 128], F32))
            Wb = Rot(bm.vs(2, [128, 4, 128], BF16))
            cw = bm.v([128, 8, 4], F32)
            cbias = bm.v([128, 8], F32)
            ba = bm.v([128, 2, 8], F32)
            bx = bm.v([128, 2, 8], F32)
            lam = bm.v([128, 2, 8], F32)
            m8 = bm.v([128, 2, 8], F32)
            m16 = bm.v([128, 2, 8], F32)
            P.dma(cw, lcw_d[L])
            P.dma(cbias, lcb_d[L])
            P.dma(ba, lba_d[L])
            P.dma(bx, lbx_d[L])
            P.dma(lam, llam_d[L])
            P.act(lam, lam, AF.Exp, scale=-1.0)
            P.act(lam, lam, AF.Ln, bias=1.0)
            P.ts(m8, lam, -8.0, ALU.mult)
            P.ts(m16, lam, -16.0, ALU.mult)
            for lx_ in LXP.items:
                P.memset(lx_, 0.0)
            SEGS = [(1, CTX, 0), (CTX + 4, LAT, CTX)]
            STILES = [(0, 256), (256, 512), (768, 512), (1280, 512), (1792, 512)]
            staged = {}

            def stage_a(c):
                if c >= 8:
                    return
                lxp = LXP.next()
                P.dma(lxp[:, 1:1 + CTX], LX[c, :, 0:CTX])
                P.dma(lxp[:, CTX + 4:CTX + 4 + LAT], LX[c, :, CTX:SEQ])
                gl_ = glg.next()
                P.dma(gl_, GLG[c])
                XC, XCb = XCr.next(), XCbr.next()
                for (base, n, o) in SEGS:
                    P.ts(XC[:, o:o + n], lxp[:, base - 1:base - 1 + n], cw[:, c, 0:1], ALU.mult,
                         cbias[:, c:c + 1], ALU.add)
                    for j in range(1, 4):
                        P.stt(XC[:, o:o + n], lxp[:, base - 1 + j:base - 1 + j + n], cw[:, c, j:j + 1],
                              XC[:, o:o + n], ALU.mult, ALU.add)
                P.copy(XCb, XC, eng="act")
                w32 = W32.next()
                wb = Wb.next()
                for n in range(2):
                    P.dma(w32[:, n, :], lwa_d[L, n, c])
                    P.dma(w32[:, 2 + n, :], lwx_d[L, n, c])
                P.copy(wb, w32, eng="pool")
                staged[c] = (XC, XCb, wb, gl_)

            stage_a(0)
            for c in range(8):
                XC, XCb, wb, gl_ = staged.pop(c)
                for n in range(2):
                    for _ in range(per):
                        if jobs:
                            jobs.pop(0)(None)
                    for (s0, sn) in STILES:
                        psa = P.ps()
                        P.mm(psa[:, :sn], wb[:, n, :], XCb[:, s0:s0 + sn])
                        P.act(Rr[n][:, s0:s0 + sn], psa[:, :sn], AF.Sigmoid, bias=ba[:, n, c:c + 1])
                        psx = P.ps()
                        P.mm(psx[:, :sn], wb[:, 2 + n, :], XCb[:, s0:s0 + sn])
                        P.act(Ii[n][:, s0:s0 + sn], psx[:, :sn], AF.Sigmoid, bias=bx[:, n, c:c + 1])
                stage_a(c + 1)
                for n in range(2):
                    P.act(Tm[n], Rr[n], AF.Exp, scale=m16[:, n, c:c + 1])
                    P.act(Rr[n], Rr[n], AF.Exp, scale=m8[:, n, c:c + 1])
                for n in range(2):
                    P.ts(Tm[n], Tm[n], -1.0, ALU.mult, 1.0, ALU.add, eng="pool")
                for n in range(2):
                    P.act(Tm[n], Tm[n], AF.Sqrt)
                for n in range(2):
                    P.tt(Ii[n], Ii[n], Tm[n], ALU.mult)
                for n in range(2):
                    P.tt(Ii[n], Ii[n], XC, ALU.mult, eng="pool")
                P.scan(H0, Rr[0], Ii[0], 0.0, ALU.mult, ALU.add)
                P.scan(Tm[1][:, 0:CTX][:, ::-1], Rr[1][:, 0:CTX][:, ::-1], Ii[1][:, 0:CTX][:, ::-1], 0.0,
                       ALU.mult, ALU.add)
                P.scan(Tm[1][:, CTX:SEQ][:, ::-1], Rr[1][:, CTX:SEQ][:, ::-1], Ii[1][:, CTX:SEQ][:, ::-1],
                       Tm[1][:, 0:1], ALU.mult, ALU.add)
                P.tt(H0, H0, Tm[1], ALU.add, eng="pool")
                y_ = yl.next()
                P.tt(y_, H0, gl_, ALU.mult)
                P.dma(YA[2, c], y_)
            while jobs:
                jobs.pop(0)(None)

        def mixers(L):
            P.barrier()
            mlstm(L)
            P.barrier()
            jobs = []
            if (not mix_test) and L + 1 < n_layers:
                mod_begin(L + 1, WS2)
                jobs = [(lambda bank, j=j: mod_panel(L + 1, j, WS2, bank)) for j in range(72)]
            na(L, jobs[:36])
            P.barrier()
            lru(L, jobs[36:])
            if jobs:
                mod_finish(L + 1)
            P.barrier()

        def merge(L, g, qtiles):
            sbm = Bump(ar, S_OFF, 197 * KB)
            sgr = Rot(sbm.vs(2, [128, 512], F32))
            mo = Rot(sbm.vs(3, [128, 512], BF16))
            acc_t = sbm.raw([128, 2, T], F32)
            ACC = [[P.new(acc_t[:, o2, t0:t0 + nn]) for (t0, nn) in TILES_ALL] for o2 in range(2)]
            for ocp in range(8):
                for n in range(3):
                    pg = WS.next()
                    pb = WS.next()
                    for o2 in range(2):
                        oc = ocp * 2 + o2
                        for ti in qtiles:
                            t0, nn = TILES_ALL[ti]
                            psg = P.ps()
                            psb = P.ps()
                            for kc in range(KC):
                                P.mm(psg[:, :nn], pg[:, kc, o2 * 128:(o2 + 1) * 128], HN[kc][ti],
                                     start=(kc == 0), stop=(kc == KC - 1))
                            for kc in range(8):
                                P.mm(psb[:, :nn], pb[:, kc, o2 * 128:(o2 + 1) * 128], Ysb[n][kc][:, t0:t0 + nn],
                                     start=(kc == 0), stop=(kc == 7))
                            sg = sgr.next()
                            P.act(sg[:, :nn], psg[:, :nn], AF.Sigmoid)
                            if n == 0:
                                P.tt(ACC[o2][ti], sg[:, :nn], psb[:, :nn], ALU.mult)
                            else:
                                P.tt(sg[:, :nn], sg[:, :nn], psb[:, :nn], ALU.mult)
                                if n == 1:
                                    P.tt(ACC[o2][ti], ACC[o2][ti], sg[:, :nn], ALU.add, eng="pool")
                                else:
                                    o = mo.next()
                                    P.tt(o[:, :nn], ACC[o2][ti], sg[:, :nn], ALU.add, eng="pool")
                                    P.dma(MRG[g][oc, :, t0:t0 + nn], o[:, :nn])

        def final_out(g):
            sbm = Bump(ar, S_OFF, 197 * KB)
            sq = Rot(sbm.vs(3, [128, 512], BF16))
            og = Rot(sbm.vs(3, [128, 512], F32))
            rs = sbm.v([128, 512], F32)
            rstd = sbm.v([128, 512], F32)
            for ti in (1, 2):
                t0, nn = TILES_ALL[ti]
                ps = P.ps()
                for kc in range(KC):
                    s = sq.next()
                    P.act(s[:, :nn], X[kc][ti], AF.Square)
                    P.mm(ps[:, :nn], onesb, s[:, :nn], start=(kc == 0), stop=(kc == KC - 1))
                P.act(rs[:, :nn], ps[:, :nn], AF.Ln, bias=epsc, scale=1.0 / D)
                P.act(rstd[:, :nn], rs[:, :nn], AF.Exp, scale=-0.5)
                for kc in range(KC):
                    o = og.next()
                    P.stt(o[:, :nn], X[kc][ti], gfin[:, kc:kc + 1], rstd[:, :nn], ALU.mult, ALU.mult)
                    P.dma(outT_d[g][kc * 128:(kc + 1) * 128, t0 - CTX:t0 - CTX + nn], o[:, :nn])

        def load_X(g, tiles):
            for ti in tiles:
                t0, nn = TILES_ALL[ti]
                xv = V(Xt[:, :, t0:t0 + nn], [X[kc][ti].bufs[0] for kc in range(KC)])
                P.dma(xv, XS[g][:, :, t0:t0 + nn])

        def stageC(L, g):
            last = (L == 1)
            qtiles = [0, 1, 2] if (g == 0 and not last) else [1, 2]
            P.barrier()
            for ti in qtiles:
                t0, nn = TILES_ALL[ti]
                hv = V(HNt[:, :, t0:t0 + nn], [HN[kc][ti].bufs[0] for kc in range(KC)])
                P.dma(hv, HS[g][:, :, t0:t0 + nn])
            specs = []
            for ocp in range(8):
                for n in range(3):
                    c0 = OFF["gx"] + n * D + ocp * 256
                    specs.append((w_in_d[L, :, c0:c0 + 256], KC, 256, True))
                    specs.append((w_br_d[L, n, :, ocp * 256:(ocp + 1) * 256], 8, 256, True))
            WS.begin(specs)
            for n in range(3):
                yv = V(Ysb_t[n][:], [b for v in Ysb[n] for b in v.bufs])
                P.dma(yv[:, :, CTX:T], YA[n, :, :, CTX + g * GL:CTX + (g + 1) * GL].re("c p t -> p c t"))
                if 0 in qtiles:
                    P.dma(yv[:, :, 0:CTX], YA[n, :, :, 0:CTX].re("c p t -> p c t"))
            merge(L, g, qtiles)
            WS.begin([(w_out_d[L, :, j * 256:(j + 1) * 256], KC, 256, True) for j in range(8)])
            P.barrier()
            for ti in qtiles:
                t0, nn = TILES_ALL[ti]
                load_X(g, [ti])
                hv = V(HNt[:, :, t0:t0 + nn], [HN[kc][ti].bufs[0] for kc in range(KC)])
                P.dma(hv, MRG[g][:, :, t0:t0 + nn].re("k p t -> p k t"))
            for j in range(8):
                pan = WS.next()
                for o2 in range(2):
                    oc = 2 * j + o2
                    for ti in qtiles:
                        t0, nn = TILES_ALL[ti]
                        ps = P.ps()
                        for kc in range(KC):
                            P.mm(ps[:, :nn], pan[:, kc, o2 * 128:(o2 + 1) * 128], HN[kc][ti],
                                 start=(kc == 0), stop=(kc == KC - 1))
                        P.stt(X[oc][ti], ps[:, :nn], Gmod[L][1][wq(ti)][:, oc:oc + 1], X[oc][ti],
                              ALU.mult, ALU.add)
            P.barrier()
            ffn(L, 2, qtiles, f2in_d, f2out_d)
            if last:
                P.barrier()
                final_out(g)
            else:
                P.dma(XS[g], Xall)
            P.barrier()

        if mix_test:
            mixers(0)
            dump("YA", YA)
        for L in range(0 if mix_test else n_layers):
            for g in groups:
                gt = [0, 1, 2] if g == 0 else [1, 2]
                P.barrier()
                if L == 0:
                    P.dma(Xlat, xT_d[g].re("(k p) t -> p k t", p=128))
                    if g == 0:
                        P.dma(Xctx, ctxT_d.re("(k p) t -> p k t", p=128))
                else:
                    load_X(g, gt)
                ffn(L, 0, gt, f1in_d, f1out_d)
                P.barrier()
                inproj(L, g)
                P.barrier()
            mixers(L)
            for g in groups:
                stageC(L, g)
        P.finalize(st)
    return nc


def _fm(v):
    v = np.asarray(v, np.float32)
    n = v.shape[-1] // 128
    return np.ascontiguousarray(np.swapaxes(v.reshape(v.shape[:-1] + (n, 128)), -1, -2))


def host_consts():
    c = {}
    perm = np.zeros((128, 128), np.float32)
    for m in range(128):
        perm[(m + 64) % 128, m] = 1.0
    c["perm"] = perm
    s = np.arange(128)[:, None]
    t = np.arange(512)[None, :]
    masks = np.zeros((128, 8, 512), np.float32)
    for o in range(4):
        masks[:, o, :] = (o * 128 + s <= t)
        masks[:, 4 + o, :] = (o * 128 + s >= t)
    c["masks"] = masks
    inv = 10000.0 ** (-np.arange(64, dtype=np.float64) / 64)
    invp = np.concatenate([inv, inv])
    sgn = np.concatenate([-np.ones(64), np.ones(64)])
    pos = np.arange(LAT)
    rope = np.zeros((2, 128, 4, GL), np.float32)
    for g in range(2):
        pg = pos[g * GL:(g + 1) * GL]
        for k, pp in enumerate((pg // 64, pg % 64)):
            ang = (pp[None, :].astype(np.float32) * invp[:, None].astype(np.float32)).astype(np.float64)
            rope[g, :, 2 * k, :] = np.cos(ang)
            rope[g, :, 2 * k + 1, :] = np.sin(ang) * sgn[:, None]
    c["rope"] = rope
    sel = np.zeros((64, 4, 128), np.float32)
    id4 = np.zeros((64, 4), np.float32)
    for base in (0, 32):
        for h in range(4):
            sel[base + h, h, :] = 1.0
            id4[base + h, h] = 1.0
    c["sel"] = sel
    c["id4"] = id4
    kc = np.arange(64)[:, None]
    qc = np.arange(64)[None, :]
    cs = np.clip(qc - 8, 0, 48)
    ok = (kc >= cs) & (kc < cs + 16)
    cm = np.where(ok, 0.0, NEG).astype(np.float32)
    cm = np.broadcast_to(cm[:, None, :], (64, 15, 64)).reshape(64, 15 * 64)
    c["colm"] = np.ascontiguousarray(np.concatenate([cm, cm], 0))
    return c


def prep_inputs(inp, n_cores=4):
    f = lambda k: np.asarray(inp[k], np.float32)
    shared = host_consts()
    for k in ("w_mod", "ffn1_w_in", "ffn1_w_out", "ffn2_w_in", "ffn2_w_out", "w_in", "w_branch", "w_out"):
        shared[k] = np.ascontiguousarray(f(k))
    shared["bmod"] = _fm(f("b_mod"))
    shared["gains"] = np.ascontiguousarray(np.stack([_fm(f("norm_ffn1")), _fm(f("norm_mix")), _fm(f("norm_ffn2"))], 1))
    shared["gfin"] = _fm(f("norm_final"))
    mb = np.zeros((2, 2, 64, 1), np.float32)
    for wi, k in enumerate(("mlstm_b_i", "mlstm_b_f")):
        v = f(k)
        for d in range(2):
            mb[wi, :, 32 * d:32 * d + 4, 0] = v[:, d, :]
    shared["mbi"] = mb[0]
    shared["mbf"] = mb[1]
    shared["mgn"] = _fm(f("mlstm_gn"))
    rpb = f("na_rpb")
    kc = np.arange(64)[:, None]
    qc = np.arange(64)[None, :]
    dc = np.clip(kc - qc, -15, 15) + 15
    e = rpb[:, :, ::-1, :][:, :, :, dc]
    shared["rpbe"] = np.ascontiguousarray(np.transpose(e, (0, 1, 3, 2, 4)).reshape(2, 8, 64, 15 * 64))
    shared["lcw"] = np.ascontiguousarray(np.transpose(_fm(f("lru_conv_w")), (0, 2, 3, 1)))
    shared["lcb"] = _fm(f("lru_conv_b"))
    shared["lwa"] = np.ascontiguousarray(f("lru_w_a"))
    shared["lwx"] = np.ascontiguousarray(f("lru_w_x"))
    for k, src in (("lba", "lru_b_a"), ("lbx", "lru_b_x"), ("llam", "lru_lambda")):
        shared[k] = np.ascontiguousarray(np.transpose(_fm(f(src)), (0, 2, 1, 3)))
    x, c, ctx, c_ctx = f("x"), f("c"), f("ctx"), f("c_ctx")
    maps = []
    for core in range(n_cores):
        b = core
        m = dict(shared)
        xt = np.ascontiguousarray(x[b].T)
        m["xT"] = np.ascontiguousarray(np.stack([xt[:, 0:GL], xt[:, GL:2 * GL]], 0))
        m["ctxT"] = np.ascontiguousarray(ctx[b].T)
        m["ccol"] = np.ascontiguousarray(np.stack([_fm(c[b]), _fm(c_ctx)], -1))
        maps.append(m)
    return maps


_NC_CACHE = {}


ACTIVE = (0, 1, 4, 5)


def kernel(**inputs):
    n_cores = 8
    if "nc" not in _NC_CACHE:
        _NC_CACHE["nc"] = build()
    nc = _NC_CACHE["nc"]
    real = prep_inputs(inputs, len(ACTIVE))
    zero = {k: np.zeros_like(v) for k, v in real[0].items()}
    maps = [zero] * n_cores
    for b, core in enumerate(ACTIVE):
        maps[core] = real[b]
    res = run_bass_kernel_spmd(nc, maps, core_ids=list(range(n_cores)))
    out = np.zeros((len(ACTIVE), LAT, D), np.float32)
    for b, core in enumerate(ACTIVE):
        o = np.asarray(res.results[core]["outT"], np.float32)
        out[b] = np.concatenate([o[0].T, o[1].T], 0)
    return out
```
